# Optimizing a Trainium2 kernel written in Bass

```python
import jax, jax.numpy as jnp
from jax import lax
import numpy as np

D_MODEL = 1024
BATCH = 2
SEQ = 8192
DEPTH = 1

N_GROUPS_A = 8
GROUP_DIM_A = 64
WIDTH_A = N_GROUPS_A * GROUP_DIM_A
CHUNK = 128
N_HEADS_B = 8
HEAD_DIM_B = 64
WIDTH_B = N_HEADS_B * HEAD_DIM_B
MOBA_BLOCK = 256
MOBA_TOPK = 3
Q_BLOCK = 64
ROPE_THETA = 10000.0
D_FF = 2816
CONV_WIDTH = 3
DEEPNORM_ALPHA = (2.0 * DEPTH) ** 0.25
DEEPNORM_BETA = (8.0 * DEPTH) ** -0.25
LN_EPS = 1e-5
PROJ_WIDTH = 2 * WIDTH_A + 3 * WIDTH_B + 2 * D_MODEL

kernel_name = "hybrid_sgu_moba_convffn_deepnorm"


def layer_norm(x, g, b):
    xf = x.astype(jnp.float32)
    mu = jnp.mean(xf, axis=-1, keepdims=True)
    xc = xf - mu
    var = jnp.mean(jnp.square(xc), axis=-1, keepdims=True)
    y = xc * lax.rsqrt(var + LN_EPS) * g.astype(jnp.float32) + b.astype(jnp.float32)
    return y.astype(x.dtype)


def rope(x):
    s, dh = x.shape[1], x.shape[3]
    half = dh // 2
    inv_freq = ROPE_THETA ** (-jnp.arange(half, dtype=jnp.float32) / half)
    ang = jnp.arange(s, dtype=jnp.float32)[:, None] * inv_freq[None, :]
    cos = jnp.cos(ang)[None, :, None, :]
    sin = jnp.sin(ang)[None, :, None, :]
    xf = x.astype(jnp.float32)
    x1, x2 = xf[..., :half], xf[..., half:]
    out = jnp.concatenate([x1 * cos - x2 * sin, x2 * cos + x1 * sin], axis=-1)
    return out.astype(x.dtype)


def spatial_gating(u, v, ln_g, ln_b, w_s, b_s):
    bsz, s, _ = u.shape
    nc = s // CHUNK
    vn = layer_norm(v, ln_g, ln_b).reshape(bsz, nc, CHUNK, N_GROUPS_A, GROUP_DIM_A)
    w_causal = jnp.tril(w_s)
    mixed = jnp.einsum('gts,bcsgd->bctgd', w_causal, vn) + b_s.T[:, :, None]
    return u * mixed.reshape(bsz, s, WIDTH_A)


def moba_attention(q, k, v):
    bsz, s, h, dh = q.shape
    nb = -(-s // MOBA_BLOCK)
    pad = nb * MOBA_BLOCK - s
    qh = q.transpose(0, 2, 1, 3)
    kh = jnp.pad(k.transpose(0, 2, 1, 3), ((0, 0), (0, 0), (0, pad), (0, 0)))
    vh = jnp.pad(v.transpose(0, 2, 1, 3), ((0, 0), (0, 0), (0, pad), (0, 0)))
    k_blocks = kh.reshape(bsz, h, nb, MOBA_BLOCK, dh)
    v_blocks = vh.reshape(bsz, h, nb, MOBA_BLOCK, dh)
    k_mean = jnp.mean(k_blocks.astype(jnp.float32), axis=3)
    n_sel = min(MOBA_TOPK, nb)
    scale = dh ** -0.5
    b_idx = jnp.arange(bsz)[:, None, None, None]
    h_idx = jnp.arange(h)[None, :, None, None]

    def one_block(qb):
        q0 = qb * Q_BLOCK
        q_blk = lax.dynamic_slice_in_dim(qh, q0, Q_BLOCK, axis=2)
        own = q0 // MOBA_BLOCK
        q_pos = q0 + jnp.arange(Q_BLOCK)
        gate = jnp.einsum('bhqd,bhnd->bhqn', q_blk.astype(jnp.float32), k_mean)
        gate = jnp.where(jnp.arange(nb) < own, gate, -jnp.inf)
        _, idx = lax.top_k(gate, n_sel)
        sel_ok = jnp.arange(n_sel) < own
        k_sel = k_blocks[b_idx, h_idx, idx]
        v_sel = v_blocks[b_idx, h_idx, idx]
        s_sel = jnp.einsum('bhqd,bhqnkd->bhqnk', q_blk, k_sel).astype(jnp.float32) * scale
        s_sel = jnp.where(sel_ok[:, None], s_sel, -jnp.inf).reshape(bsz, h, Q_BLOCK, n_sel * MOBA_BLOCK)
        k_own = lax.dynamic_slice_in_dim(kh, own * MOBA_BLOCK, MOBA_BLOCK, axis=2)
        v_own = lax.dynamic_slice_in_dim(vh, own * MOBA_BLOCK, MOBA_BLOCK, axis=2)
        k_pos = own * MOBA_BLOCK + jnp.arange(MOBA_BLOCK)
        s_own = jnp.einsum('bhqd,bhkd->bhqk', q_blk, k_own).astype(jnp.float32) * scale
        s_own = jnp.where(k_pos[None, :] <= q_pos[:, None], s_own, -jnp.inf)
        p = jax.nn.softmax(jnp.concatenate([s_sel, s_own], axis=-1), axis=-1).astype(v.dtype)
        p_sel = p[..., :n_sel * MOBA_BLOCK].reshape(bsz, h, Q_BLOCK, n_sel, MOBA_BLOCK)
        p_own = p[..., n_sel * MOBA_BLOCK:]
        return (jnp.einsum('bhqnk,bhqnkd->bhqd', p_sel, v_sel)
                + jnp.einsum('bhqk,bhkd->bhqd', p_own, v_own))

    outs = lax.map(one_block, jnp.arange(s // Q_BLOCK))
    return outs.transpose(1, 0, 3, 2, 4).reshape(bsz, s, h * dh)


def token_mixer(x, w_in, b_gate, sgu_ln_g, sgu_ln_b, w_spatial, b_spatial,
                w_branch_a, w_branch_b, w_out):
    bsz, s, _ = x.shape
    proj = x @ w_in
    cuts = [WIDTH_A, 2 * WIDTH_A, 2 * WIDTH_A + WIDTH_B, 2 * WIDTH_A + 2 * WIDTH_B,
            2 * WIDTH_A + 3 * WIDTH_B, 2 * WIDTH_A + 3 * WIDTH_B + D_MODEL]
    u_a, v_a, q_b, k_b, v_b, g_a, g_b = jnp.split(proj, cuts, axis=-1)
    y_a = spatial_gating(jax.nn.gelu(u_a, approximate=False), jax.nn.gelu(v_a, approximate=False),
                         sgu_ln_g, sgu_ln_b, w_spatial, b_spatial)
    q_b = rope(q_b.reshape(bsz, s, N_HEADS_B, HEAD_DIM_B))
    k_b = rope(k_b.reshape(bsz, s, N_HEADS_B, HEAD_DIM_B))
    v_b = v_b.reshape(bsz, s, N_HEADS_B, HEAD_DIM_B)
    y_b = moba_attention(q_b, k_b, v_b)
    gate_a = jax.nn.sigmoid(g_a + b_gate[:D_MODEL])
    gate_b = jax.nn.sigmoid(g_b + b_gate[D_MODEL:])
    merged = gate_a * (y_a @ w_branch_a) + gate_b * (y_b @ w_branch_b)
    return merged @ w_out


def conv_ffn(x, w_up, conv_w, conv_b, w_down):
    h = x @ w_up
    c = h.shape[-1]
    h = lax.conv_general_dilated(h, conv_w[:, None, :], window_strides=(1,),
                                 padding=[(CONV_WIDTH - 1, 0)],
                                 dimension_numbers=('NWC', 'WIO', 'NWC'),
                                 feature_group_count=c) + conv_b
    a, g = jnp.split(h, 2, axis=-1)
    return (jax.nn.gelu(a, approximate=False) * g) @ w_down


def setup_inputs(seed: int = 0) -> dict:
    key = jax.random.key(seed)
    ks = jax.random.split(key, 20)
    f32 = jnp.float32
    nrm = lambda k, shape, sc: jax.random.normal(k, shape, f32) * sc
    L = DEPTH
    return {
        "x": nrm(ks[0], (BATCH, SEQ, D_MODEL), 1.0),
        "w_in": nrm(ks[1], (L, D_MODEL, PROJ_WIDTH), D_MODEL ** -0.5),
        "b_gate": nrm(ks[2], (L, 2 * D_MODEL), 0.01),
        "sgu_ln_g": 1.0 + nrm(ks[3], (L, WIDTH_A), 0.01),
        "sgu_ln_b": nrm(ks[4], (L, WIDTH_A), 0.01),
        "w_spatial": nrm(ks[5], (L, N_GROUPS_A, CHUNK, CHUNK), CHUNK ** -0.5),
        "b_spatial": 1.0 + nrm(ks[6], (L, N_GROUPS_A, CHUNK), 0.01),
        "w_branch_a": nrm(ks[7], (L, WIDTH_A, D_MODEL), WIDTH_A ** -0.5 * DEEPNORM_BETA),
        "w_branch_b": nrm(ks[8], (L, WIDTH_B, D_MODEL), WIDTH_B ** -0.5 * DEEPNORM_BETA),
        "w_out": nrm(ks[9], (L, D_MODEL, D_MODEL), D_MODEL ** -0.5 * DEEPNORM_BETA),
        "ln1_g": 1.0 + nrm(ks[10], (L, D_MODEL), 0.01),
        "ln1_b": nrm(ks[11], (L, D_MODEL), 0.01),
        "w_up": nrm(ks[12], (L, D_MODEL, 2 * D_FF), D_MODEL ** -0.5),
        "conv_w": nrm(ks[13], (L, CONV_WIDTH, 2 * D_FF), CONV_WIDTH ** -0.5),
        "conv_b": nrm(ks[14], (L, 2 * D_FF), 0.01),
        "w_down": nrm(ks[15], (L, D_FF, D_MODEL), D_FF ** -0.5 * DEEPNORM_BETA),
        "ln2_g": 1.0 + nrm(ks[16], (L, D_MODEL), 0.01),
        "ln2_b": nrm(ks[17], (L, D_MODEL), 0.01),
    }


def reference(x, w_in, b_gate, sgu_ln_g, sgu_ln_b, w_spatial, b_spatial, w_branch_a,
              w_branch_b, w_out, ln1_g, ln1_b, w_up, conv_w, conv_b, w_down, ln2_g, ln2_b):
    for l in range(DEPTH):
        mix = token_mixer(x, w_in[l], b_gate[l], sgu_ln_g[l], sgu_ln_b[l], w_spatial[l],
                          b_spatial[l], w_branch_a[l], w_branch_b[l], w_out[l])
        x = layer_norm(DEEPNORM_ALPHA * x + mix, ln1_g[l], ln1_b[l])
        ffn = conv_ffn(x, w_up[l], conv_w[l], conv_b[l], w_down[l])
        x = layer_norm(DEEPNORM_ALPHA * x + ffn, ln2_g[l], ln2_b[l])
    return x
```

```python
import contextlib
import numpy as np
import concourse.bass as bass
import concourse.mybir as mybir
from concourse.bass_utils import run_bass_kernel_spmd

F32 = mybir.dt.float32
BF16 = mybir.dt.bfloat16
AF = mybir.ActivationFunctionType
ALU = mybir.AluOpType
AX = mybir.AxisListType

D = 1024
KT = 8
SEQ = 8192
NBSEQ = 32
NB = 33
BLK = 256
SLOTC = NB * BLK
NOWN = 10
TOWN = NOWN * BLK
HALOS = (0, 5)
NOTH_A = 11
GS = 36
KROWS = 64 + NB
NTOK = 2048
DFF = 2816
NCH = 22
ALPHA = 2.0 ** 0.25
EPS = 1e-5
BIG = 32768.0
X1C = 2 * (NTOK // 2 + 2)
NSLOT_RUN = NB
N_OWN_RUN = NOWN
N_HEAD_RUN = 8
N_HALF_RUN = 2
B_PARTS = "uqgs"
Q_STAGE = 9
QM_TILES = None
G_STAGE = 9
B_TILES = None

ENGS = ("tensor", "scalar", "vector", "gpsimd", "sync")


class Res:
    __slots__ = ("w", "r", "name", "excl")

    def __init__(self, name="", excl=False):
        self.w = None
        self.r = {}
        self.name = name
        self.excl = excl


class Op:
    __slots__ = ("eng", "fn", "deps", "inc", "count", "dma", "sem", "phase")

    def __init__(self, eng, fn, dma):
        self.eng = eng
        self.fn = fn
        self.deps = []
        self.inc = False
        self.count = None
        self.dma = dma
        self.sem = None


class Prog:
    def __init__(self, nc, es):
        self.nc = nc
        self.es = es
        self.esem = {e: es.enter_context(nc.semaphore("s_" + e)) for e in ENGS}
        self.ecount = {e: 0 for e in ENGS}
        self.dsem = {}
        self.dcount = {}
        self.ops = None
        self.dma_ops = None
        self.nblock = 0

    def begin(self):
        self.ops = {e: [] for e in ENGS}
        self.dma_ops = []

    def _dsem(self, key):
        if key not in self.dsem:
            self.dsem[key] = self.es.enter_context(self.nc.semaphore("d_" + key))
            self.dcount[key] = 0
        return self.dsem[key]

    def op(self, eng, fn, reads=(), writes=(), dma=None):
        if dma is not None and dma.endswith("*"):
            self.nuniq = getattr(self, "nuniq", 0) + 1
            dma = dma[:-1] + "_u%d" % self.nuniq
        o = Op(eng, fn, dma)
        o.phase = self.nblock
        deps = []
        for r in reads:
            if r.w is not None:
                deps.append(r.w)
            if r.excl:
                for k, rd in r.r.items():
                    if k != eng:
                        deps.append(rd)
        for w in writes:
            if w.w is not None:
                deps.append(w.w)
            for rd in w.r.values():
                deps.append(rd)
        seen = set()
        for d in deps:
            if id(d) in seen or d is o or d.phase != self.nblock:
                continue
            seen.add(id(d))
            if d.dma is None and d.eng == "tensor" and eng == "tensor" and dma is None:
                continue
            o.deps.append(d)
            d.inc = True
        for r in reads:
            r.r[eng if dma is None else ("dma", dma)] = o
        for w in writes:
            w.w = o
            w.r = {}
        if dma is not None:
            o.sem = self._dsem(dma)
            self.dcount[dma] += 16
            o.count = self.dcount[dma]
            self.dma_ops.append(o)
        self.ops[eng].append(o)
        return o

    def end(self, name=None):
        nc = self.nc
        finals = {}
        for o in self.dma_ops:
            finals[o.dma] = (o.sem, max(o.count, finals.get(o.dma, (None, 0))[1]))
        fin = Op("sync", None, None)
        self.ops["sync"].append(fin)
        for e in ENGS:
            for o in self.ops[e]:
                if o.dma is None and o.inc:
                    self.ecount[e] += 1
                    o.count = self.ecount[e]
                    o.sem = self.esem[e]
        self.nblock += 1
        with nc.Block(name or ("blk%d" % self.nblock)) as block:
            for e in ENGS:
                ops = self.ops[e]

                def body(eh, ops=ops):
                    waited = {}
                    for o in ops:
                        if o is fin:
                            for (s, c) in finals.values():
                                eh.wait_ge(s, c)
                            continue
                        for d in o.deps:
                            k = id(d.sem)
                            if waited.get(k, 0) < d.count:
                                eh.wait_ge(d.sem, d.count)
                                waited[k] = d.count
                        inst = o.fn(eh)
                        if o.dma is not None:
                            inst.then_inc(o.sem, 16)
                        elif o.inc:
                            inst.then_inc(o.sem, 1)

                getattr(block, e)(body)
        self.ops = None
        self.dma_ops = None


def _perm_for(j):
    A = [4 * j + i for i in range(4)]
    hA = 4 * j - 1
    B = [28 - 4 * j + i for i in range(4)]
    hB = 27 - 4 * j
    own = [hA] + A + [hB] + B
    ownset = set(b for b in own if b >= 0)
    past = sorted(b for b in range(NBSEQ) if b < hB and b not in ownset)
    left = [b for b in range(NBSEQ) if b not in ownset and b not in past]
    fill = list(left)
    while len(past) + len(fill) < NB - NOWN:
        fill.append(NBSEQ - 1)
    others = (past + fill)[: NB - NOWN]
    assert len(past) <= NB - NOWN
    return own + others


def _consts():
    c = {}
    perm = np.zeros((128, 128), np.float32)
    for m in range(128):
        if (m % 64) < 32:
            perm[m + 32, m] = -1.0
        else:
            perm[m - 32, m] = 1.0
    c["perms"] = perm
    c["ident"] = np.eye(128, dtype=np.float32)
    c["ones"] = np.ones((128, 128), np.float32)
    k = np.arange(128)[:, None]
    q = np.arange(128)[None, :]
    tri = np.where(k <= q, 0.0, -BIG).astype(np.float32)
    m0 = np.concatenate([tri, np.zeros((128, 128), np.float32)], axis=1)
    m1 = np.concatenate([np.full((128, 128), -BIG, np.float32), tri], axis=1)
    c["mtri"] = np.stack([m0, m1], axis=1)
    e = np.zeros((NB, SLOTC), np.float32)
    for j in range(NB):
        e[j, j * BLK:(j + 1) * BLK] = BIG
    c["eind"] = e
    s = np.arange(128)[:, None]
    t = np.arange(128)[None, :]
    c["trilT"] = (s <= t).astype(np.float32)
    return c


def _rope_tables(perm):
    half = 32
    inv_freq = (np.float32(10000.0) ** (-np.arange(half, dtype=np.float32) / np.float32(half))).astype(np.float32)
    pos = np.concatenate([np.arange(b * BLK, (b + 1) * BLK) for b in perm]).astype(np.float32)
    ang = (pos[None, :] * inv_freq[:, None]).astype(np.float32)
    cos = np.cos(ang).astype(np.float32)
    sin = np.sin(ang).astype(np.float32)
    cos = np.tile(cos, (4, 1))
    sin = np.tile(sin, (4, 1))
    return np.stack([cos, sin], axis=1)


def _vmask(perm):
    vm = np.full((NOWN, NB), -1e30, np.float32)
    first = {}
    for s_, b in enumerate(perm):
        if b >= 0 and b not in first:
            first[b] = s_
    for i in range(NOWN):
        a = perm[i]
        for s_ in range(NB):
            b = perm[s_]
            if b >= 0 and b < a and first[b] == s_:
                vm[i, s_] = 0.0
    return np.ascontiguousarray(np.broadcast_to(vm[None], (128, NOWN, NB)))


def _fm(v, n):
    return np.ascontiguousarray(np.asarray(v, np.float32).reshape(n, 128).T)


def build(upto="F", dbg=False):
    nc = bass.Bass("TRN2", target_bir_lowering=False)
    es = contextlib.ExitStack()

    def din(name, shape, dt=F32):
        return nc.dram_tensor(name, list(shape), dt, kind="ExternalInput").ap()

    xT_all = din("xT_all", [KT, 128, SLOTC])
    rope_all = din("rope_all", [128, 2, SLOTC])
    vmask_d = din("vmask", [128, NOWN, NB])
    hscale_d = din("hscale", [128, 2])
    w_in_d = din("w_in", [KT, 128, 4608])
    bgate_d = din("b_gate", [128, 16])
    lng_d = din("sgu_ln_g", [128, 512])
    lnb_d = din("sgu_ln_b", [128, 512])
    wsT_d = din("wsT", [128, 8, 128])
    bsT_d = din("bsT", [128, 4, 128])
    wa_d = din("w_branch_a", [4, 128, D])
    wb_d = din("w_branch_b", [4, 128, D])
    wo_d = din("w_out", [KT, 128, D])
    ln1g_d = din("ln1_g", [128, 8])
    ln1b_d = din("ln1_b", [128, 8])
    wup_d = din("w_up", [KT, 128, 2 * DFF])
    cw_d = din("conv_w", [128, 44, 3])
    cb_d = din("conv_b", [128, 44])
    wd_d = din("w_down", [NCH, 128, D])
    ln2g_d = din("ln2_g", [128, 8])
    ln2b_d = din("ln2_b", [128, 8])
    perms_d = din("perms", [128, 128])
    ident_d = din("ident", [128, 128])
    mtri_d = din("mtri", [128, 2, 256])
    eind_d = din("eind", [NB, SLOTC])
    trilT_d = din("trilT", [128, 128])

    out_d = nc.dram_tensor("out", [NTOK, D], F32, kind="ExternalOutput").ap()

    kT_scr = nc.dram_tensor("kT_scr", [8, 64, SLOTC], BF16, kind="Internal").ap()
    v_scr = nc.dram_tensor("v_scr", [SLOTC, 8, 128], BF16, kind="Internal").ap()
    x1_scr = nc.dram_tensor("x1_scr", [128, KT, X1C], F32, kind="Internal").ap()
    wgS = nc.dram_tensor("wgS", [KT, 128, 2 * D], BF16, kind="Internal").ap()
    waS = nc.dram_tensor("waS", [4, 128, D], BF16, kind="Internal").ap()
    wbS = nc.dram_tensor("wbS", [4, 128, D], BF16, kind="Internal").ap()
    woS = nc.dram_tensor("woS", [KT, 128, D], BF16, kind="Internal").ap()
    wupS = nc.dram_tensor("wupS", [KT, 128, 2 * DFF], BF16, kind="Internal").ap()
    wdS = nc.dram_tensor("wdS", [NCH, 128, D], BF16, kind="Internal").ap()

    dbg_out = {}

    def dout(name, shape, dt=F32):
        a = nc.dram_tensor(name, list(shape), dt, kind="ExternalOutput").ap()
        dbg_out[name] = a
        return a

    P = Prog(nc, es)

    def sb(name, shape, dt):
        return es.enter_context(nc.sbuf_tensor(name, list(shape), dt))

    ps_all = es.enter_context(nc.psum_tensor("ps_all", [128, 8 * 512], F32))
    banks = [ps_all[:, b * 512:(b + 1) * 512] for b in range(8)]
    RB = [Res("bank%d" % b, excl=True) for b in range(8)]

    ident_bf = sb("ident_bf", [128, 128], BF16)
    ident_f = sb("ident_f", [128, 128], F32)
    ones_f = sb("ones_f", [128, 128], F32)
    perms_bf = sb("perms_bf", [128, 128], BF16)
    kmean = sb("kmean", [128, 4, NB], F32)
    eps_t = sb("eps_t", [128, 1], F32)
    R_kmean = Res("kmean")

    with contextlib.ExitStack() as pa:
        def sba(name, shape, dt):
            return pa.enter_context(nc.sbuf_tensor(name, list(shape), dt))

        wk = sba("wk", [128, KT, 512], BF16)
        wv = sba("wv", [128, KT, 512], BF16)
        xt = [sba("xtA%d" % i, [128, KT, BLK], BF16) for i in range(2)]
        cs = [sba("csA%d" % i, [128, 2, BLK], F32) for i in range(2)]
        kb = [sba("kbA%d" % i, [128, 4, BLK], BF16) for i in range(2)]
        ta = [sba("taA%d" % i, [128, 2, BLK], F32) for i in range(2)]
        tb = [sba("tbA%d" % i, [128, 2, BLK], F32) for i in range(2)]
        kst = [sba("kstA%d" % i, [128, 4, BLK], BF16) for i in range(2)]
        vst = [sba("vstA%d" % i, [128, 2, 8, 128], BF16) for i in range(2)]
        ksum = sba("ksum", [128, 4, NB], F32)

        P.begin()
        R_const = Res("const")
        R_wk, R_wv = Res("wk"), Res("wv")
        R_xt = [Res("xt0"), Res("xt1")]
        R_cs = [Res("cs0"), Res("cs1")]
        R_kraw = [[RB[2 * a + p // 2] for p in range(4)] for a in range(2)]
        R_rot = [RB[4 + p % 2] for p in range(4)]
        R_vps = [RB[6], RB[7]]
        R_kb = [[Res() for p in range(4)] for a in range(2)]
        R_ta, R_tb = [Res(), Res()], [Res(), Res()]
        R_kst = [[Res() for p in range(4)] for a in range(2)]
        R_vst = [Res(), Res()]
        R_ksum = Res("ksum")
        R_scr = Res("scr")

        w_in_v = w_in_d.rearrange("k p c -> p k c")
        xT_v = xT_all.rearrange("k p t -> p k t")
        P.op("gpsimd", lambda e: e.dma_start(out=ident_bf[:], in_=ident_d[:, :]), writes=[R_const], dma="c0*")
        P.op("gpsimd", lambda e: e.dma_start(out=perms_bf[:], in_=perms_d[:, :]), writes=[R_const], dma="c0*")
        P.op("sync", lambda e: e.dma_start(out=ident_f[:], in_=ident_d[:, :]), writes=[R_const], dma="c1*")
        P.op("vector", lambda e: e.memset(ones_f[:], 1.0), writes=[R_const])
        P.op("vector", lambda e: e.memset(eps_t[:], EPS), writes=[R_const])
        P.op("gpsimd", lambda e: e.dma_start(out=wk[:], in_=w_in_v[:, :, 1536:2048]), writes=[R_wk], dma="wk*")
        P.op("gpsimd", lambda e: e.dma_start(out=wv[:], in_=w_in_v[:, :, 2048:2560]), writes=[R_wv], dma="wv*")
        for i in range(2):
            P.op("vector", lambda e, i=i: e.memset(vst[i][:], 1.0), writes=[R_vst[i]])
        P.op("vector", lambda e: e.memset(ksum[:], 0.0), writes=[R_ksum])

        kT_v = kT_scr.rearrange("(q hh) d t -> (hh d) q t", hh=2)
        v_v = v_scr.rearrange("(s p) h e -> p s h e", p=128)

        def load_A(s_):
            if s_ >= NSLOT_RUN:
                return
            a_ = s_ % 2
            c0_ = s_ * BLK
            P.op("gpsimd", lambda e, a_=a_, c0_=c0_: e.dma_start(out=xt[a_][:], in_=xT_v[:, :, c0_:c0_ + BLK]),
                 writes=[R_xt[a_]], dma="xt%d" % a_)
            P.op("sync", lambda e, a_=a_, c0_=c0_: e.dma_start(out=cs[a_][:], in_=rope_all[:, :, c0_:c0_ + BLK]),
                 writes=[R_cs[a_]], dma="cs%d" % a_)

        load_A(0)
        for s in range(NSLOT_RUN):
            a = s % 2
            c0 = s * BLK
            load_A(s + 1)
            for p in range(4):
                kr = banks[2 * a + p // 2][:, (p % 2) * BLK:(p % 2 + 1) * BLK]
                for kt in range(KT):
                    P.op("tensor", lambda e, kr=kr, a=a, p=p, kt=kt: e.matmul(
                        kr, lhsT=wk[:, kt, p * 128:(p + 1) * 128], rhs=xt[a][:, kt, :],
                        start=(kt == 0), stop=(kt == KT - 1)),
                        reads=[R_wk, R_xt[a]], writes=[R_kraw[a][p]])
            for sub in range(2):
                vp = banks[6 + sub]
                for kt in range(KT):
                    P.op("tensor", lambda e, vp=vp, a=a, sub=sub, kt=kt: e.matmul(
                        vp[:, :], lhsT=xt[a][:, kt, sub * 128:(sub + 1) * 128], rhs=wv[:, kt, :],
                        start=(kt == 0), stop=(kt == KT - 1)),
                        reads=[R_wv, R_xt[a]], writes=[R_vps[sub]])
            for q2 in range(2):
                kr2 = banks[2 * a + q2][:, :].rearrange("p (j n) -> p j n", j=2)
                rp2 = banks[4 + q2][:, :].rearrange("p (j n) -> p j n", j=2)
                RK = R_kraw[a][2 * q2]
                RR = RB[4 + q2]
                P.op("scalar", lambda e, kr2=kr2, a=a, q2=q2: e.copy(out=kb[a][:, 2 * q2:2 * q2 + 2, :], in_=kr2),
                     reads=[RK], writes=[R_kb[a][q2]])
                for j in range(2):
                    p = 2 * q2 + j
                    P.op("tensor", lambda e, q2=q2, j=j, a=a, p=p: e.matmul(
                        banks[4 + q2][:, j * BLK:(j + 1) * BLK], lhsT=perms_bf[:, :], rhs=kb[a][:, p, :], start=True, stop=True),
                        reads=[R_kb[a][q2], R_const], writes=[RR])
                P.op("vector", lambda e, kr2=kr2, a=a, q2=q2: e.tensor_tensor(
                    out=ta[q2][:], in0=kr2, in1=cs[a][:, 0:1, :].to_broadcast([128, 2, BLK]), op=ALU.mult),
                    reads=[RK, R_cs[a]], writes=[R_ta[q2]])
                P.op("vector", lambda e, rp2=rp2, a=a, q2=q2: e.tensor_tensor(
                    out=tb[q2][:], in0=rp2, in1=cs[a][:, 1:2, :].to_broadcast([128, 2, BLK]), op=ALU.mult),
                    reads=[RR, R_cs[a]], writes=[R_tb[q2]])
                P.op("vector", lambda e, a=a, q2=q2: e.tensor_tensor(
                    out=kst[a][:, 2 * q2:2 * q2 + 2, :], in0=ta[q2][:], in1=tb[q2][:], op=ALU.add),
                    reads=[R_ta[q2], R_tb[q2]], writes=[R_kst[a][q2]])
                P.op("vector", lambda e, a=a, q2=q2, s=s: e.tensor_reduce(
                    out=ksum[:, 2 * q2:2 * q2 + 2, s], in_=kst[a][:, 2 * q2:2 * q2 + 2, :], axis=AX.X, op=ALU.add),
                    reads=[R_kst[a][q2]], writes=[R_ksum])
            P.op("sync", lambda e, a=a, c0=c0: e.dma_start(out=kT_v[:, :, c0:c0 + BLK], in_=kst[a][:]),
                 reads=R_kst[a][0:2], writes=[], dma="ko%d" % a)
            for sub in range(2):
                P.op("scalar", lambda e, a=a, sub=sub: e.copy(
                    out=vst[a][:, sub, :, 0:64], in_=banks[6 + sub][:, :].rearrange("p (h e) -> p h e", h=8)),
                    reads=[R_vps[sub]], writes=[R_vst[a]])
            P.op("sync", lambda e, a=a, s=s: e.dma_start(out=v_v[:, 2 * s:2 * s + 2, :, :], in_=vst[a][:]),
                 reads=[R_vst[a]], writes=[], dma="vo%d" % a)
        P.op("vector", lambda e: e.tensor_scalar(out=kmean[:], in0=ksum[:], scalar1=1.0 / BLK, scalar2=None,
                                                 op0=ALU.mult),
             reads=[R_ksum], writes=[R_kmean])
        if dbg:
            dk = dout("dbg_kmean", [128, 4, NB])
            P.op("sync", lambda e: e.dma_start(out=dk[:, :, :], in_=kmean[:]), reads=[R_kmean], dma="dbg*")
        P.end("phaseA")

    if dbg and upto == "A":
        nt = NSLOT_RUN * BLK
        dko = dout("dbg_kT", [8, 64, nt], BF16)
        dvo = dout("dbg_v", [nt, 8, 128], BF16)
        P.begin()
        P.op("sync", lambda e: e.dma_start(out=dko[:, :, :], in_=kT_scr[:, :, 0:nt]), dma="dbg*")
        P.op("sync", lambda e: e.dma_start(out=dvo[:, :, :], in_=v_scr[0:nt, :, :]), dma="dbg*")
        P.end("dump")
    if upto == "A":
        return nc, dbg_out, es

    w_in_v = w_in_d.rearrange("k p c -> p k c")
    xT_v = xT_all.rearrange("k p t -> p k t")
    x1b = sb("x1b", [128, KT, X1C], BF16)
    R_x1b = Res("x1b")

    with contextlib.ExitStack() as pbd:
        def sbp(name, shape, dt):
            return pbd.enter_context(nc.sbuf_tensor(name, list(shape), dt))
        yAT = sbp("yAT", [128, 4, TOWN], BF16)
        yBT = sbp("yBT", [128, 4, TOWN], BF16)
        R_yAT, R_yBT = Res("yAT"), Res("yBT")
        with contextlib.ExitStack() as pbc:
            QM = pbc.enter_context(nc.sbuf_tensor("QM", [KROWS, 8, TOWN], BF16))
            R_QM = Res("QM")
            with contextlib.ExitStack() as pb:
                def sbb(name, shape, dt):
                    return pb.enter_context(nc.sbuf_tensor(name, list(shape), dt))
                wu = sbb("wu", [128, KT, 512], BF16)
                wq = sbb("wq", [128, KT, 512], BF16)
                wva = sbb("wva", [128, KT, 512], BF16)
                xt = [sbb("xtB%d" % i, [128, KT, BLK], BF16) for i in range(3)]
                cs = [sbb("csB%d" % i, [128, 2, BLK], F32) for i in range(3)]
                ug = [sbb("ugB%d" % i, [128, 4, BLK], BF16) for i in range(2)]
                qb = [[sbb("qbB%d_%d" % (j, i), [128, 2, BLK], BF16) for i in range(2)] for j in range(2)]
                ta = [[sbb("taB%d_%d" % (j, i), [128, 2, BLK], F32) for i in range(2)] for j in range(2)]
                tb = [sbb("tbB%d" % i, [128, 2, BLK], F32) for i in range(2)]
                qf = [sbb("qfB%d" % i, [128, 2, BLK], F32) for i in range(2)]
                vg = [sbb("vgB%d" % i, [128, 512], F32) for i in range(2)]
                vn = [sbb("vn_%d" % i, [128, 512], BF16) for i in range(2)]
                tt4 = [sbb("tt4_%d" % i, [128, 512], F32) for i in range(2)]
                trilT = sbb("trilT_sb", [128, 128], F32)
                wsT_b = sbb("wsT_b", [128, 8, 128], BF16)
                lng = sbb("lng", [128, 512], F32)
                lnb = sbb("lnb", [128, 512], F32)
                bsT = sbb("bsT_sb", [128, 512], F32)
                vmask = sbb("vmask_sb", [128, NOWN, NB], F32)
                st6 = [sbb("st6_%d" % i, [128, 6], F32) for i in range(2)]
                mv = [sbb("mv_%d" % i, [128, 2], F32) for i in range(2)]
                sd = [sbb("sd_%d" % i, [128, 1], F32) for i in range(2)]
                rstd = [sbb("rstd_%d" % i, [128, 1], F32) for i in range(2)]
                gm = sbb("gm", [128, 16, NB], F32)
                top8 = sbb("top8", [128, 16, 8], F32)
                thr = sbb("thr", [128, 16], F32)
                gsel = sbb("gsel", [128, 16, NB], F32)
                msel = sbb("msel", [128, 16, NB], BF16)

                P.begin()
                R_w = Res("wB")
                R_c = Res("cB")
                R_xt = [Res(), Res(), Res()]
                R_cs = [Res(), Res(), Res()]
                R_ug = [Res(), Res()]
                R_qb = [[Res(), Res()], [Res(), Res()]]
                R_ta = [[Res(), Res()], [Res(), Res()]]
                R_tb, R_qf = [Res() for _ in range(4)], [Res() for _ in range(4)]
                R_vg = [Res(), Res()]
                R_vn0, R_vn1, R_vn, R_tt4 = [Res(), Res()], [Res(), Res()], [Res(), Res()], [Res(), Res()]
                R_ws = Res()
                R_st, R_mv, R_sd, R_rstd = [Res(), Res()], [Res(), Res()], [Res(), Res()], [Res(), Res()]
                R_gm, R_top, R_thr, R_msel, R_gsel = Res(), Res(), Res(), Res(), Res()

                P.op("gpsimd", lambda e: e.dma_start(out=wu[:], in_=w_in_v[:, :, 0:512]), writes=[R_w], dma="wB*")
                P.op("gpsimd", lambda e: e.dma_start(out=wq[:], in_=w_in_v[:, :, 1024:1536]), writes=[R_w], dma="wB*")
                P.op("gpsimd", lambda e: e.dma_start(out=wva[:], in_=w_in_v[:, :, 512:1024]), writes=[R_w], dma="wB*")
                P.op("gpsimd", lambda e: e.dma_start(out=wsT_b[:], in_=wsT_d[:, :, :]), writes=[R_ws], dma="cB*")
                P.op("sync", lambda e: e.dma_start(out=trilT[:], in_=trilT_d[:, :]), writes=[R_c], dma="cB*")
                P.op("sync", lambda e: e.dma_start(out=lng[:], in_=lng_d[:, :]), writes=[R_c], dma="cB*")
                P.op("sync", lambda e: e.dma_start(out=lnb[:], in_=lnb_d[:, :]), writes=[R_c], dma="cB*")
                P.op("sync", lambda e: e.dma_start(out=bsT[:], in_=bsT_d.rearrange("p a t -> p (a t)")), writes=[R_c], dma="cB*")
                P.op("sync", lambda e: e.dma_start(out=vmask[:], in_=vmask_d[:, :, :]), writes=[R_c], dma="cB*")
                if dbg:
                    P.op("gpsimd", lambda e: e.memset(yBT[:], 0.0), writes=[R_yBT])
                    P.op("gpsimd", lambda e: e.memset(yAT[:], 0.0), writes=[R_yAT])
                for g in range(8):
                    P.op("vector", lambda e, g=g: e.tensor_tensor(out=wsT_b[:, g, :], in0=wsT_b[:, g, :], in1=trilT[:], op=ALU.mult),
                         reads=[R_ws, R_c], writes=[R_ws])

                def load_tile_B(it):
                    if it >= N_OWN_RUN:
                        return
                    a_ = it % 3
                    c0_ = it * BLK
                    P.op("gpsimd", lambda e, a_=a_, c0_=c0_: e.dma_start(out=xt[a_][:], in_=xT_v[:, :, c0_:c0_ + BLK]),
                         writes=[R_xt[a_]], dma="xtB%d" % a_)
                    P.op("sync", lambda e, a_=a_, c0_=c0_: e.dma_start(out=cs[a_][:], in_=rope_all[:, :, c0_:c0_ + BLK]),
                         writes=[R_cs[a_]], dma="csB%d" % a_)

                def part1_B(it):
                    a = it % 2
                    x3 = it % 3
                    load_tile_B(it + 1)
                    for mt in range(4):
                        ur = banks[mt // 2][:, (mt % 2) * BLK:(mt % 2 + 1) * BLK]
                        for kt in range(KT):
                            P.op("tensor", lambda e, ur=ur, x3=x3, mt=mt, kt=kt: e.matmul(
                                ur, lhsT=wu[:, kt, mt * 128:(mt + 1) * 128], rhs=xt[x3][:, kt, :],
                                start=(kt == 0), stop=(kt == KT - 1)), reads=[R_w, R_xt[x3]], writes=[RB[mt // 2]])
                    for p in range(4):
                        qr = banks[2 + p // 2][:, (p % 2) * BLK:(p % 2 + 1) * BLK]
                        for kt in range(KT):
                            P.op("tensor", lambda e, qr=qr, x3=x3, p=p, kt=kt: e.matmul(
                                qr, lhsT=wq[:, kt, p * 128:(p + 1) * 128], rhs=xt[x3][:, kt, :],
                                start=(kt == 0), stop=(kt == KT - 1)), reads=[R_w, R_xt[x3]], writes=[RB[2 + p // 2]])
                    for q2 in range(2):
                        ur2 = banks[q2][:, :].rearrange("p (j n) -> p j n", j=2)
                        P.op("scalar", lambda e, ur2=ur2, a=a, q2=q2: e.activation(out=ug[a][:, 2 * q2:2 * q2 + 2, :], in_=ur2, func=AF.Gelu),
                             reads=[RB[q2]], writes=[R_ug[a]])
                    for q2 in range(2):
                        qr2 = banks[2 + q2][:, :].rearrange("p (j n) -> p j n", j=2)
                        P.op("scalar", lambda e, qr2=qr2, a=a, q2=q2: e.copy(out=qb[a][q2][:], in_=qr2), reads=[RB[2 + q2]], writes=[R_qb[a][q2]])
                    for q2 in range(2):
                        qr2 = banks[2 + q2][:, :].rearrange("p (j n) -> p j n", j=2)
                        P.op("vector", lambda e, qr2=qr2, a=a, q2=q2, x3=x3: e.tensor_tensor(
                            out=ta[a][q2][:], in0=qr2, in1=cs[x3][:, 0:1, :].to_broadcast([128, 2, BLK]), op=ALU.mult),
                            reads=[RB[2 + q2], R_cs[x3]], writes=[R_ta[a][q2]])

                def part2_B(it):
                    a = it % 2
                    x3 = it % 3
                    c0 = it * BLK
                    subs = [1] if it in HALOS else [0, 1]
                    for p in range(4):
                        rp = banks[4 + p // 2][:, (p % 2) * BLK:(p % 2 + 1) * BLK]
                        P.op("tensor", lambda e, rp=rp, p=p, a=a: e.matmul(rp, lhsT=perms_bf[:, :], rhs=qb[a][p // 2][:, p % 2, :], start=True, stop=True),
                             reads=[R_qb[a][p // 2]], writes=[RB[4 + p // 2]])
                    for sub in subs:
                        for kt in range(KT):
                            P.op("tensor", lambda e, x3=x3, sub=sub, kt=kt: e.matmul(
                                banks[6 + sub][:, :], lhsT=xt[x3][:, kt, sub * 128:(sub + 1) * 128], rhs=wva[:, kt, :],
                                start=(kt == 0), stop=(kt == KT - 1)), reads=[R_w, R_xt[x3]], writes=[RB[6 + sub]])
                    for q2 in range(2):
                        rp2 = banks[4 + q2][:, :].rearrange("p (j n) -> p j n", j=2)
                        P.op("vector", lambda e, rp2=rp2, x3=x3, q2=q2: e.tensor_tensor(
                            out=tb[q2][:], in0=rp2, in1=cs[x3][:, 1:2, :].to_broadcast([128, 2, BLK]), op=ALU.mult),
                            reads=[RB[4 + q2], R_cs[x3]], writes=[R_tb[q2]])
                        P.op("vector", lambda e, q2=q2, a=a: e.tensor_tensor(out=qf[q2][:], in0=ta[a][q2][:], in1=tb[q2][:], op=ALU.add),
                             reads=[R_ta[a][q2], R_tb[q2]], writes=[R_qf[q2]])
                        for hh in range(2):
                            h0 = 4 * q2 + hh
                            P.op("vector", lambda e, q2=q2, hh=hh, h0=h0, c0=c0, a=a: e.tensor_tensor(
                                out=QM[0:64, h0:h0 + 3:2, c0:c0 + BLK], in0=ta[a][q2][hh * 64:(hh + 1) * 64, :, :],
                                in1=tb[q2][hh * 64:(hh + 1) * 64, :, :], op=ALU.add),
                                reads=[R_ta[a][q2], R_tb[q2]], writes=[R_QM])
                    for sub in subs:
                        P.op("scalar", lambda e, sub=sub: e.activation(out=vg[sub][:], in_=banks[6 + sub][:, :], func=AF.Gelu),
                             reads=[RB[6 + sub]], writes=[R_vg[sub]])
                    for sub in subs:
                        P.op("vector", lambda e, sub=sub: e.bn_stats(out=st6[sub][:], in_=vg[sub][:]), reads=[R_vg[sub]], writes=[R_st[sub]])
                        P.op("vector", lambda e, sub=sub: e.bn_aggr(out=mv[sub][:], in_=st6[sub][:]), reads=[R_st[sub]], writes=[R_mv[sub]])
                        P.op("scalar", lambda e, sub=sub: e.activation(out=sd[sub][:], in_=mv[sub][:, 1:2], func=AF.Sqrt, bias=eps_t[:, 0:1], scale=1.0),
                             reads=[R_mv[sub], R_const], writes=[R_sd[sub]])
                        P.op("vector", lambda e, sub=sub: e.reciprocal(out=rstd[sub][:], in_=sd[sub][:]), reads=[R_sd[sub]], writes=[R_rstd[sub]])
                        P.op("vector", lambda e, sub=sub: e.tensor_scalar(
                            out=vg[sub][:], in0=vg[sub][:], scalar1=mv[sub][:, 0:1], scalar2=rstd[sub][:, 0:1],
                            op0=ALU.subtract, op1=ALU.mult), reads=[R_vg[sub], R_mv[sub], R_rstd[sub]], writes=[R_vg[sub]])
                        P.op("gpsimd", lambda e, sub=sub: e.tensor_tensor(out=vg[sub][:], in0=vg[sub][:], in1=lng[:], op=ALU.mult),
                             reads=[R_vg[sub], R_c], writes=[R_vg[sub]])
                        P.op("gpsimd", lambda e, sub=sub: e.tensor_tensor(out=vn[sub][:], in0=vg[sub][:], in1=lnb[:], op=ALU.add),
                             reads=[R_vg[sub], R_c], writes=[R_vn[sub]])
                    for p in range(4):
                        for hh in range(2):
                            for sub in subs:
                                jj = p * 2 + sub
                                P.op("tensor", lambda e, p=p, hh=hh, sub=sub, jj=jj: e.matmul(
                                    banks[7 - hh][:, jj * GS:jj * GS + NB],
                                    lhsT=qf[p // 2][hh * 64:(hh + 1) * 64, p % 2, sub * 128:(sub + 1) * 128],
                                    rhs=kmean[hh * 64:(hh + 1) * 64, p, :], start=True, stop=True),
                                    reads=[R_qf[p // 2], R_kmean], writes=[RB[7 - hh]])
                    for sub in subs:
                        for gp in range(4):
                            for hh in range(2):
                                g = 2 * gp + hh
                                P.op("tensor", lambda e, gp=gp, hh=hh, g=g, sub=sub: e.matmul(
                                    banks[4 + sub][hh * 64:(hh + 1) * 64, gp * 128:(gp + 1) * 128],
                                    lhsT=vn[sub][:, g * 64:(g + 1) * 64], rhs=wsT_b[:, g, :], start=True, stop=True),
                                    reads=[R_vn[sub], R_ws], writes=[RB[4 + sub]])
                    if it in HALOS:
                        for hh in range(2):
                            P.op("vector", lambda e, hh=hh: e.memset(gm[:, hh * 8:(hh + 1) * 8, :], 0.0), writes=[R_gm])
                    for hh in range(2):
                        if it in HALOS:
                            for p in range(4):
                                jj = p * 2 + 1
                                P.op("vector", lambda e, hh=hh, jj=jj, it=it: e.tensor_tensor(
                                    out=gm[:, hh * 8 + jj, :], in0=banks[7 - hh][:, jj * GS:jj * GS + NB],
                                    in1=vmask[:, it, :], op=ALU.add), reads=[RB[7 - hh], R_c], writes=[R_gm])
                        else:
                            P.op("vector", lambda e, hh=hh, it=it: e.tensor_tensor(
                                out=gm[:, hh * 8:(hh + 1) * 8, :],
                                in0=banks[7 - hh][:, 0:8 * GS].rearrange("p (j n) -> p j n", j=8)[:, :, 0:NB],
                                in1=vmask[:, it:it + 1, :].to_broadcast([128, 8, NB]), op=ALU.add),
                                reads=[RB[7 - hh], R_c], writes=[R_gm])
                    for j in range(16):
                        P.op("vector", lambda e, j=j: e.max(out=top8[:, j, :], in_=gm[:, j, :]), reads=[R_gm], writes=[R_top])
                    P.op("vector", lambda e: e.tensor_scalar(out=thr[:], in0=top8[:, :, 2], scalar1=-1e29, scalar2=None, op0=ALU.max),
                         reads=[R_top], writes=[R_thr])
                    P.op("vector", lambda e: e.tensor_tensor(
                        out=gsel[:], in0=gm[:], in1=thr[:, :].unsqueeze(2).to_broadcast([128, 16, NB]), op=ALU.is_ge),
                        reads=[R_gm, R_thr], writes=[R_gsel])
                    P.op("vector", lambda e: e.tensor_scalar(out=msel[:], in0=gsel[:], scalar1=-1.0, scalar2=None, op0=ALU.add),
                         reads=[R_gsel], writes=[R_msel])
                    for sub in subs:
                        P.op("vector", lambda e, sub=sub: e.tensor_tensor(out=tt4[sub][:], in0=banks[4 + sub][:, :], in1=bsT[:], op=ALU.add),
                             reads=[RB[4 + sub], R_c], writes=[R_tt4[sub]])
                        P.op("gpsimd", lambda e, a=a, sub=sub, c0=c0: e.tensor_tensor(
                            out=yAT[:, :, c0 + sub * 128:c0 + (sub + 1) * 128],
                            in0=tt4[sub][:].rearrange("p (g t) -> p g t", g=4),
                            in1=ug[a][:, :, sub * 128:(sub + 1) * 128], op=ALU.mult),
                            reads=[R_tt4[sub], R_ug[a]], writes=[R_yAT])
                    for hh in range(2):
                        tpv = banks[7 - hh][0:NB, :].bitcast(BF16)
                        for jj in range(8):
                            P.op("tensor", lambda e, hh=hh, jj=jj, tpv=tpv: e.transpose(
                                out=tpv[:, jj * 128:(jj + 1) * 128], in_=msel[:, hh * 8 + jj, :], identity=ident_bf[:, :]),
                                reads=[R_msel, R_const], writes=[RB[7 - hh]])
                        for p in range(4):
                            P.op("scalar", lambda e, hh=hh, p=p, c0=c0, tpv=tpv: e.copy(
                                out=QM[64:KROWS, 2 * p + hh, c0:c0 + BLK], in_=tpv[:, p * 256:(p + 1) * 256]),
                                reads=[RB[7 - hh]], writes=[R_QM])

                load_tile_B(0)
                part1_B(0)
                for it in range(N_OWN_RUN):
                    if it + 1 < N_OWN_RUN:
                        part1_B(it + 1)
                    part2_B(it)
                if dbg:
                    ntb = N_OWN_RUN * BLK
                    d1 = dout("dbg_QM", [KROWS, 8, ntb], BF16)
                    d2 = dout("dbg_yAT", [128, 4, ntb], BF16)
                    P.op("sync", lambda e: e.dma_start(out=d1[:, :, :], in_=QM[:, :, 0:ntb]), reads=[R_QM], dma="dbg*")
                    P.op("sync", lambda e: e.dma_start(out=d2[:, :, :], in_=yAT[:, :, 0:ntb]), reads=[R_yAT], dma="dbg*")
                P.end("phaseB")
            if upto == "B":
                return nc, dbg_out, es

            with contextlib.ExitStack() as pc:
                def sbc(name, shape, dt):
                    return pc.enter_context(nc.sbuf_tensor(name, list(shape), dt))
                KE = [sbc("KE%d" % i, [KROWS, SLOTC], BF16) for i in range(2)]
                Vb = [sbc("Vb%d" % i, [128, 2 * NB, 128], BF16) for i in range(2)]
                Pt = [sbc("Pt%d" % i, [128, 1024], BF16) for i in range(3)]
                mtri = sbc("mtri_sb", [128, 2, 256], BF16)
                rc = [sbc("rc%d" % i, [64, 2 * BLK], F32) for i in range(2)]
                P.begin()
                R_KE = [Res(), Res()]
                R_V = [[Res() for _ in range(4)] for _ in range(2)]
                R_Pt = [Res(), Res(), Res()]
                R_m = Res()
                R_rc = [Res(), Res()]
                P.op("gpsimd", lambda e: e.dma_start(out=mtri[:], in_=mtri_d[:, :, :]), writes=[R_m], dma="cC*")
                for i in range(2):
                    P.op("gpsimd", lambda e, i=i: e.dma_start(out=KE[i][64:KROWS, :], in_=eind_d[:, :]), writes=[R_KE[i]], dma="cC*")
                v_hv = v_scr.rearrange("(s p) h e -> p s h e", p=128)
                for k in range(KT):
                    P.op("gpsimd", lambda e, k=k: e.dma_start(out=wgS[k, :, :], in_=w_in_d[k, :, 2560:4608]), dma="stg")
                P.op("gpsimd", lambda e: e.dma_start(out=waS[:, :, :], in_=wa_d[:, :, :]), dma="stg")
                P.op("gpsimd", lambda e: e.dma_start(out=wbS[:, :, :], in_=wb_d[:, :, :]), dma="stg")
                P.op("gpsimd", lambda e: e.dma_start(out=woS[:, :, :], in_=wo_d[:, :, :]), dma="stg")
                for k in range(KT):
                    P.op("gpsimd", lambda e, k=k: e.dma_start(out=wupS[k, :, :], in_=wup_d[k, :, :]), dma="stg")
                for k in range(0, NCH, 2):
                    P.op("gpsimd", lambda e, k=k: e.dma_start(out=wdS[k:k + 2, :, :], in_=wd_d[k:k + 2, :, :]), dma="stg")
                steps = []
                units = [(0, None), (1, 2), (3, 4), (5, None), (6, 7), (8, 9)]
                for h in range(N_HEAD_RUN):
                    for (t1, t2) in units:
                        if t1 >= N_OWN_RUN:
                            continue
                        oth = list(range(NOWN, min(NOWN + NOTH_A, NSLOT_RUN))) if t1 < 5 else list(range(NOWN, NSLOT_RUN))
                        ust = []
                        if t2 is None:
                            q0 = t1 * BLK + BLK - 2
                            for s_ in list(range(t1)) + oth:
                                ust.append((s_, [(False, q0, 2, 0)], 2))
                            ust.append((t1, [(True, q0, 2, 0)], 2))
                        else:
                            q0 = t1 * BLK
                            for s_ in list(range(t1)) + oth:
                                ust.append((s_, [(False, q0, 2 * BLK, 0)], 2 * BLK))
                            ust.append((t1, [(True, q0, BLK, 0), (False, q0 + BLK, BLK, BLK)], 2 * BLK))
                            ust.append((t2, [(True, q0 + BLK, BLK, BLK)], 2 * BLK))
                        for k, (s_, segs, wtot) in enumerate(ust):
                            steps.append(dict(h=h, t1=t1, t2=t2, s=s_, segs=segs, wtot=wtot,
                                              first=(k == 0), last=(k == len(ust) - 1)))
                loaded = set()
                VSPL = [0, 17, 34, 50, 2 * NB]

                def vpart(k):
                    return max(q for q in range(4) if VSPL[q] <= k)

                def load_head(h):
                    if h in loaded or h >= N_HEAD_RUN:
                        return
                    loaded.add(h)
                    hb = h % 2
                    P.op("sync", lambda e, h=h, hb=hb: e.dma_start(out=KE[hb][0:64, :], in_=kT_scr[h, :, :]),
                         writes=[R_KE[hb]], dma="ke%d" % hb)
                    for q4 in range(4):
                        k0, k1 = VSPL[q4], VSPL[q4 + 1]
                        P.op("sync", lambda e, h=h, hb=hb, k0=k0, k1=k1: e.dma_start(
                            out=Vb[hb][:, k0:k1, :], in_=v_hv[:, k0:k1, h, :]),
                            writes=[R_V[hb][q4]], dma="vb%d_%d" % (hb, q4))

                def s_region(n):
                    k3 = n % 3
                    return ps_all[:, (2 * k3) * 512:(2 * k3 + 2) * 512].rearrange("p (k n) -> p k n", k=2), [RB[2 * k3], RB[2 * k3 + 1]]

                def emit_S(n):
                    st = steps[n]
                    h, s_ = st["h"], st["s"]
                    hb = h % 2
                    k3 = n % 3
                    sreg, rbs = s_region(n)
                    for kt in range(2):
                        kc = s_ * BLK + kt * 128
                        for (own, qc, w, off) in st["segs"]:
                            ov = sreg[:, kt, off:off + w]
                            if not own:
                                P.op("tensor", lambda e, ov=ov, hb=hb, kc=kc, h=h, qc=qc, w=w: e.matmul(
                                    ov, lhsT=KE[hb][0:KROWS, kc:kc + 128], rhs=QM[0:KROWS, h, qc:qc + w], start=True, stop=True),
                                    reads=[R_KE[hb], R_QM], writes=[rbs[kt]])
                            else:
                                mo = BLK - w
                                P.op("tensor", lambda e, ov=ov, hb=hb, kc=kc, h=h, qc=qc, w=w: e.matmul(
                                    ov, lhsT=KE[hb][0:64, kc:kc + 128], rhs=QM[0:64, h, qc:qc + w], start=True, stop=False),
                                    reads=[R_KE[hb], R_QM], writes=[rbs[kt]])
                                P.op("tensor", lambda e, ov=ov, kt=kt, mo=mo: e.matmul(
                                    ov, lhsT=ident_bf[:, :], rhs=mtri[:, kt, mo:BLK], start=False, stop=True),
                                    reads=[R_m, R_const], writes=[rbs[kt]])
                    lo = min(sg_[3] for sg_ in st["segs"])
                    hi = max(sg_[3] + sg_[2] for sg_ in st["segs"])
                    ptv = Pt[k3][:].rearrange("p (k n) -> p k n", k=2)
                    P.op("scalar", lambda e, sreg=sreg, ptv=ptv, lo=lo, hi=hi: e.activation(
                        out=ptv[:, :, lo:hi], in_=sreg[:, :, lo:hi], func=AF.Exp, scale=0.125),
                        reads=rbs, writes=[R_Pt[k3]])
                    st["lo"], st["hi"] = lo, hi

                def emit_PV(n):
                    st = steps[n]
                    h, s_ = st["h"], st["s"]
                    hb = h % 2
                    k3 = n % 3
                    lo, hi = st["lo"], st["hi"]
                    ob = st["ob"]
                    ptv = Pt[k3][:].rearrange("p (k n) -> p k n", k=2)
                    for kt in range(2):
                        P.op("tensor", lambda e, ob=ob, hb=hb, s_=s_, kt=kt, ptv=ptv, lo=lo, hi=hi, st=st: e.matmul(
                            banks[6 + ob][:, lo:hi], lhsT=Vb[hb][:, 2 * s_ + kt, :], rhs=ptv[:, kt, lo:hi],
                            start=(st["first"] and kt == 0), stop=(st["last"] and kt == 1)),
                            reads=[R_V[hb][vpart(2 * s_ + kt)], R_Pt[k3]], writes=[RB[6 + ob]])
                    if st["last"]:
                        w = st["wtot"]
                        q0 = st["t1"] * BLK + (BLK - 2 if st["t2"] is None else 0)
                        P.op("vector", lambda e, ob=ob, w=w: e.reciprocal(out=rc[ob][:, 0:w], in_=banks[6 + ob][64:128, 0:w]),
                             reads=[RB[6 + ob]], writes=[R_rc[ob]])
                        P.op("vector", lambda e, ob=ob, h=h, q0=q0, w=w: e.tensor_tensor(
                            out=yBT[(h % 2) * 64:(h % 2 + 1) * 64, h // 2, q0:q0 + w],
                            in0=banks[6 + ob][0:64, 0:w], in1=rc[ob][:, 0:w], op=ALU.mult),
                            reads=[RB[6 + ob], R_rc[ob]], writes=[R_yBT])

                uc = -1
                for st in steps:
                    if st["first"]:
                        uc += 1
                    st["ob"] = uc % 2
                load_head(0)
                nsteps = len(steps)
                for n in range(nsteps + 2):
                    if n < nsteps:
                        st = steps[n]
                        if st["first"] and st["t1"] == 0:
                            load_head(st["h"])
                        if st["first"] and st["t1"] == 1:
                            load_head(st["h"] + 1)
                        emit_S(n)
                    if n >= 2:
                        emit_PV(n - 2)
                if dbg:
                    ntb = N_OWN_RUN * BLK
                    nhp = (N_HEAD_RUN + 1) // 2
                    d3 = dout("dbg_yBT", [128, nhp, ntb], BF16)
                    P.op("sync", lambda e: e.dma_start(out=d3[:, :, :], in_=yBT[:, 0:nhp, 0:ntb]), reads=[R_yBT], dma="dbg*")
                P.end("phaseC")
        if upto == "C":
            return nc, dbg_out, es

        with contextlib.ExitStack() as pd:
            def sbd(name, shape, dt):
                return pd.enter_context(nc.sbuf_tensor(name, list(shape), dt))
            wa = sbd("wa", [128, 4, D], BF16)
            wb = sbd("wb", [128, 4, D], BF16)
            wg = sbd("wg", [128, KT, 2 * D], BF16)
            wo = sbd("wo", [128, KT, D], BF16)
            bg = sbd("bg", [128, 16], F32)
            l1g = sbd("l1g", [128, 8], F32)
            l1b = sbd("l1b", [128, 8], F32)
            hsc = sbd("hsc", [128, 2], F32)
            xtb = [sbd("xtbD%d" % i, [128, KT, BLK], BF16) for i in range(2)]
            xtf2 = [sbd("xtfD%d" % i, [128, KT, BLK], F32) for i in range(2)]
            sa = [sbd("saD%d" % i, [128, BLK], F32) for i in range(2)]
            sg = [sbd("sgD%d" % i, [128, BLK], F32) for i in range(2)]
            t1 = [sbd("t1D%d" % i, [128, BLK], F32) for i in range(2)]
            t2 = [sbd("t2D%d" % i, [128, BLK], F32) for i in range(2)]
            mT = sbd("mT", [128, KT, BLK], BF16)
            zT2 = [sbd("zT%d" % i, [128, KT, BLK], F32) for i in range(2)]
            zsq = [sbd("zsqD%d" % i, [128, BLK], F32) for i in range(2)]
            mean2 = [sbd("meanD%d" % i, [128, BLK], F32) for i in range(2)]
            msq = sbd("msqD", [128, BLK], F32)
            var = sbd("varD", [128, BLK], F32)
            sdv = sbd("sdvD", [128, BLK], F32)
            rsv2 = [sbd("rsvD%d" % i, [128, BLK], F32) for i in range(2)]
            tn = [sbd("tnD%d" % i, [128, BLK], F32) for i in range(2)]
            tn2 = [sbd("tn2D%d" % i, [128, BLK], F32) for i in range(2)]
            P.begin()
            R_w, R_c = Res(), Res()
            R_wgD, R_wgD2, R_waD, R_wbD = Res(), Res(), Res(), Res()
            R_xtb = [Res(), Res()]
            R_xtf2 = [Res(), Res()]
            R_sa, R_sg, R_t1, R_t2 = [Res(), Res()], [Res(), Res()], [Res(), Res()], [Res(), Res()]
            R_mT = Res()
            R_zT2 = [[Res() for _ in range(8)] for _ in range(2)]
            R_zsq = [Res(), Res()]
            R_msq, R_var, R_sdv = Res(), Res(), Res()
            R_mean2, R_rsv2 = [Res(), Res()], [Res(), Res()]
            R_tn, R_tn2 = [Res(), Res()], [Res(), Res()]
            wg_v = wgS.rearrange("k p c -> p k c")
            wa_v = waS.rearrange("k p c -> p k c")
            wb_v = wbS.rearrange("k p c -> p k c")
            wo_v = woS.rearrange("k p c -> p k c")
            P.op("sync", lambda e: e.dma_start(out=wg[:, :, 0:D], in_=wg_v[:, :, 0:D]), writes=[R_wgD], dma="wD*")
            P.op("gpsimd", lambda e: e.dma_start(out=wa[:], in_=wa_v[:, :, :]), writes=[R_waD], dma="wD*")
            P.op("gpsimd", lambda e: e.dma_start(out=wb[:], in_=wb_v[:, :, :]), writes=[R_wbD], dma="wD*")
            P.op("sync", lambda e: e.dma_start(out=wg[:, :, D:2 * D], in_=wg_v[:, :, D:2 * D]), writes=[R_wgD2], dma="wD*")
            P.op("gpsimd", lambda e: e.dma_start(out=wo[:], in_=wo_v[:, :, :]), writes=[R_w], dma="wD*")
            P.op("sync", lambda e: e.dma_start(out=bg[:], in_=bgate_d[:, :]), writes=[R_c], dma="cD*")
            P.op("sync", lambda e: e.dma_start(out=l1g[:], in_=ln1g_d[:, :]), writes=[R_c], dma="cD*")
            P.op("sync", lambda e: e.dma_start(out=l1b[:], in_=ln1b_d[:, :]), writes=[R_c], dma="cD*")
            P.op("sync", lambda e: e.dma_start(out=hsc[:], in_=hscale_d[:, :]), writes=[R_c], dma="cD*")
            def tile_geom(it):
                xb = (it // 5) * (NTOK // 2 + 2)
                if it in HALOS:
                    return it * BLK + BLK - 2, 2, xb
                return it * BLK, BLK, xb + 2 + (it % 5 - 1) * BLK

            def load_tile_D(it):
                if it >= N_OWN_RUN:
                    return
                a_ = it % 2
                c0_, n_, _ = tile_geom(it)
                P.op("gpsimd", lambda e, a_=a_, c0_=c0_, n_=n_: e.dma_start(out=xtb[a_][:, :, 0:n_], in_=xT_v[:, :, c0_:c0_ + n_]),
                     writes=[R_xtb[a_]], dma="xtbD%d" % a_)
                P.op("sync", lambda e, a_=a_, c0_=c0_, n_=n_: e.dma_start(out=xtf2[a_][:, :, 0:n_], in_=xT_v[:, :, c0_:c0_ + n_]),
                     writes=[R_xtf2[a_]], dma="xtfD%d" % a_)

            def part1(it, prev=None):
                a = it % 2
                c0, n, xc0 = tile_geom(it)
                xtf = xtf2[a]
                R_xtf = R_xtf2[a]
                zTa, R_zTa = zT2[a], R_zT2[a]
                load_tile_D(it + 1)
                for mt in range(8):
                    b2 = mt % 2
                    ms = slice(mt * 128, (mt + 1) * 128)
                    bab = banks[b2]
                    bgg = banks[2 + b2]
                    for kt in range(KT):
                        P.op("tensor", lambda e, kt=kt, mt=mt, a=a, n=n, bgg=bgg: e.matmul(
                            bgg[:, 0:n], lhsT=wg[:, kt, mt * 128:(mt + 1) * 128], rhs=xtb[a][:, kt, 0:n],
                            start=(kt == 0), stop=(kt == KT - 1)), reads=[R_wgD, R_xtb[a]], writes=[RB[2 + b2]])
                    for kt in range(KT):
                        P.op("tensor", lambda e, kt=kt, mt=mt, a=a, n=n, bgg=bgg: e.matmul(
                            bgg[:, BLK:BLK + n], lhsT=wg[:, kt, D + mt * 128:D + (mt + 1) * 128], rhs=xtb[a][:, kt, 0:n],
                            start=(kt == 0), stop=(kt == KT - 1)), reads=[R_wgD2, R_xtb[a]], writes=[RB[2 + b2]])
                    for kt in range(4):
                        P.op("tensor", lambda e, kt=kt, ms=ms, c0=c0, n=n, bab=bab: e.matmul(
                            bab[:, 0:n], lhsT=wa[:, kt, ms], rhs=yAT[:, kt, c0:c0 + n], start=(kt == 0), stop=(kt == 3)),
                            reads=[R_waD, R_yAT], writes=[RB[b2]])
                    for kt in range(4):
                        P.op("tensor", lambda e, kt=kt, ms=ms, c0=c0, n=n, bab=bab: e.matmul(
                            bab[:, BLK:BLK + n], lhsT=wb[:, kt, ms], rhs=yBT[:, kt, c0:c0 + n], start=(kt == 0), stop=(kt == 3)),
                            reads=[R_wbD, R_yBT], writes=[RB[b2]])
                    P.op("scalar", lambda e, b2=b2, mt=mt, n=n, bgg=bgg: e.activation(
                        out=sa[b2][:, 0:n], in_=bgg[:, 0:n], func=AF.Sigmoid, bias=bg[:, mt:mt + 1], scale=1.0),
                        reads=[RB[2 + b2], R_c], writes=[R_sa[b2]])
                    P.op("scalar", lambda e, b2=b2, mt=mt, n=n, bgg=bgg: e.activation(
                        out=sg[b2][:, 0:n], in_=bgg[:, BLK:BLK + n], func=AF.Sigmoid, bias=bg[:, 8 + mt:9 + mt], scale=1.0),
                        reads=[RB[2 + b2], R_c], writes=[R_sg[b2]])
                    P.op("vector", lambda e, b2=b2, n=n, bab=bab: e.tensor_tensor(out=t1[b2][:, 0:n], in0=bab[:, 0:n], in1=sa[b2][:, 0:n], op=ALU.mult),
                         reads=[RB[b2], R_sa[b2]], writes=[R_t1[b2]])
                    P.op("vector", lambda e, b2=b2, n=n, bab=bab: e.tensor_tensor(out=t2[b2][:, 0:n], in0=bab[:, BLK:BLK + n], in1=sg[b2][:, 0:n], op=ALU.mult),
                         reads=[RB[b2], R_sg[b2]], writes=[R_t2[b2]])
                    P.op("gpsimd", lambda e, b2=b2, mt=mt, n=n: e.tensor_tensor(out=mT[:, mt, 0:n], in0=t1[b2][:, 0:n], in1=t2[b2][:, 0:n], op=ALU.add),
                         reads=[R_t1[b2], R_t2[b2]], writes=[R_mT])

                def stats_D(m_):
                    c2 = m_ % 2
                    P.op("tensor", lambda e, m_=m_, n=n: e.matmul(banks[6][:, 0:n], lhsT=ones_f[:, :], rhs=zTa[:, m_, 0:n],
                                                                   start=(m_ == 0), stop=(m_ == 7)), reads=[R_zTa[m_], R_const], writes=[RB[6]])
                    P.op("tensor", lambda e, m_=m_, c2=c2, n=n: e.matmul(banks[7][:, 0:n], lhsT=ones_f[:, :], rhs=zsq[c2][:, 0:n],
                                                                          start=(m_ == 0), stop=(m_ == 7)), reads=[R_zsq[c2], R_const], writes=[RB[7]])
                for mt in range(8):
                    b2 = mt % 2
                    for kt in range(KT):
                        P.op("tensor", lambda e, kt=kt, mt=mt, b2=b2, n=n: e.matmul(
                            banks[4 + b2][:, 0:n], lhsT=wo[:, kt, mt * 128:(mt + 1) * 128], rhs=mT[:, kt, 0:n],
                            start=(kt == 0), stop=(kt == KT - 1)), reads=[R_w, R_mT], writes=[RB[4 + b2]])
                    if mt >= 1:
                        stats_D(mt - 1)
                    P.op("vector", lambda e, mt=mt, b2=b2, n=n, xtf=xtf: e.scalar_tensor_tensor(
                        out=zTa[:, mt, 0:n], in0=xtf[:, mt, 0:n], scalar=ALPHA, in1=banks[4 + b2][:, 0:n], op0=ALU.mult, op1=ALU.add),
                        reads=[R_xtf, RB[4 + b2]], writes=[R_zTa[mt]])
                    P.op("scalar", lambda e, mt=mt, b2=b2, n=n: e.activation(out=zsq[b2][:, 0:n], in_=zTa[:, mt, 0:n], func=AF.Square),
                         reads=[R_zTa[mt]], writes=[R_zsq[b2]])
                    if prev is not None:
                        part2(prev, mts=[mt], store=(mt == 7))
                stats_D(7)
                P.op("vector", lambda e, n=n, a=a: e.tensor_scalar(out=mean2[a][:, 0:n], in0=banks[6][:, 0:n], scalar1=1.0 / D, scalar2=None, op0=ALU.mult),
                     reads=[RB[6]], writes=[R_mean2[a]])
                P.op("vector", lambda e, n=n, a=a: e.tensor_tensor(out=msq[:, 0:n], in0=mean2[a][:, 0:n], in1=mean2[a][:, 0:n], op=ALU.mult),
                     reads=[R_mean2[a]], writes=[R_msq])
                P.op("vector", lambda e, n=n: e.scalar_tensor_tensor(out=var[:, 0:n], in0=banks[7][:, 0:n], scalar=1.0 / D, in1=msq[:, 0:n],
                                                                    op0=ALU.mult, op1=ALU.subtract), reads=[RB[7], R_msq], writes=[R_var])
                P.op("scalar", lambda e, n=n: e.activation(out=sdv[:, 0:n], in_=var[:, 0:n], func=AF.Sqrt, bias=eps_t[:, 0:1], scale=1.0),
                     reads=[R_var, R_const], writes=[R_sdv])
                P.op("vector", lambda e, n=n, a=a: e.reciprocal(out=rsv2[a][:, 0:n], in_=sdv[:, 0:n]), reads=[R_sdv], writes=[R_rsv2[a]])

            def part2(it, mts=range(8), store=True):
                a = it % 2
                c0, n, xc0 = tile_geom(it)
                zTa, R_zTa = zT2[a], R_zT2[a]
                for mt in mts:
                    b2 = mt % 2
                    P.op("vector", lambda e, mt=mt, b2=b2, n=n, a=a: e.tensor_tensor(out=tn[b2][:, 0:n], in0=zTa[:, mt, 0:n], in1=mean2[a][:, 0:n], op=ALU.subtract),
                         reads=[R_zTa[mt], R_mean2[a]], writes=[R_tn[b2]])
                    P.op("vector", lambda e, b2=b2, n=n, a=a: e.tensor_tensor(out=tn2[b2][:, 0:n], in0=tn[b2][:, 0:n], in1=rsv2[a][:, 0:n], op=ALU.mult),
                         reads=[R_tn[b2], R_rsv2[a]], writes=[R_tn2[b2]])
                    P.op("scalar", lambda e, mt=mt, b2=b2, n=n: e.activation(
                        out=zTa[:, mt, 0:n], in_=tn2[b2][:, 0:n], func=AF.Identity, bias=l1b[:, mt:mt + 1], scale=l1g[:, mt:mt + 1]),
                        reads=[R_tn2[b2], R_c], writes=[R_zTa[mt]])
                    if it in HALOS:
                        P.op("gpsimd", lambda e, mt=mt, n=n, xc0=xc0, it=it: e.tensor_scalar(
                            out=x1b[:, mt, xc0:xc0 + n], in0=zTa[:, mt, 0:n], scalar1=hsc[:, it // 5:it // 5 + 1], scalar2=None, op0=ALU.mult),
                            reads=[R_zTa[mt], R_c], writes=[R_x1b])
                    else:
                        P.op("scalar", lambda e, mt=mt, b2=b2, n=n, xc0=xc0: e.activation(
                            out=x1b[:, mt, xc0:xc0 + n], in_=tn2[b2][:, 0:n], func=AF.Identity, bias=l1b[:, mt:mt + 1], scale=l1g[:, mt:mt + 1]),
                            reads=[R_tn2[b2], R_c], writes=[R_x1b])
                if store:
                    P.op("sync", lambda e, n=n, xc0=xc0: e.dma_start(out=x1_scr[:, :, xc0:xc0 + n], in_=zTa[:, :, 0:n]),
                         reads=R_zTa, dma="x1o%d" % a)

            load_tile_D(0)
            for it in range(N_OWN_RUN):
                part1(it, prev=(it - 1 if it >= 1 else None))
            part2(N_OWN_RUN - 1)
            if dbg:
                nx = X1C
                d4 = dout("dbg_x1", [128, KT, nx], BF16)
                P.op("sync", lambda e: e.dma_start(out=d4[:, :, :], in_=x1b[:, :, 0:nx]), reads=[R_x1b], dma="dbg*")
            P.end("phaseD")
    if upto == "D":
        return nc, dbg_out, es

    with contextlib.ExitStack() as pe:
        def sbe(name, shape, dt):
            return pe.enter_context(nc.sbuf_tensor(name, list(shape), dt))
        HT = NTOK // 2
        hgT = sbe("hgT", [128, NCH, HT], BF16)
        wd = sbe("wd", [128, NCH, D], BF16)
        cw = sbe("cw", [128, 44, 3], F32)
        cb = sbe("cb", [128, 44], F32)
        l2g = sbe("l2g", [128, 8], F32)
        l2b = sbe("l2b", [128, 8], F32)
        wuc = [sbe("wuc%d" % i, [128, KT, 256], BF16) for i in range(2)]
        ct1 = [sbe("ct1_%d" % i, [128, 512], F32) for i in range(2)]
        ct2 = [sbe("ct2_%d" % i, [128, 512], F32) for i in range(2)]
        ct3 = [sbe("ct3_%d" % i, [128, 512], F32) for i in range(2)]
        gact = sbe("gact", [128, 512], F32)
        xf = sbe("xfF", [128, KT, 512], F32)
        z2 = sbe("z2F", [128, KT, 512], F32)
        zq = [sbe("zqF%d" % i, [128, 512], F32) for i in range(2)]
        meanF = sbe("meanF", [128, 512], F32)
        msqF = sbe("msqF", [128, 512], F32)
        varF = sbe("varF", [128, 512], F32)
        sdF = sbe("sdF", [128, 512], F32)
        rsF = sbe("rsF", [128, 512], F32)
        tnF = [sbe("tnF%d" % i, [128, 512], F32) for i in range(2)]
        tn2F = [sbe("tn2F%d" % i, [128, 512], F32) for i in range(2)]
        ost = [sbe("ost%d" % i, [128, D], F32) for i in range(2)]
        wup_v = wupS.rearrange("k p c -> p k c")
        wd_v = wdS.rearrange("k p c -> p k c")
        P.begin()
        R_c, R_hg = Res(), Res()
        R_wd2 = [Res(), Res()]
        R_wuc = [[Res(), Res()], [Res(), Res()]]
        R_ct1, R_ct2, R_ct3 = [Res(), Res()], [Res(), Res()], [Res(), Res()]
        R_ga = Res()
        R_xf = Res()
        R_z2m = [Res() for _ in range(8)]
        R_zq = [Res(), Res()]
        R_mean, R_msq, R_var, R_sd, R_rs = Res(), Res(), Res(), Res(), Res()
        R_tn, R_tn2 = [Res(), Res()], [Res(), Res()]
        R_ost = [Res(), Res()]
        P.op("sync", lambda e: e.dma_start(out=cw[:], in_=cw_d[:, :, :]), writes=[R_c], dma="cE*")
        P.op("sync", lambda e: e.dma_start(out=cb[:], in_=cb_d[:, :]), writes=[R_c], dma="cE*")
        P.op("sync", lambda e: e.dma_start(out=l2g[:], in_=ln2g_d[:, :]), writes=[R_c], dma="cE*")
        P.op("sync", lambda e: e.dma_start(out=l2b[:], in_=ln2b_d[:, :]), writes=[R_c], dma="cE*")
        nchunk = 0
        ntile = 0

        def load_chunk(n):
            if n >= N_HALF_RUN * NCH:
                return
            c_ = n % NCH
            wb_ = n % 2
            for part in range(2):
                cc = part * DFF + c_ * 128
                P.op("gpsimd", lambda e, wb_=wb_, part=part, cc=cc: e.dma_start(
                    out=wuc[wb_][:, :, part * 128:(part + 1) * 128], in_=wup_v[:, :, cc:cc + 128]),
                    writes=[R_wuc[wb_][part]], dma="wuc%d%d" % (wb_, part))

        load_chunk(0)
        wd_loaded = [False]
        for hf in range(N_HALF_RUN):
            base = (HT + 2) * hf
            for c in range(NCH):
                wbuf = nchunk % 2
                nchunk += 1
                load_chunk(nchunk)
                if not wd_loaded[0]:
                    wd_loaded[0] = True
                    for q3 in range(2):
                        P.op("gpsimd", lambda e, q3=q3: e.dma_start(out=wd[:, q3 * 11:(q3 + 1) * 11, :], in_=wd_v[:, q3 * 11:(q3 + 1) * 11, :]),
                             writes=[R_wd2[q3]], dma="wdF*")
                for T in range(3):
                    col0 = base + 510 * T
                    ncol = min(512, base + HT + 2 - col0)
                    nout = ncol - 2
                    pb2 = ntile % 2
                    ntile += 1
                    for part in range(2):
                        bk = 2 * pb2 + part
                        for kt in range(KT):
                            P.op("tensor", lambda e, bk=bk, wbuf=wbuf, part=part, kt=kt, col0=col0, ncol=ncol: e.matmul(
                                banks[bk][:, 0:ncol], lhsT=wuc[wbuf][:, kt, part * 128:(part + 1) * 128],
                                rhs=x1b[:, kt, col0:col0 + ncol], start=(kt == 0), stop=(kt == KT - 1)),
                                reads=[R_wuc[wbuf][part], R_x1b], writes=[RB[bk]])
                    for part in range(2):
                        bk = 2 * pb2 + part
                        ci = part * NCH + c
                        P.op("scalar", lambda e, bk=bk, part=part, ci=ci, ncol=ncol, nout=nout: e.activation(
                            out=ct1[part][:, 0:nout], in_=banks[bk][:, 2:ncol], func=AF.Identity,
                            bias=cb[:, ci:ci + 1], scale=cw[:, ci, 2:3]), reads=[RB[bk], R_c], writes=[R_ct1[part]])
                        P.op("vector", lambda e, bk=bk, part=part, ci=ci, ncol=ncol, nout=nout: e.scalar_tensor_tensor(
                            out=ct2[part][:, 0:nout], in0=banks[bk][:, 1:ncol - 1], scalar=cw[:, ci, 1:2], in1=ct1[part][:, 0:nout],
                            op0=ALU.mult, op1=ALU.add), reads=[RB[bk], R_c, R_ct1[part]], writes=[R_ct2[part]])
                        P.op("vector", lambda e, bk=bk, part=part, ci=ci, ncol=ncol, nout=nout: e.scalar_tensor_tensor(
                            out=ct3[part][:, 0:nout], in0=banks[bk][:, 0:ncol - 2], scalar=cw[:, ci, 0:1], in1=ct2[part][:, 0:nout],
                            op0=ALU.mult, op1=ALU.add), reads=[RB[bk], R_c, R_ct2[part]], writes=[R_ct3[part]])
                    P.op("scalar", lambda e, nout=nout: e.activation(out=gact[:, 0:nout], in_=ct3[0][:, 0:nout], func=AF.Gelu),
                         reads=[R_ct3[0]], writes=[R_ga])
                    P.op("gpsimd", lambda e, c=c, T=T, nout=nout: e.tensor_tensor(
                        out=hgT[:, c, 510 * T:510 * T + nout], in0=gact[:, 0:nout], in1=ct3[1][:, 0:nout], op=ALU.mult),
                        reads=[R_ga, R_ct3[1]], writes=[R_hg])
            for T2 in range(2):
                tb0 = HT * hf + 512 * T2
                xcol = base + 2 + 512 * T2
                P.op("sync", lambda e, xcol=xcol: e.dma_start(out=xf[:], in_=x1_scr[:, :, xcol:xcol + 512]),
                     writes=[R_xf], dma="xfF")
                for mt in range(8):
                    b2 = mt % 2
                    for kt in range(NCH):
                        P.op("tensor", lambda e, b2=b2, kt=kt, mt=mt, T2=T2: e.matmul(
                            banks[b2][:, :], lhsT=wd[:, kt, mt * 128:(mt + 1) * 128], rhs=hgT[:, kt, 512 * T2:512 * (T2 + 1)],
                            start=(kt == 0), stop=(kt == NCH - 1)), reads=[R_wd2[kt // 11], R_hg], writes=[RB[b2]])
                    P.op("vector", lambda e, mt=mt, b2=b2: e.scalar_tensor_tensor(
                        out=z2[:, mt, :], in0=xf[:, mt, :], scalar=ALPHA, in1=banks[b2][:, :], op0=ALU.mult, op1=ALU.add),
                        reads=[R_xf, RB[b2]], writes=[R_z2m[mt]])
                    P.op("scalar", lambda e, mt=mt, b2=b2: e.activation(out=zq[b2][:], in_=z2[:, mt, :], func=AF.Square),
                         reads=[R_z2m[mt]], writes=[R_zq[b2]])
                    def stats_F(m_):
                        c2 = m_ % 2
                        P.op("tensor", lambda e, m_=m_: e.matmul(banks[2][:, :], lhsT=ones_f[:, :], rhs=z2[:, m_, :],
                                                                  start=(m_ == 0), stop=(m_ == 7)), reads=[R_z2m[m_], R_const], writes=[RB[2]])
                        P.op("tensor", lambda e, m_=m_, c2=c2: e.matmul(banks[3][:, :], lhsT=ones_f[:, :], rhs=zq[c2][:],
                                                                         start=(m_ == 0), stop=(m_ == 7)), reads=[R_zq[c2], R_const], writes=[RB[3]])
                    if mt >= 1:
                        stats_F(mt - 1)
                    if mt == 7:
                        stats_F(7)
                P.op("vector", lambda e: e.tensor_scalar(out=meanF[:], in0=banks[2][:, :], scalar1=1.0 / D, scalar2=None, op0=ALU.mult),
                     reads=[RB[2]], writes=[R_mean])
                P.op("vector", lambda e: e.tensor_tensor(out=msqF[:], in0=meanF[:], in1=meanF[:], op=ALU.mult), reads=[R_mean], writes=[R_msq])
                P.op("vector", lambda e: e.scalar_tensor_tensor(out=varF[:], in0=banks[3][:, :], scalar=1.0 / D, in1=msqF[:],
                                                               op0=ALU.mult, op1=ALU.subtract), reads=[RB[3], R_msq], writes=[R_var])
                P.op("scalar", lambda e: e.activation(out=sdF[:], in_=varF[:], func=AF.Sqrt, bias=eps_t[:, 0:1], scale=1.0),
                     reads=[R_var, R_const], writes=[R_sd])
                P.op("vector", lambda e: e.reciprocal(out=rsF[:], in_=sdF[:]), reads=[R_sd], writes=[R_rs])
                for mt in range(8):
                    b2 = mt % 2
                    P.op("vector", lambda e, mt=mt, b2=b2: e.tensor_tensor(out=tnF[b2][:], in0=z2[:, mt, :], in1=meanF[:], op=ALU.subtract),
                         reads=[R_z2m[mt], R_mean], writes=[R_tn[b2]])
                    P.op("vector", lambda e, b2=b2: e.tensor_tensor(out=tn2F[b2][:], in0=tnF[b2][:], in1=rsF[:], op=ALU.mult),
                         reads=[R_tn[b2], R_rs], writes=[R_tn2[b2]])
                    P.op("scalar", lambda e, mt=mt, b2=b2: e.activation(
                        out=z2[:, mt, :], in_=tn2F[b2][:], func=AF.Identity, bias=l2b[:, mt:mt + 1], scale=l2g[:, mt:mt + 1]),
                        reads=[R_tn2[b2], R_c], writes=[R_z2m[mt]])
                for tt in range(4):
                    o2 = tt % 2
                    for mt in range(8):
                        bk = 4 + 2 * o2 + mt // 4
                        P.op("tensor", lambda e, bk=bk, mt=mt, tt=tt: e.transpose(
                            out=banks[bk][:, (mt % 4) * 128:(mt % 4 + 1) * 128], in_=z2[:, mt, tt * 128:(tt + 1) * 128],
                            identity=ident_f[:, :]), reads=[R_z2m[mt], R_const], writes=[RB[bk]])
                    for hb2 in range(2):
                        bk = 4 + 2 * o2 + hb2
                        if hb2 == 0:
                            P.op("scalar", lambda e, bk=bk, o2=o2: e.copy(out=ost[o2][:, 0:512], in_=banks[bk][:, :]),
                                 reads=[RB[bk]], writes=[R_ost[o2]])
                        else:
                            P.op("vector", lambda e, bk=bk, o2=o2: e.tensor_copy(out=ost[o2][:, 512:1024], in_=banks[bk][:, :]),
                                 reads=[RB[bk]], writes=[R_ost[o2]])
                    r0 = tb0 + tt * 128
                    P.op("sync", lambda e, o2=o2, r0=r0: e.dma_start(out=out_d[r0:r0 + 128, :], in_=ost[o2][:]),
                         reads=[R_ost[o2]], dma="out%d" % o2)
        P.end("phaseEF")

    return nc, dbg_out, es


def make_in_maps(inp):
    x = np.asarray(inp["x"], np.float32)
    cst = _consts()
    shared = {}
    shared["w_in"] = np.ascontiguousarray(np.asarray(inp["w_in"], np.float32)[0].reshape(KT, 128, 4608))
    shared["b_gate"] = _fm(np.asarray(inp["b_gate"])[0], 16)
    shared["sgu_ln_g"] = np.ascontiguousarray(np.broadcast_to(np.asarray(inp["sgu_ln_g"], np.float32)[0][None], (128, 512)))
    shared["sgu_ln_b"] = np.ascontiguousarray(np.broadcast_to(np.asarray(inp["sgu_ln_b"], np.float32)[0][None], (128, 512)))
    ws = np.asarray(inp["w_spatial"], np.float32)[0]
    shared["wsT"] = np.ascontiguousarray(ws.transpose(2, 0, 1))
    bs = np.asarray(inp["b_spatial"], np.float32)[0]
    shared["bsT"] = np.ascontiguousarray(np.repeat(bs.reshape(4, 2, 1, 128), 64, axis=2).reshape(4, 128, 128).transpose(1, 0, 2))
    shared["w_branch_a"] = np.ascontiguousarray(np.asarray(inp["w_branch_a"], np.float32)[0].reshape(4, 128, D))
    shared["w_branch_b"] = np.ascontiguousarray(np.asarray(inp["w_branch_b"], np.float32)[0].reshape(4, 128, D))
    shared["w_out"] = np.ascontiguousarray(np.asarray(inp["w_out"], np.float32)[0].reshape(KT, 128, D))
    shared["ln1_g"] = _fm(np.asarray(inp["ln1_g"])[0], 8)
    shared["ln1_b"] = _fm(np.asarray(inp["ln1_b"])[0], 8)
    shared["w_up"] = np.ascontiguousarray(np.asarray(inp["w_up"], np.float32)[0].reshape(KT, 128, 2 * DFF))
    cw = np.asarray(inp["conv_w"], np.float32)[0]
    shared["conv_w"] = np.ascontiguousarray(cw.reshape(3, 44, 128).transpose(2, 1, 0))
    shared["conv_b"] = _fm(np.asarray(inp["conv_b"])[0], 44)
    shared["w_down"] = np.ascontiguousarray(np.asarray(inp["w_down"], np.float32)[0].reshape(NCH, 128, D))
    shared["ln2_g"] = _fm(np.asarray(inp["ln2_g"])[0], 8)
    shared["ln2_b"] = _fm(np.asarray(inp["ln2_b"])[0], 8)
    for k in ("perms", "ident", "mtri", "eind", "trilT"):
        shared[k] = cst[k]
    in_maps = []
    zero_blk = np.zeros((BLK, D), np.float32)
    for c in range(8):
        b, j = c // 4, c % 4
        perm = _perm_for(j)
        xp = np.concatenate([x[b, p * BLK:(p + 1) * BLK] if p >= 0 else zero_blk for p in perm], axis=0)
        m = dict(shared)
        m["xT_all"] = np.ascontiguousarray(xp.T).reshape(KT, 128, SLOTC)
        m["rope_all"] = _rope_tables(perm)
        m["vmask"] = _vmask(perm)
        hs = np.ones((128, 2), np.float32)
        if j == 0:
            hs[:, 0] = 0.0
        m["hscale"] = hs
        in_maps.append(m)
    return in_maps


def kernel(**inputs):
    in_maps = make_in_maps(inputs)
    nc, _, es = build("F", False)
    res = run_bass_kernel_spmd(nc, in_maps, core_ids=list(range(8)))
    out = np.zeros((2, SEQ, D), np.float32)
    H = NTOK // 2
    for c in range(8):
        b, j = c // 4, c % 4
        o = res.results[c]["out"]
        out[b, j * H:(j + 1) * H] = o[0:H]
        out[b, (7 - j) * H:(8 - j) * H] = o[H:2 * H]
    return out
```

```python
import contextlib
import numpy as np
import concourse.bass as bass
import concourse.mybir as mybir
from concourse.bass_utils import run_bass_kernel_spmd

F32 = mybir.dt.float32
BF16 = mybir.dt.bfloat16
AF = mybir.ActivationFunctionType
ALU = mybir.AluOpType
AX = mybir.AxisListType

D = 1024
KT = 8
SEQ = 8192
NBSEQ = 32
NB = 33
BLK = 256
SLOTC = NB * BLK
NOWN = 10
TOWN = NOWN * BLK
HALOS = (0, 5)
NOTH_A = 11
GS = 36
KROWS = 64 + NB
NTOK = 2048
DFF = 2816
NCH = 22
ALPHA = 2.0 ** 0.25
EPS = 1e-5
BIG = 32768.0
X1C = 2 * (NTOK // 2 + 2)
NSLOT_RUN = NB
N_OWN_RUN = NOWN
N_HEAD_RUN = 8
N_HALF_RUN = 2
B_PARTS = "uqgs"
Q_STAGE = 9
QM_TILES = None
G_STAGE = 9
B_TILES = None

ENGS = ("tensor", "scalar", "vector", "gpsimd", "sync")


class Res:
    __slots__ = ("w", "r", "name", "excl")

    def __init__(self, name="", excl=False):
        self.w = None
        self.r = {}
        self.name = name
        self.excl = excl


class Op:
    __slots__ = ("eng", "fn", "deps", "inc", "count", "dma", "sem", "phase")

    def __init__(self, eng, fn, dma):
        self.eng = eng
        self.fn = fn
        self.deps = []
        self.inc = False
        self.count = None
        self.dma = dma
        self.sem = None


class Prog:
    def __init__(self, nc, es):
        self.nc = nc
        self.es = es
        self.esem = {e: es.enter_context(nc.semaphore("s_" + e)) for e in ENGS}
        self.ecount = {e: 0 for e in ENGS}
        self.dsem = {}
        self.dcount = {}
        self.ops = None
        self.dma_ops = None
        self.nblock = 0

    def begin(self):
        self.ops = {e: [] for e in ENGS}
        self.dma_ops = []

    def _dsem(self, key):
        if key not in self.dsem:
            self.dsem[key] = self.es.enter_context(self.nc.semaphore("d_" + key))
            self.dcount[key] = 0
        return self.dsem[key]

    def op(self, eng, fn, reads=(), writes=(), dma=None):
        if dma is not None and dma.endswith("*"):
            self.nuniq = getattr(self, "nuniq", 0) + 1
            dma = dma[:-1] + "_u%d" % self.nuniq
        o = Op(eng, fn, dma)
        o.phase = self.nblock
        deps = []
        for r in reads:
            if r.w is not None:
                deps.append(r.w)
            if r.excl:
                for k, rd in r.r.items():
                    if k != eng:
                        deps.append(rd)
        for w in writes:
            if w.w is not None:
                deps.append(w.w)
            for rd in w.r.values():
                deps.append(rd)
        seen = set()
        for d in deps:
            if id(d) in seen or d is o or d.phase != self.nblock:
                continue
            seen.add(id(d))
            if d.dma is None and d.eng == "tensor" and eng == "tensor" and dma is None:
                continue
            o.deps.append(d)
            d.inc = True
        for r in reads:
            r.r[eng if dma is None else ("dma", dma)] = o
        for w in writes:
            w.w = o
            w.r = {}
        if dma is not None:
            o.sem = self._dsem(dma)
            self.dcount[dma] += 16
            o.count = self.dcount[dma]
            self.dma_ops.append(o)
        self.ops[eng].append(o)
        return o

    def end(self, name=None):
        nc = self.nc
        finals = {}
        for o in self.dma_ops:
            finals[o.dma] = (o.sem, max(o.count, finals.get(o.dma, (None, 0))[1]))
        fin = Op("sync", None, None)
        self.ops["sync"].append(fin)
        for e in ENGS:
            for o in self.ops[e]:
                if o.dma is None and o.inc:
                    self.ecount[e] += 1
                    o.count = self.ecount[e]
                    o.sem = self.esem[e]
        self.nblock += 1
        with nc.Block(name or ("blk%d" % self.nblock)) as block:
            for e in ENGS:
                ops = self.ops[e]

                def body(eh, ops=ops):
                    waited = {}
                    for o in ops:
                        if o is fin:
                            for (s, c) in finals.values():
                                eh.wait_ge(s, c)
                            continue
                        for d in o.deps:
                            k = id(d.sem)
                            if waited.get(k, 0) < d.count:
                                eh.wait_ge(d.sem, d.count)
                                waited[k] = d.count
                        inst = o.fn(eh)
                        if o.dma is not None:
                            inst.then_inc(o.sem, 16)
                        elif o.inc:
                            inst.then_inc(o.sem, 1)

                getattr(block, e)(body)
        self.ops = None
        self.dma_ops = None


def _perm_for(j):
    A = [4 * j + i for i in range(4)]
    hA = 4 * j - 1
    B = [28 - 4 * j + i for i in range(4)]
    hB = 27 - 4 * j
    own = [hA] + A + [hB] + B
    ownset = set(b for b in own if b >= 0)
    past = sorted(b for b in range(NBSEQ) if b < hB and b not in ownset)
    left = [b for b in range(NBSEQ) if b not in ownset and b not in past]
    fill = list(left)
    while len(past) + len(fill) < NB - NOWN:
        fill.append(NBSEQ - 1)
    others = (past + fill)[: NB - NOWN]
    assert len(past) <= NB - NOWN
    return own + others


def _consts():
    c = {}
    perm = np.zeros((128, 128), np.float32)
    for m in range(128):
        if (m % 64) < 32:
            perm[m + 32, m] = -1.0
        else:
            perm[m - 32, m] = 1.0
    c["perms"] = perm
    c["ident"] = np.eye(128, dtype=np.float32)
    c["ones"] = np.ones((128, 128), np.float32)
    k = np.arange(128)[:, None]
    q = np.arange(128)[None, :]
    tri = np.where(k <= q, 0.0, -BIG).astype(np.float32)
    m0 = np.concatenate([tri, np.zeros((128, 128), np.float32)], axis=1)
    m1 = np.concatenate([np.full((128, 128), -BIG, np.float32), tri], axis=1)
    c["mtri"] = np.stack([m0, m1], axis=1)
    e = np.zeros((NB, SLOTC), np.float32)
    for j in range(NB):
        e[j, j * BLK:(j + 1) * BLK] = BIG
    c["eind"] = e
    s = np.arange(128)[:, None]
    t = np.arange(128)[None, :]
    c["trilT"] = (s <= t).astype(np.float32)
    return c


def _rope_tables(perm):
    half = 32
    inv_freq = (np.float32(10000.0) ** (-np.arange(half, dtype=np.float32) / np.float32(half))).astype(np.float32)
    pos = np.concatenate([np.arange(b * BLK, (b + 1) * BLK) for b in perm]).astype(np.float32)
    ang = (pos[None, :] * inv_freq[:, None]).astype(np.float32)
    cos = np.cos(ang).astype(np.float32)
    sin = np.sin(ang).astype(np.float32)
    cos = np.tile(cos, (4, 1))
    sin = np.tile(sin, (4, 1))
    return np.stack([cos, sin], axis=1)


def _vmask(perm):
    vm = np.full((NOWN, NB), -1e30, np.float32)
    first = {}
    for s_, b in enumerate(perm):
        if b >= 0 and b not in first:
            first[b] = s_
    for i in range(NOWN):
        a = perm[i]
        for s_ in range(NB):
            b = perm[s_]
            if b >= 0 and b < a and first[b] == s_:
                vm[i, s_] = 0.0
    return np.ascontiguousarray(np.broadcast_to(vm[None], (128, NOWN, NB)))


def _fm(v, n):
    return np.ascontiguousarray(np.asarray(v, np.float32).reshape(n, 128).T)


def build(upto="F", dbg=False):
    nc = bass.Bass("TRN2", target_bir_lowering=False)
    es = contextlib.ExitStack()

    def din(name, shape, dt=F32):
        return nc.dram_tensor(name, list(shape), dt, kind="ExternalInput").ap()

    xT_all = din("xT_all", [KT, 128, SLOTC])
    rope_all = din("rope_all", [128, 2, SLOTC])
    vmask_d = din("vmask", [128, NOWN, NB])
    hscale_d = din("hscale", [128, 2])
    w_in_d = din("w_in", [KT, 128, 4608])
    bgate_d = din("b_gate", [128, 16])
    lng_d = din("sgu_ln_g", [128, 512])
    lnb_d = din("sgu_ln_b", [128, 512])
    wsT_d = din("wsT", [128, 8, 128])
    bsT_d = din("bsT", [128, 4, 128])
    wa_d = din("w_branch_a", [4, 128, D])
    wb_d = din("w_branch_b", [4, 128, D])
    wo_d = din("w_out", [KT, 128, D])
    ln1g_d = din("ln1_g", [128, 8])
    ln1b_d = din("ln1_b", [128, 8])
    wup_d = din("w_up", [KT, 128, 2 * DFF])
    cw_d = din("conv_w", [128, 44, 3])
    cb_d = din("conv_b", [128, 44])
    wd_d = din("w_down", [NCH, 128, D])
    ln2g_d = din("ln2_g", [128, 8])
    ln2b_d = din("ln2_b", [128, 8])
    perms_d = din("perms", [128, 128])
    ident_d = din("ident", [128, 128])
    mtri_d = din("mtri", [128, 2, 256])
    eind_d = din("eind", [NB, SLOTC])
    trilT_d = din("trilT", [128, 128])

    out_d = nc.dram_tensor("out", [NTOK, D], F32, kind="ExternalOutput").ap()

    kT_scr = nc.dram_tensor("kT_scr", [8, 64, SLOTC], BF16, kind="Internal").ap()
    v_scr = nc.dram_tensor("v_scr", [SLOTC, 8, 128], BF16, kind="Internal").ap()
    x1_scr = nc.dram_tensor("x1_scr", [128, KT, X1C], F32, kind="Internal").ap()
    wgS = nc.dram_tensor("wgS", [KT, 128, 2 * D], BF16, kind="Internal").ap()
    waS = nc.dram_tensor("waS", [4, 128, D], BF16, kind="Internal").ap()
    wbS = nc.dram_tensor("wbS", [4, 128, D], BF16, kind="Internal").ap()
    woS = nc.dram_tensor("woS", [KT, 128, D], BF16, kind="Internal").ap()
    wupS = nc.dram_tensor("wupS", [KT, 128, 2 * DFF], BF16, kind="Internal").ap()
    wdS = nc.dram_tensor("wdS", [NCH, 128, D], BF16, kind="Internal").ap()

    dbg_out = {}

    def dout(name, shape, dt=F32):
        a = nc.dram_tensor(name, list(shape), dt, kind="ExternalOutput").ap()
        dbg_out[name] = a
        return a

    P = Prog(nc, es)

    def sb(name, shape, dt):
        return es.enter_context(nc.sbuf_tensor(name, list(shape), dt))

    ps_all = es.enter_context(nc.psum_tensor("ps_all", [128, 8 * 512], F32))
    banks = [ps_all[:, b * 512:(b + 1) * 512] for b in range(8)]
    RB = [Res("bank%d" % b, excl=True) for b in range(8)]

    ident_bf = sb("ident_bf", [128, 128], BF16)
    ident_f = sb("ident_f", [128, 128], F32)
    ones_f = sb("ones_f", [128, 128], F32)
    perms_bf = sb("perms_bf", [128, 128], BF16)
    kmean = sb("kmean", [128, 4, NB], F32)
    eps_t = sb("eps_t", [128, 1], F32)
    R_kmean = Res("kmean")

    with contextlib.ExitStack() as pa:
        def sba(name, shape, dt):
            return pa.enter_context(nc.sbuf_tensor(name, list(shape), dt))

        wk = sba("wk", [128, KT, 512], BF16)
        wv = sba("wv", [128, KT, 512], BF16)
        xt = [sba("xtA%d" % i, [128, KT, BLK], BF16) for i in range(2)]
        cs = [sba("csA%d" % i, [128, 2, BLK], F32) for i in range(2)]
        kb = [sba("kbA%d" % i, [128, 4, BLK], BF16) for i in range(2)]
        ta = [sba("taA%d" % i, [128, 2, BLK], F32) for i in range(2)]
        tb = [sba("tbA%d" % i, [128, 2, BLK], F32) for i in range(2)]
        kst = [sba("kstA%d" % i, [128, 4, BLK], BF16) for i in range(2)]
        vst = [sba("vstA%d" % i, [128, 2, 8, 128], BF16) for i in range(2)]
        ksum = sba("ksum", [128, 4, NB], F32)

        P.begin()
        R_const = Res("const")
        R_wk, R_wv = Res("wk"), Res("wv")
        R_xt = [Res("xt0"), Res("xt1")]
        R_cs = [Res("cs0"), Res("cs1")]
        R_kraw = [[RB[2 * a + p // 2] for p in range(4)] for a in range(2)]
        R_rot = [RB[4 + p % 2] for p in range(4)]
        R_vps = [RB[6], RB[7]]
        R_kb = [[Res() for p in range(4)] for a in range(2)]
        R_ta, R_tb = [Res(), Res()], [Res(), Res()]
        R_kst = [[Res() for p in range(4)] for a in range(2)]
        R_vst = [Res(), Res()]
        R_ksum = Res("ksum")
        R_scr = Res("scr")

        w_in_v = w_in_d.rearrange("k p c -> p k c")
        xT_v = xT_all.rearrange("k p t -> p k t")
        P.op("gpsimd", lambda e: e.dma_start(out=ident_bf[:], in_=ident_d[:, :]), writes=[R_const], dma="c0*")
        P.op("gpsimd", lambda e: e.dma_start(out=perms_bf[:], in_=perms_d[:, :]), writes=[R_const], dma="c0*")
        P.op("sync", lambda e: e.dma_start(out=ident_f[:], in_=ident_d[:, :]), writes=[R_const], dma="c1*")
        P.op("vector", lambda e: e.memset(ones_f[:], 1.0), writes=[R_const])
        P.op("vector", lambda e: e.memset(eps_t[:], EPS), writes=[R_const])
        P.op("gpsimd", lambda e: e.dma_start(out=wk[:], in_=w_in_v[:, :, 1536:2048]), writes=[R_wk], dma="wk*")
        P.op("gpsimd", lambda e: e.dma_start(out=wv[:], in_=w_in_v[:, :, 2048:2560]), writes=[R_wv], dma="wv*")
        for i in range(2):
            P.op("vector", lambda e, i=i: e.memset(vst[i][:], 1.0), writes=[R_vst[i]])
        P.op("vector", lambda e: e.memset(ksum[:], 0.0), writes=[R_ksum])

        kT_v = kT_scr.rearrange("(q hh) d t -> (hh d) q t", hh=2)
        v_v = v_scr.rearrange("(s p) h e -> p s h e", p=128)

        def load_A(s_):
            if s_ >= NSLOT_RUN:
                return
            a_ = s_ % 2
            c0_ = s_ * BLK
            P.op("gpsimd", lambda e, a_=a_, c0_=c0_: e.dma_start(out=xt[a_][:], in_=xT_v[:, :, c0_:c0_ + BLK]),
                 writes=[R_xt[a_]], dma="xt%d" % a_)
            P.op("sync", lambda e, a_=a_, c0_=c0_: e.dma_start(out=cs[a_][:], in_=rope_all[:, :, c0_:c0_ + BLK]),
                 writes=[R_cs[a_]], dma="cs%d" % a_)

        load_A(0)
        for s in range(NSLOT_RUN):
            a = s % 2
            c0 = s * BLK
            load_A(s + 1)
            for p in range(4):
                kr = banks[2 * a + p // 2][:, (p % 2) * BLK:(p % 2 + 1) * BLK]
                for kt in range(KT):
                    P.op("tensor", lambda e, kr=kr, a=a, p=p, kt=kt: e.matmul(
                        kr, lhsT=wk[:, kt, p * 128:(p + 1) * 128], rhs=xt[a][:, kt, :],
                        start=(kt == 0), stop=(kt == KT - 1)),
                        reads=[R_wk, R_xt[a]], writes=[R_kraw[a][p]])
            for sub in range(2):
                vp = banks[6 + sub]
                for kt in range(KT):
                    P.op("tensor", lambda e, vp=vp, a=a, sub=sub, kt=kt: e.matmul(
                        vp[:, :], lhsT=xt[a][:, kt, sub * 128:(sub + 1) * 128], rhs=wv[:, kt, :],
                        start=(kt == 0), stop=(kt == KT - 1)),
                        reads=[R_wv, R_xt[a]], writes=[R_vps[sub]])
            for q2 in range(2):
                kr2 = banks[2 * a + q2][:, :].rearrange("p (j n) -> p j n", j=2)
                rp2 = banks[4 + q2][:, :].rearrange("p (j n) -> p j n", j=2)
                RK = R_kraw[a][2 * q2]
                RR = RB[4 + q2]
                P.op("scalar", lambda e, kr2=kr2, a=a, q2=q2: e.copy(out=kb[a][:, 2 * q2:2 * q2 + 2, :], in_=kr2),
                     reads=[RK], writes=[R_kb[a][q2]])
                for j in range(2):
                    p = 2 * q2 + j
                    P.op("tensor", lambda e, q2=q2, j=j, a=a, p=p: e.matmul(
                        banks[4 + q2][:, j * BLK:(j + 1) * BLK], lhsT=perms_bf[:, :], rhs=kb[a][:, p, :], start=True, stop=True),
                        reads=[R_kb[a][q2], R_const], writes=[RR])
                P.op("vector", lambda e, kr2=kr2, a=a, q2=q2: e.tensor_tensor(
                    out=ta[q2][:], in0=kr2, in1=cs[a][:, 0:1, :].to_broadcast([128, 2, BLK]), op=ALU.mult),
                    reads=[RK, R_cs[a]], writes=[R_ta[q2]])
                P.op("vector", lambda e, rp2=rp2, a=a, q2=q2: e.tensor_tensor(
                    out=tb[q2][:], in0=rp2, in1=cs[a][:, 1:2, :].to_broadcast([128, 2, BLK]), op=ALU.mult),
                    reads=[RR, R_cs[a]], writes=[R_tb[q2]])
                P.op("vector", lambda e, a=a, q2=q2: e.tensor_tensor(
                    out=kst[a][:, 2 * q2:2 * q2 + 2, :], in0=ta[q2][:], in1=tb[q2][:], op=ALU.add),
                    reads=[R_ta[q2], R_tb[q2]], writes=[R_kst[a][q2]])
                P.op("vector", lambda e, a=a, q2=q2, s=s: e.tensor_reduce(
                    out=ksum[:, 2 * q2:2 * q2 + 2, s], in_=kst[a][:, 2 * q2:2 * q2 + 2, :], axis=AX.X, op=ALU.add),
                    reads=[R_kst[a][q2]], writes=[R_ksum])
            P.op("sync", lambda e, a=a, c0=c0: e.dma_start(out=kT_v[:, :, c0:c0 + BLK], in_=kst[a][:]),
                 reads=R_kst[a][0:2], writes=[], dma="ko%d" % a)
            for sub in range(2):
                P.op("scalar", lambda e, a=a, sub=sub: e.copy(
                    out=vst[a][:, sub, :, 0:64], in_=banks[6 + sub][:, :].rearrange("p (h e) -> p h e", h=8)),
                    reads=[R_vps[sub]], writes=[R_vst[a]])
            P.op("sync", lambda e, a=a, s=s: e.dma_start(out=v_v[:, 2 * s:2 * s + 2, :, :], in_=vst[a][:]),
                 reads=[R_vst[a]], writes=[], dma="vo%d" % a)
        P.op("vector", lambda e: e.tensor_scalar(out=kmean[:], in0=ksum[:], scalar1=1.0 / BLK, scalar2=None,
                                                 op0=ALU.mult),
             reads=[R_ksum], writes=[R_kmean])
        if dbg:
            dk = dout("dbg_kmean", [128, 4, NB])
            P.op("sync", lambda e: e.dma_start(out=dk[:, :, :], in_=kmean[:]), reads=[R_kmean], dma="dbg*")
        P.end("phaseA")

    if dbg and upto == "A":
        nt = NSLOT_RUN * BLK
        dko = dout("dbg_kT", [8, 64, nt], BF16)
        dvo = dout("dbg_v", [nt, 8, 128], BF16)
        P.begin()
        P.op("sync", lambda e: e.dma_start(out=dko[:, :, :], in_=kT_scr[:, :, 0:nt]), dma="dbg*")
        P.op("sync", lambda e: e.dma_start(out=dvo[:, :, :], in_=v_scr[0:nt, :, :]), dma="dbg*")
        P.end("dump")
    if upto == "A":
        return nc, dbg_out, es

    w_in_v = w_in_d.rearrange("k p c -> p k c")
    xT_v = xT_all.rearrange("k p t -> p k t")
    x1b = sb("x1b", [128, KT, X1C], BF16)
    R_x1b = Res("x1b")

    with contextlib.ExitStack() as pbd:
        def sbp(name, shape, dt):
            return pbd.enter_context(nc.sbuf_tensor(name, list(shape), dt))
        yAT = sbp("yAT", [128, 4, TOWN], BF16)
        yBT = sbp("yBT", [128, 4, TOWN], BF16)
        R_yAT, R_yBT = Res("yAT"), Res("yBT")
        with contextlib.ExitStack() as pbc:
            QM = pbc.enter_context(nc.sbuf_tensor("QM", [KROWS, 8, TOWN], BF16))
            R_QM = Res("QM")
            with contextlib.ExitStack() as pb:
                def sbb(name, shape, dt):
                    return pb.enter_context(nc.sbuf_tensor(name, list(shape), dt))
                wu = sbb("wu", [128, KT, 512], BF16)
                wq = sbb("wq", [128, KT, 512], BF16)
                wva = sbb("wva", [128, KT, 512], BF16)
                xt = [sbb("xtB%d" % i, [128, KT, BLK], BF16) for i in range(3)]
                cs = [sbb("csB%d" % i, [128, 2, BLK], F32) for i in range(3)]
                ug = [sbb("ugB%d" % i, [128, 4, BLK], BF16) for i in range(2)]
                qb = [[sbb("qbB%d_%d" % (j, i), [128, 2, BLK], BF16) for i in range(2)] for j in range(2)]
                ta = [[sbb("taB%d_%d" % (j, i), [128, 2, BLK], F32) for i in range(2)] for j in range(2)]
                tb = [sbb("tbB%d" % i, [128, 2, BLK], F32) for i in range(2)]
                qf = [sbb("qfB%d" % i, [128, 2, BLK], F32) for i in range(2)]
                vg = [sbb("vgB%d" % i, [128, 512], F32) for i in range(2)]
                vn = [sbb("vn_%d" % i, [128, 512], BF16) for i in range(2)]
                tt4 = [sbb("tt4_%d" % i, [128, 512], F32) for i in range(2)]
                trilT = sbb("trilT_sb", [128, 128], F32)
                wsT_b = sbb("wsT_b", [128, 8, 128], BF16)
                lng = sbb("lng", [128, 512], F32)
                lnb = sbb("lnb", [128, 512], F32)
                bsT = sbb("bsT_sb", [128, 512], F32)
                vmask = sbb("vmask_sb", [128, NOWN, NB], F32)
                st6 = [sbb("st6_%d" % i, [128, 6], F32) for i in range(2)]
                mv = [sbb("mv_%d" % i, [128, 2], F32) for i in range(2)]
                sd = [sbb("sd_%d" % i, [128, 1], F32) for i in range(2)]
                rstd = [sbb("rstd_%d" % i, [128, 1], F32) for i in range(2)]
                gm = sbb("gm", [128, 16, NB], F32)
                top8 = sbb("top8", [128, 16, 8], F32)
                thr = sbb("thr", [128, 16], F32)
                gsel = sbb("gsel", [128, 16, NB], F32)
                msel = sbb("msel", [128, 16, NB], BF16)

                P.begin()
                R_w = Res("wB")
                R_c = Res("cB")
                R_xt = [Res(), Res(), Res()]
                R_cs = [Res(), Res(), Res()]
                R_ug = [Res(), Res()]
                R_qb = [[Res(), Res()], [Res(), Res()]]
                R_ta = [[Res(), Res()], [Res(), Res()]]
                R_tb, R_qf = [Res() for _ in range(4)], [Res() for _ in range(4)]
                R_vg = [Res(), Res()]
                R_vn0, R_vn1, R_vn, R_tt4 = [Res(), Res()], [Res(), Res()], [Res(), Res()], [Res(), Res()]
                R_ws = Res()
                R_st, R_mv, R_sd, R_rstd = [Res(), Res()], [Res(), Res()], [Res(), Res()], [Res(), Res()]
                R_gm, R_top, R_thr, R_msel, R_gsel = Res(), Res(), Res(), Res(), Res()

                P.op("gpsimd", lambda e: e.dma_start(out=wu[:], in_=w_in_v[:, :, 0:512]), writes=[R_w], dma="wB*")
                P.op("gpsimd", lambda e: e.dma_start(out=wq[:], in_=w_in_v[:, :, 1024:1536]), writes=[R_w], dma="wB*")
                P.op("gpsimd", lambda e: e.dma_start(out=wva[:], in_=w_in_v[:, :, 512:1024]), writes=[R_w], dma="wB*")
                P.op("gpsimd", lambda e: e.dma_start(out=wsT_b[:], in_=wsT_d[:, :, :]), writes=[R_ws], dma="cB*")
                P.op("sync", lambda e: e.dma_start(out=trilT[:], in_=trilT_d[:, :]), writes=[R_c], dma="cB*")
                P.op("sync", lambda e: e.dma_start(out=lng[:], in_=lng_d[:, :]), writes=[R_c], dma="cB*")
                P.op("sync", lambda e: e.dma_start(out=lnb[:], in_=lnb_d[:, :]), writes=[R_c], dma="cB*")
                P.op("sync", lambda e: e.dma_start(out=bsT[:], in_=bsT_d.rearrange("p a t -> p (a t)")), writes=[R_c], dma="cB*")
                P.op("sync", lambda e: e.dma_start(out=vmask[:], in_=vmask_d[:, :, :]), writes=[R_c], dma="cB*")
                if dbg:
                    P.op("gpsimd", lambda e: e.memset(yBT[:], 0.0), writes=[R_yBT])
                    P.op("gpsimd", lambda e: e.memset(yAT[:], 0.0), writes=[R_yAT])
                for g in range(8):
                    P.op("vector", lambda e, g=g: e.tensor_tensor(out=wsT_b[:, g, :], in0=wsT_b[:, g, :], in1=trilT[:], op=ALU.mult),
                         reads=[R_ws, R_c], writes=[R_ws])

                def load_tile_B(it):
                    if it >= N_OWN_RUN:
                        return
                    a_ = it % 3
                    c0_ = it * BLK
                    P.op("gpsimd", lambda e, a_=a_, c0_=c0_: e.dma_start(out=xt[a_][:], in_=xT_v[:, :, c0_:c0_ + BLK]),
                         writes=[R_xt[a_]], dma="xtB%d" % a_)
                    P.op("sync", lambda e, a_=a_, c0_=c0_: e.dma_start(out=cs[a_][:], in_=rope_all[:, :, c0_:c0_ + BLK]),
                         writes=[R_cs[a_]], dma="csB%d" % a_)

                def part1_B(it):
                    a = it % 2
                    x3 = it % 3
                    load_tile_B(it + 1)
                    for mt in range(4):
                        ur = banks[mt // 2][:, (mt % 2) * BLK:(mt % 2 + 1) * BLK]
                        for kt in range(KT):
                            P.op("tensor", lambda e, ur=ur, x3=x3, mt=mt, kt=kt: e.matmul(
                                ur, lhsT=wu[:, kt, mt * 128:(mt + 1) * 128], rhs=xt[x3][:, kt, :],
                                start=(kt == 0), stop=(kt == KT - 1)), reads=[R_w, R_xt[x3]], writes=[RB[mt // 2]])
                    for p in range(4):
                        qr = banks[2 + p // 2][:, (p % 2) * BLK:(p % 2 + 1) * BLK]
                        for kt in range(KT):
                            P.op("tensor", lambda e, qr=qr, x3=x3, p=p, kt=kt: e.matmul(
                                qr, lhsT=wq[:, kt, p * 128:(p + 1) * 128], rhs=xt[x3][:, kt, :],
                                start=(kt == 0), stop=(kt == KT - 1)), reads=[R_w, R_xt[x3]], writes=[RB[2 + p // 2]])
                    for q2 in range(2):
                        ur2 = banks[q2][:, :].rearrange("p (j n) -> p j n", j=2)
                        P.op("scalar", lambda e, ur2=ur2, a=a, q2=q2: e.activation(out=ug[a][:, 2 * q2:2 * q2 + 2, :], in_=ur2, func=AF.Gelu),
                             reads=[RB[q2]], writes=[R_ug[a]])
                    for q2 in range(2):
                        qr2 = banks[2 + q2][:, :].rearrange("p (j n) -> p j n", j=2)
                        P.op("scalar", lambda e, qr2=qr2, a=a, q2=q2: e.copy(out=qb[a][q2][:], in_=qr2), reads=[RB[2 + q2]], writes=[R_qb[a][q2]])
                    for q2 in range(2):
                        qr2 = banks[2 + q2][:, :].rearrange("p (j n) -> p j n", j=2)
                        P.op("vector", lambda e, qr2=qr2, a=a, q2=q2, x3=x3: e.tensor_tensor(
                            out=ta[a][q2][:], in0=qr2, in1=cs[x3][:, 0:1, :].to_broadcast([128, 2, BLK]), op=ALU.mult),
                            reads=[RB[2 + q2], R_cs[x3]], writes=[R_ta[a][q2]])

                def part2_B(it):
                    a = it % 2
                    x3 = it % 3
                    c0 = it * BLK
                    subs = [1] if it in HALOS else [0, 1]
                    for p in range(4):
                        rp = banks[4 + p // 2][:, (p % 2) * BLK:(p % 2 + 1) * BLK]
                        P.op("tensor", lambda e, rp=rp, p=p, a=a: e.matmul(rp, lhsT=perms_bf[:, :], rhs=qb[a][p // 2][:, p % 2, :], start=True, stop=True),
                             reads=[R_qb[a][p // 2]], writes=[RB[4 + p // 2]])
                    for sub in subs:
                        for kt in range(KT):
                            P.op("tensor", lambda e, x3=x3, sub=sub, kt=kt: e.matmul(
                                banks[6 + sub][:, :], lhsT=xt[x3][:, kt, sub * 128:(sub + 1) * 128], rhs=wva[:, kt, :],
                                start=(kt == 0), stop=(kt == KT - 1)), reads=[R_w, R_xt[x3]], writes=[RB[6 + sub]])
                    for q2 in range(2):
                        rp2 = banks[4 + q2][:, :].rearrange("p (j n) -> p j n", j=2)
                        P.op("vector", lambda e, rp2=rp2, x3=x3, q2=q2: e.tensor_tensor(
                            out=tb[q2][:], in0=rp2, in1=cs[x3][:, 1:2, :].to_broadcast([128, 2, BLK]), op=ALU.mult),
                            reads=[RB[4 + q2], R_cs[x3]], writes=[R_tb[q2]])
                        P.op("vector", lambda e, q2=q2, a=a: e.tensor_tensor(out=qf[q2][:], in0=ta[a][q2][:], in1=tb[q2][:], op=ALU.add),
                             reads=[R_ta[a][q2], R_tb[q2]], writes=[R_qf[q2]])
                        for hh in range(2):
                            h0 = 4 * q2 + hh
                            P.op("vector", lambda e, q2=q2, hh=hh, h0=h0, c0=c0, a=a: e.tensor_tensor(
                                out=QM[0:64, h0:h0 + 3:2, c0:c0 + BLK], in0=ta[a][q2][hh * 64:(hh + 1) * 64, :, :],
                                in1=tb[q2][hh * 64:(hh + 1) * 64, :, :], op=ALU.add),
                                reads=[R_ta[a][q2], R_tb[q2]], writes=[R_QM])
                    for sub in subs:
                        P.op("scalar", lambda e, sub=sub: e.activation(out=vg[sub][:], in_=banks[6 + sub][:, :], func=AF.Gelu),
                             reads=[RB[6 + sub]], writes=[R_vg[sub]])
                    for sub in subs:
                        P.op("vector", lambda e, sub=sub: e.bn_stats(out=st6[sub][:], in_=vg[sub][:]), reads=[R_vg[sub]], writes=[R_st[sub]])
                        P.op("vector", lambda e, sub=sub: e.bn_aggr(out=mv[sub][:], in_=st6[sub][:]), reads=[R_st[sub]], writes=[R_mv[sub]])
                        P.op("scalar", lambda e, sub=sub: e.activation(out=sd[sub][:], in_=mv[sub][:, 1:2], func=AF.Sqrt, bias=eps_t[:, 0:1], scale=1.0),
                             reads=[R_mv[sub], R_const], writes=[R_sd[sub]])
                        P.op("vector", lambda e, sub=sub: e.reciprocal(out=rstd[sub][:], in_=sd[sub][:]), reads=[R_sd[sub]], writes=[R_rstd[sub]])
                        P.op("vector", lambda e, sub=sub: e.tensor_scalar(
                            out=vg[sub][:], in0=vg[sub][:], scalar1=mv[sub][:, 0:1], scalar2=rstd[sub][:, 0:1],
                            op0=ALU.subtract, op1=ALU.mult), reads=[R_vg[sub], R_mv[sub], R_rstd[sub]], writes=[R_vg[sub]])
                        P.op("gpsimd", lambda e, sub=sub: e.tensor_tensor(out=vg[sub][:], in0=vg[sub][:], in1=lng[:], op=ALU.mult),
                             reads=[R_vg[sub], R_c], writes=[R_vg[sub]])
                        P.op("gpsimd", lambda e, sub=sub: e.tensor_tensor(out=vn[sub][:], in0=vg[sub][:], in1=lnb[:], op=ALU.add),
                             reads=[R_vg[sub], R_c], writes=[R_vn[sub]])
                    for p in range(4):
                        for hh in range(2):
                            for sub in subs:
                                jj = p * 2 + sub
                                P.op("tensor", lambda e, p=p, hh=hh, sub=sub, jj=jj: e.matmul(
                                    banks[7 - hh][:, jj * GS:jj * GS + NB],
                                    lhsT=qf[p // 2][hh * 64:(hh + 1) * 64, p % 2, sub * 128:(sub + 1) * 128],
                                    rhs=kmean[hh * 64:(hh + 1) * 64, p, :], start=True, stop=True),
                                    reads=[R_qf[p // 2], R_kmean], writes=[RB[7 - hh]])
                    for sub in subs:
                        for gp in range(4):
                            for hh in range(2):
                                g = 2 * gp + hh
                                P.op("tensor", lambda e, gp=gp, hh=hh, g=g, sub=sub: e.matmul(
                                    banks[4 + sub][hh * 64:(hh + 1) * 64, gp * 128:(gp + 1) * 128],
                                    lhsT=vn[sub][:, g * 64:(g + 1) * 64], rhs=wsT_b[:, g, :], start=True, stop=True),
                                    reads=[R_vn[sub], R_ws], writes=[RB[4 + sub]])
                    if it in HALOS:
                        for hh in range(2):
                            P.op("vector", lambda e, hh=hh: e.memset(gm[:, hh * 8:(hh + 1) * 8, :], 0.0), writes=[R_gm])
                    for hh in range(2):
                        if it in HALOS:
                            for p in range(4):
                                jj = p * 2 + 1
                                P.op("vector", lambda e, hh=hh, jj=jj, it=it: e.tensor_tensor(
                                    out=gm[:, hh * 8 + jj, :], in0=banks[7 - hh][:, jj * GS:jj * GS + NB],
                                    in1=vmask[:, it, :], op=ALU.add), reads=[RB[7 - hh], R_c], writes=[R_gm])
                        else:
                            P.op("vector", lambda e, hh=hh, it=it: e.tensor_tensor(
                                out=gm[:, hh * 8:(hh + 1) * 8, :],
                                in0=banks[7 - hh][:, 0:8 * GS].rearrange("p (j n) -> p j n", j=8)[:, :, 0:NB],
                                in1=vmask[:, it:it + 1, :].to_broadcast([128, 8, NB]), op=ALU.add),
                                reads=[RB[7 - hh], R_c], writes=[R_gm])
                    for j in range(16):
                        P.op("vector", lambda e, j=j: e.max(out=top8[:, j, :], in_=gm[:, j, :]), reads=[R_gm], writes=[R_top])
                    P.op("vector", lambda e: e.tensor_scalar(out=thr[:], in0=top8[:, :, 2], scalar1=-1e29, scalar2=None, op0=ALU.max),
                         reads=[R_top], writes=[R_thr])
                    P.op("vector", lambda e: e.tensor_tensor(
                        out=gsel[:], in0=gm[:], in1=thr[:, :].unsqueeze(2).to_broadcast([128, 16, NB]), op=ALU.is_ge),
                        reads=[R_gm, R_thr], writes=[R_gsel])
                    P.op("vector", lambda e: e.tensor_scalar(out=msel[:], in0=gsel[:], scalar1=-1.0, scalar2=None, op0=ALU.add),
                         reads=[R_gsel], writes=[R_msel])
                    for sub in subs:
                        P.op("vector", lambda e, sub=sub: e.tensor_tensor(out=tt4[sub][:], in0=banks[4 + sub][:, :], in1=bsT[:], op=ALU.add),
                             reads=[RB[4 + sub], R_c], writes=[R_tt4[sub]])
                        P.op("gpsimd", lambda e, a=a, sub=sub, c0=c0: e.tensor_tensor(
                            out=yAT[:, :, c0 + sub * 128:c0 + (sub + 1) * 128],
                            in0=tt4[sub][:].rearrange("p (g t) -> p g t", g=4),
                            in1=ug[a][:, :, sub * 128:(sub + 1) * 128], op=ALU.mult),
                            reads=[R_tt4[sub], R_ug[a]], writes=[R_yAT])
                    for hh in range(2):
                        tpv = banks[7 - hh][0:NB, :].bitcast(BF16)
                        for jj in range(8):
                            P.op("tensor", lambda e, hh=hh, jj=jj, tpv=tpv: e.transpose(
                                out=tpv[:, jj * 128:(jj + 1) * 128], in_=msel[:, hh * 8 + jj, :], identity=ident_bf[:, :]),
                                reads=[R_msel, R_const], writes=[RB[7 - hh]])
                        for p in range(4):
                            P.op("scalar", lambda e, hh=hh, p=p, c0=c0, tpv=tpv: e.copy(
                                out=QM[64:KROWS, 2 * p + hh, c0:c0 + BLK], in_=tpv[:, p * 256:(p + 1) * 256]),
                                reads=[RB[7 - hh]], writes=[R_QM])

                load_tile_B(0)
                part1_B(0)
                for it in range(N_OWN_RUN):
                    if it + 1 < N_OWN_RUN:
                        part1_B(it + 1)
                    part2_B(it)
                if dbg:
                    ntb = N_OWN_RUN * BLK
                    d1 = dout("dbg_QM", [KROWS, 8, ntb], BF16)
                    d2 = dout("dbg_yAT", [128, 4, ntb], BF16)
                    P.op("sync", lambda e: e.dma_start(out=d1[:, :, :], in_=QM[:, :, 0:ntb]), reads=[R_QM], dma="dbg*")
                    P.op("sync", lambda e: e.dma_start(out=d2[:, :, :], in_=yAT[:, :, 0:ntb]), reads=[R_yAT], dma="dbg*")
                P.end("phaseB")
            if upto == "B":
                return nc, dbg_out, es

            with contextlib.ExitStack() as pc:
                def sbc(name, shape, dt):
                    return pc.enter_context(nc.sbuf_tensor(name, list(shape), dt))
                KE = [sbc("KE%d" % i, [KROWS, SLOTC], BF16) for i in range(2)]
                Vb = [sbc("Vb%d" % i, [128, 2 * NB, 128], BF16) for i in range(2)]
                Pt = [sbc("Pt%d" % i, [128, 1024], BF16) for i in range(3)]
                mtri = sbc("mtri_sb", [128, 2, 256], BF16)
                rc = [sbc("rc%d" % i, [64, 2 * BLK], F32) for i in range(2)]
                P.begin()
                R_KE = [Res(), Res()]
                R_V = [[Res() for _ in range(4)] for _ in range(2)]
                R_Pt = [Res(), Res(), Res()]
                R_m = Res()
                R_rc = [Res(), Res()]
                P.op("gpsimd", lambda e: e.dma_start(out=mtri[:], in_=mtri_d[:, :, :]), writes=[R_m], dma="cC*")
                for i in range(2):
                    P.op("gpsimd", lambda e, i=i: e.dma_start(out=KE[i][64:KROWS, :], in_=eind_d[:, :]), writes=[R_KE[i]], dma="cC*")
                v_hv = v_scr.rearrange("(s p) h e -> p s h e", p=128)
                stage_q = []
                for k in range(KT):
                    stage_q.append(lambda e, k=k: e.dma_start(out=wgS[k, :, :], in_=w_in_d[k, :, 2560:4608]))
                stage_q.append(lambda e: e.dma_start(out=waS[:, :, :], in_=wa_d[:, :, :]))
                stage_q.append(lambda e: e.dma_start(out=wbS[:, :, :], in_=wb_d[:, :, :]))
                stage_q.append(lambda e: e.dma_start(out=woS[:, :, :], in_=wo_d[:, :, :]))
                for k in range(KT):
                    stage_q.append(lambda e, k=k: e.dma_start(out=wupS[k, :, :], in_=wup_d[k, :, :]))
                for k in range(0, NCH, 2):
                    stage_q.append(lambda e, k=k: e.dma_start(out=wdS[k:k + 2, :, :], in_=wd_d[k:k + 2, :, :]))

                def stage_some(nmax):
                    for _ in range(nmax):
                        if stage_q:
                            P.op("gpsimd", stage_q.pop(0), dma="stg")
                steps = []
                units = [(0, None), (1, 2), (3, 4), (5, None), (6, 7), (8, 9)]
                for h in range(N_HEAD_RUN):
                    for (t1, t2) in units:
                        if t1 >= N_OWN_RUN:
                            continue
                        oth = list(range(NOWN, min(NOWN + NOTH_A, NSLOT_RUN))) if t1 < 5 else list(range(NOWN, NSLOT_RUN))
                        ust = []
                        if t2 is None:
                            q0 = t1 * BLK + BLK - 2
                            for s_ in list(range(t1)) + oth:
                                ust.append((s_, [(False, q0, 2, 0)], 2))
                            ust.append((t1, [(True, q0, 2, 0)], 2))
                        else:
                            q0 = t1 * BLK
                            for s_ in list(range(t1)) + oth:
                                ust.append((s_, [(False, q0, 2 * BLK, 0)], 2 * BLK))
                            ust.append((t1, [(True, q0, BLK, 0), (False, q0 + BLK, BLK, BLK)], 2 * BLK))
                            ust.append((t2, [(True, q0 + BLK, BLK, BLK)], 2 * BLK))
                        for k, (s_, segs, wtot) in enumerate(ust):
                            steps.append(dict(h=h, t1=t1, t2=t2, s=s_, segs=segs, wtot=wtot,
                                              first=(k == 0), last=(k == len(ust) - 1)))
                loaded = set()
                VSPL = [0, 17, 34, 50, 2 * NB]

                def vpart(k):
                    return max(q for q in range(4) if VSPL[q] <= k)

                def load_head(h):
                    if h in loaded or h >= N_HEAD_RUN:
                        return
                    loaded.add(h)
                    hb = h % 2
                    P.op("sync", lambda e, h=h, hb=hb: e.dma_start(out=KE[hb][0:64, :], in_=kT_scr[h, :, :]),
                         writes=[R_KE[hb]], dma="ke%d" % hb)
                    for q4 in range(4):
                        k0, k1 = VSPL[q4], VSPL[q4 + 1]
                        P.op("sync", lambda e, h=h, hb=hb, k0=k0, k1=k1: e.dma_start(
                            out=Vb[hb][:, k0:k1, :], in_=v_hv[:, k0:k1, h, :]),
                            writes=[R_V[hb][q4]], dma="vb%d_%d" % (hb, q4))

                def s_region(n):
                    k3 = n % 3
                    return ps_all[:, (2 * k3) * 512:(2 * k3 + 2) * 512].rearrange("p (k n) -> p k n", k=2), [RB[2 * k3], RB[2 * k3 + 1]]

                def emit_S(n):
                    st = steps[n]
                    h, s_ = st["h"], st["s"]
                    hb = h % 2
                    k3 = n % 3
                    sreg, rbs = s_region(n)
                    for kt in range(2):
                        kc = s_ * BLK + kt * 128
                        for (own, qc, w, off) in st["segs"]:
                            ov = sreg[:, kt, off:off + w]
                            if not own:
                                P.op("tensor", lambda e, ov=ov, hb=hb, kc=kc, h=h, qc=qc, w=w: e.matmul(
                                    ov, lhsT=KE[hb][0:KROWS, kc:kc + 128], rhs=QM[0:KROWS, h, qc:qc + w], start=True, stop=True),
                                    reads=[R_KE[hb], R_QM], writes=[rbs[kt]])
                            else:
                                mo = BLK - w
                                P.op("tensor", lambda e, ov=ov, hb=hb, kc=kc, h=h, qc=qc, w=w: e.matmul(
                                    ov, lhsT=KE[hb][0:64, kc:kc + 128], rhs=QM[0:64, h, qc:qc + w], start=True, stop=False),
                                    reads=[R_KE[hb], R_QM], writes=[rbs[kt]])
                                P.op("tensor", lambda e, ov=ov, kt=kt, mo=mo: e.matmul(
                                    ov, lhsT=ident_bf[:, :], rhs=mtri[:, kt, mo:BLK], start=False, stop=True),
                                    reads=[R_m, R_const], writes=[rbs[kt]])
                    lo = min(sg_[3] for sg_ in st["segs"])
                    hi = max(sg_[3] + sg_[2] for sg_ in st["segs"])
                    ptv = Pt[k3][:].rearrange("p (k n) -> p k n", k=2)
                    P.op("scalar", lambda e, sreg=sreg, ptv=ptv, lo=lo, hi=hi: e.activation(
                        out=ptv[:, :, lo:hi], in_=sreg[:, :, lo:hi], func=AF.Exp, scale=0.125),
                        reads=rbs, writes=[R_Pt[k3]])
                    st["lo"], st["hi"] = lo, hi

                def emit_PV(n):
                    st = steps[n]
                    h, s_ = st["h"], st["s"]
                    hb = h % 2
                    k3 = n % 3
                    lo, hi = st["lo"], st["hi"]
                    ob = st["ob"]
                    ptv = Pt[k3][:].rearrange("p (k n) -> p k n", k=2)
                    for kt in range(2):
                        P.op("tensor", lambda e, ob=ob, hb=hb, s_=s_, kt=kt, ptv=ptv, lo=lo, hi=hi, st=st: e.matmul(
                            banks[6 + ob][:, lo:hi], lhsT=Vb[hb][:, 2 * s_ + kt, :], rhs=ptv[:, kt, lo:hi],
                            start=(st["first"] and kt == 0), stop=(st["last"] and kt == 1)),
                            reads=[R_V[hb][vpart(2 * s_ + kt)], R_Pt[k3]], writes=[RB[6 + ob]])
                    if st["last"]:
                        w = st["wtot"]
                        q0 = st["t1"] * BLK + (BLK - 2 if st["t2"] is None else 0)
                        P.op("vector", lambda e, ob=ob, w=w: e.reciprocal(out=rc[ob][:, 0:w], in_=banks[6 + ob][64:128, 0:w]),
                             reads=[RB[6 + ob]], writes=[R_rc[ob]])
                        P.op("vector", lambda e, ob=ob, h=h, q0=q0, w=w: e.tensor_tensor(
                            out=yBT[(h % 2) * 64:(h % 2 + 1) * 64, h // 2, q0:q0 + w],
                            in0=banks[6 + ob][0:64, 0:w], in1=rc[ob][:, 0:w], op=ALU.mult),
                            reads=[RB[6 + ob], R_rc[ob]], writes=[R_yBT])

                uc = -1
                for st in steps:
                    if st["first"]:
                        uc += 1
                    st["ob"] = uc % 2
                load_head(0)
                nsteps = len(steps)
                for n in range(nsteps + 2):
                    if n < nsteps:
                        st = steps[n]
                        if st["first"] and st["t1"] == 0:
                            load_head(st["h"])
                        if st["first"] and st["t1"] == 1:
                            load_head(st["h"] + 1)
                        if st["first"] and st["t1"] == 3:
                            stage_some(2)
                        if st["first"] and st["t1"] == 6:
                            stage_some(2)
                        emit_S(n)
                    if n >= 2:
                        emit_PV(n - 2)
                stage_some(len(stage_q))
                if dbg:
                    ntb = N_OWN_RUN * BLK
                    nhp = (N_HEAD_RUN + 1) // 2
                    d3 = dout("dbg_yBT", [128, nhp, ntb], BF16)
                    P.op("sync", lambda e: e.dma_start(out=d3[:, :, :], in_=yBT[:, 0:nhp, 0:ntb]), reads=[R_yBT], dma="dbg*")
                P.end("phaseC")
        if upto == "C":
            return nc, dbg_out, es

        with contextlib.ExitStack() as pd:
            def sbd(name, shape, dt):
                return pd.enter_context(nc.sbuf_tensor(name, list(shape), dt))
            wa = sbd("wa", [128, 4, D], BF16)
            wb = sbd("wb", [128, 4, D], BF16)
            wg = sbd("wg", [128, KT, 2 * D], BF16)
            wo = sbd("wo", [128, KT, D], BF16)
            bg = sbd("bg", [128, 16], F32)
            l1g = sbd("l1g", [128, 8], F32)
            l1b = sbd("l1b", [128, 8], F32)
            hsc = sbd("hsc", [128, 2], F32)
            xtb = [sbd("xtbD%d" % i, [128, KT, BLK], BF16) for i in range(2)]
            xtf2 = [sbd("xtfD%d" % i, [128, KT, BLK], F32) for i in range(2)]
            sa = [sbd("saD%d" % i, [128, BLK], F32) for i in range(2)]
            sg = [sbd("sgD%d" % i, [128, BLK], F32) for i in range(2)]
            t1 = [sbd("t1D%d" % i, [128, BLK], F32) for i in range(2)]
            t2 = [sbd("t2D%d" % i, [128, BLK], F32) for i in range(2)]
            mT = sbd("mT", [128, KT, BLK], BF16)
            zT2 = [sbd("zT%d" % i, [128, KT, BLK], F32) for i in range(2)]
            zsq = [sbd("zsqD%d" % i, [128, BLK], F32) for i in range(2)]
            mean2 = [sbd("meanD%d" % i, [128, BLK], F32) for i in range(2)]
            msq = sbd("msqD", [128, BLK], F32)
            var = sbd("varD", [128, BLK], F32)
            sdv = sbd("sdvD", [128, BLK], F32)
            rsv2 = [sbd("rsvD%d" % i, [128, BLK], F32) for i in range(2)]
            tn = [sbd("tnD%d" % i, [128, BLK], F32) for i in range(2)]
            tn2 = [sbd("tn2D%d" % i, [128, BLK], F32) for i in range(2)]
            P.begin()
            R_w, R_c = Res(), Res()
            R_wgD, R_wgD2, R_waD, R_wbD = Res(), Res(), Res(), Res()
            R_xtb = [Res(), Res()]
            R_xtf2 = [Res(), Res()]
            R_sa, R_sg, R_t1, R_t2 = [Res(), Res()], [Res(), Res()], [Res(), Res()], [Res(), Res()]
            R_mT = Res()
            R_zT2 = [[Res() for _ in range(8)] for _ in range(2)]
            R_zsq = [Res(), Res()]
            R_msq, R_var, R_sdv = Res(), Res(), Res()
            R_mean2, R_rsv2 = [Res(), Res()], [Res(), Res()]
            R_tn, R_tn2 = [Res(), Res()], [Res(), Res()]
            wg_v = wgS.rearrange("k p c -> p k c")
            wa_v = waS.rearrange("k p c -> p k c")
            wb_v = wbS.rearrange("k p c -> p k c")
            wo_v = woS.rearrange("k p c -> p k c")
            P.op("sync", lambda e: e.dma_start(out=wg[:, :, 0:D], in_=wg_v[:, :, 0:D]), writes=[R_wgD], dma="wD*")
            P.op("gpsimd", lambda e: e.dma_start(out=wa[:], in_=wa_v[:, :, :]), writes=[R_waD], dma="wD*")
            P.op("gpsimd", lambda e: e.dma_start(out=wb[:], in_=wb_v[:, :, :]), writes=[R_wbD], dma="wD*")
            P.op("sync", lambda e: e.dma_start(out=wg[:, :, D:2 * D], in_=wg_v[:, :, D:2 * D]), writes=[R_wgD2], dma="wD*")
            P.op("gpsimd", lambda e: e.dma_start(out=wo[:], in_=wo_v[:, :, :]), writes=[R_w], dma="wD*")
            P.op("sync", lambda e: e.dma_start(out=bg[:], in_=bgate_d[:, :]), writes=[R_c], dma="cD*")
            P.op("sync", lambda e: e.dma_start(out=l1g[:], in_=ln1g_d[:, :]), writes=[R_c], dma="cD*")
            P.op("sync", lambda e: e.dma_start(out=l1b[:], in_=ln1b_d[:, :]), writes=[R_c], dma="cD*")
            P.op("sync", lambda e: e.dma_start(out=hsc[:], in_=hscale_d[:, :]), writes=[R_c], dma="cD*")
            def tile_geom(it):
                xb = (it // 5) * (NTOK // 2 + 2)
                if it in HALOS:
                    return it * BLK + BLK - 2, 2, xb
                return it * BLK, BLK, xb + 2 + (it % 5 - 1) * BLK

            def load_tile_D(it):
                if it >= N_OWN_RUN:
                    return
                a_ = it % 2
                c0_, n_, _ = tile_geom(it)
                P.op("gpsimd", lambda e, a_=a_, c0_=c0_, n_=n_: e.dma_start(out=xtb[a_][:, :, 0:n_], in_=xT_v[:, :, c0_:c0_ + n_]),
                     writes=[R_xtb[a_]], dma="xtbD%d" % a_)
                P.op("sync", lambda e, a_=a_, c0_=c0_, n_=n_: e.dma_start(out=xtf2[a_][:, :, 0:n_], in_=xT_v[:, :, c0_:c0_ + n_]),
                     writes=[R_xtf2[a_]], dma="xtfD%d" % a_)

            def part1(it, prev=None):
                a = it % 2
                c0, n, xc0 = tile_geom(it)
                xtf = xtf2[a]
                R_xtf = R_xtf2[a]
                zTa, R_zTa = zT2[a], R_zT2[a]
                load_tile_D(it + 1)
                for mt in range(8):
                    b2 = mt % 2
                    ms = slice(mt * 128, (mt + 1) * 128)
                    bab = banks[b2]
                    bgg = banks[2 + b2]
                    for kt in range(KT):
                        P.op("tensor", lambda e, kt=kt, mt=mt, a=a, n=n, bgg=bgg: e.matmul(
                            bgg[:, 0:n], lhsT=wg[:, kt, mt * 128:(mt + 1) * 128], rhs=xtb[a][:, kt, 0:n],
                            start=(kt == 0), stop=(kt == KT - 1)), reads=[R_wgD, R_xtb[a]], writes=[RB[2 + b2]])
                    for kt in range(KT):
                        P.op("tensor", lambda e, kt=kt, mt=mt, a=a, n=n, bgg=bgg: e.matmul(
                            bgg[:, BLK:BLK + n], lhsT=wg[:, kt, D + mt * 128:D + (mt + 1) * 128], rhs=xtb[a][:, kt, 0:n],
                            start=(kt == 0), stop=(kt == KT - 1)), reads=[R_wgD2, R_xtb[a]], writes=[RB[2 + b2]])
                    for kt in range(4):
                        P.op("tensor", lambda e, kt=kt, ms=ms, c0=c0, n=n, bab=bab: e.matmul(
                            bab[:, 0:n], lhsT=wa[:, kt, ms], rhs=yAT[:, kt, c0:c0 + n], start=(kt == 0), stop=(kt == 3)),
                            reads=[R_waD, R_yAT], writes=[RB[b2]])
                    for kt in range(4):
                        P.op("tensor", lambda e, kt=kt, ms=ms, c0=c0, n=n, bab=bab: e.matmul(
                            bab[:, BLK:BLK + n], lhsT=wb[:, kt, ms], rhs=yBT[:, kt, c0:c0 + n], start=(kt == 0), stop=(kt == 3)),
                            reads=[R_wbD, R_yBT], writes=[RB[b2]])
                    P.op("scalar", lambda e, b2=b2, mt=mt, n=n, bgg=bgg: e.activation(
                        out=sa[b2][:, 0:n], in_=bgg[:, 0:n], func=AF.Sigmoid, bias=bg[:, mt:mt + 1], scale=1.0),
                        reads=[RB[2 + b2], R_c], writes=[R_sa[b2]])
                    P.op("scalar", lambda e, b2=b2, mt=mt, n=n, bgg=bgg: e.activation(
                        out=sg[b2][:, 0:n], in_=bgg[:, BLK:BLK + n], func=AF.Sigmoid, bias=bg[:, 8 + mt:9 + mt], scale=1.0),
                        reads=[RB[2 + b2], R_c], writes=[R_sg[b2]])
                    P.op("vector", lambda e, b2=b2, n=n, bab=bab: e.tensor_tensor(out=t1[b2][:, 0:n], in0=bab[:, 0:n], in1=sa[b2][:, 0:n], op=ALU.mult),
                         reads=[RB[b2], R_sa[b2]], writes=[R_t1[b2]])
                    P.op("vector", lambda e, b2=b2, n=n, bab=bab: e.tensor_tensor(out=t2[b2][:, 0:n], in0=bab[:, BLK:BLK + n], in1=sg[b2][:, 0:n], op=ALU.mult),
                         reads=[RB[b2], R_sg[b2]], writes=[R_t2[b2]])
                    P.op("gpsimd", lambda e, b2=b2, mt=mt, n=n: e.tensor_tensor(out=mT[:, mt, 0:n], in0=t1[b2][:, 0:n], in1=t2[b2][:, 0:n], op=ALU.add),
                         reads=[R_t1[b2], R_t2[b2]], writes=[R_mT])

                def stats_D(m_):
                    c2 = m_ % 2
                    P.op("tensor", lambda e, m_=m_, n=n: e.matmul(banks[6][:, 0:n], lhsT=ones_f[:, :], rhs=zTa[:, m_, 0:n],
                                                                   start=(m_ == 0), stop=(m_ == 7)), reads=[R_zTa[m_], R_const], writes=[RB[6]])
                    P.op("tensor", lambda e, m_=m_, c2=c2, n=n: e.matmul(banks[7][:, 0:n], lhsT=ones_f[:, :], rhs=zsq[c2][:, 0:n],
                                                                          start=(m_ == 0), stop=(m_ == 7)), reads=[R_zsq[c2], R_const], writes=[RB[7]])
                for mt in range(8):
                    b2 = mt % 2
                    for kt in range(KT):
                        P.op("tensor", lambda e, kt=kt, mt=mt, b2=b2, n=n: e.matmul(
                            banks[4 + b2][:, 0:n], lhsT=wo[:, kt, mt * 128:(mt + 1) * 128], rhs=mT[:, kt, 0:n],
                            start=(kt == 0), stop=(kt == KT - 1)), reads=[R_w, R_mT], writes=[RB[4 + b2]])
                    if mt >= 1:
                        stats_D(mt - 1)
                    P.op("vector", lambda e, mt=mt, b2=b2, n=n, xtf=xtf: e.scalar_tensor_tensor(
                        out=zTa[:, mt, 0:n], in0=xtf[:, mt, 0:n], scalar=ALPHA, in1=banks[4 + b2][:, 0:n], op0=ALU.mult, op1=ALU.add),
                        reads=[R_xtf, RB[4 + b2]], writes=[R_zTa[mt]])
                    P.op("scalar", lambda e, mt=mt, b2=b2, n=n: e.activation(out=zsq[b2][:, 0:n], in_=zTa[:, mt, 0:n], func=AF.Square),
                         reads=[R_zTa[mt]], writes=[R_zsq[b2]])
                    if prev is not None:
                        part2(prev, mts=[mt], store=(mt == 7))
                stats_D(7)
                P.op("vector", lambda e, n=n, a=a: e.tensor_scalar(out=mean2[a][:, 0:n], in0=banks[6][:, 0:n], scalar1=1.0 / D, scalar2=None, op0=ALU.mult),
                     reads=[RB[6]], writes=[R_mean2[a]])
                P.op("vector", lambda e, n=n, a=a: e.tensor_tensor(out=msq[:, 0:n], in0=mean2[a][:, 0:n], in1=mean2[a][:, 0:n], op=ALU.mult),
                     reads=[R_mean2[a]], writes=[R_msq])
                P.op("vector", lambda e, n=n: e.scalar_tensor_tensor(out=var[:, 0:n], in0=banks[7][:, 0:n], scalar=1.0 / D, in1=msq[:, 0:n],
                                                                    op0=ALU.mult, op1=ALU.subtract), reads=[RB[7], R_msq], writes=[R_var])
                P.op("scalar", lambda e, n=n: e.activation(out=sdv[:, 0:n], in_=var[:, 0:n], func=AF.Sqrt, bias=eps_t[:, 0:1], scale=1.0),
                     reads=[R_var, R_const], writes=[R_sdv])
                P.op("vector", lambda e, n=n, a=a: e.reciprocal(out=rsv2[a][:, 0:n], in_=sdv[:, 0:n]), reads=[R_sdv], writes=[R_rsv2[a]])

            def part2(it, mts=range(8), store=True):
                a = it % 2
                c0, n, xc0 = tile_geom(it)
                zTa, R_zTa = zT2[a], R_zT2[a]
                for mt in mts:
                    b2 = mt % 2
                    P.op("vector", lambda e, mt=mt, b2=b2, n=n, a=a: e.tensor_tensor(out=tn[b2][:, 0:n], in0=zTa[:, mt, 0:n], in1=mean2[a][:, 0:n], op=ALU.subtract),
                         reads=[R_zTa[mt], R_mean2[a]], writes=[R_tn[b2]])
                    P.op("vector", lambda e, b2=b2, n=n, a=a: e.tensor_tensor(out=tn2[b2][:, 0:n], in0=tn[b2][:, 0:n], in1=rsv2[a][:, 0:n], op=ALU.mult),
                         reads=[R_tn[b2], R_rsv2[a]], writes=[R_tn2[b2]])
                    P.op("scalar", lambda e, mt=mt, b2=b2, n=n: e.activation(
                        out=zTa[:, mt, 0:n], in_=tn2[b2][:, 0:n], func=AF.Identity, bias=l1b[:, mt:mt + 1], scale=l1g[:, mt:mt + 1]),
                        reads=[R_tn2[b2], R_c], writes=[R_zTa[mt]])
                    if it in HALOS:
                        P.op("gpsimd", lambda e, mt=mt, n=n, xc0=xc0, it=it: e.tensor_scalar(
                            out=x1b[:, mt, xc0:xc0 + n], in0=zTa[:, mt, 0:n], scalar1=hsc[:, it // 5:it // 5 + 1], scalar2=None, op0=ALU.mult),
                            reads=[R_zTa[mt], R_c], writes=[R_x1b])
                    else:
                        P.op("scalar", lambda e, mt=mt, b2=b2, n=n, xc0=xc0: e.activation(
                            out=x1b[:, mt, xc0:xc0 + n], in_=tn2[b2][:, 0:n], func=AF.Identity, bias=l1b[:, mt:mt + 1], scale=l1g[:, mt:mt + 1]),
                            reads=[R_tn2[b2], R_c], writes=[R_x1b])
                if store:
                    P.op("sync", lambda e, n=n, xc0=xc0: e.dma_start(out=x1_scr[:, :, xc0:xc0 + n], in_=zTa[:, :, 0:n]),
                         reads=R_zTa, dma="x1o%d" % a)

            load_tile_D(0)
            for it in range(N_OWN_RUN):
                part1(it, prev=(it - 1 if it >= 1 else None))
            part2(N_OWN_RUN - 1)
            if dbg:
                nx = X1C
                d4 = dout("dbg_x1", [128, KT, nx], BF16)
                P.op("sync", lambda e: e.dma_start(out=d4[:, :, :], in_=x1b[:, :, 0:nx]), reads=[R_x1b], dma="dbg*")
            P.end("phaseD")
    if upto == "D":
        return nc, dbg_out, es

    with contextlib.ExitStack() as pe:
        def sbe(name, shape, dt):
            return pe.enter_context(nc.sbuf_tensor(name, list(shape), dt))
        HT = NTOK // 2
        hgT = sbe("hgT", [128, NCH, HT], BF16)
        wd = sbe("wd", [128, NCH, D], BF16)
        cw = sbe("cw", [128, 44, 3], F32)
        cb = sbe("cb", [128, 44], F32)
        l2g = sbe("l2g", [128, 8], F32)
        l2b = sbe("l2b", [128, 8], F32)
        wuc = [sbe("wuc%d" % i, [128, KT, 256], BF16) for i in range(2)]
        ct1 = [sbe("ct1_%d" % i, [128, 512], F32) for i in range(2)]
        ct2 = [sbe("ct2_%d" % i, [128, 512], F32) for i in range(2)]
        ct3 = [sbe("ct3_%d" % i, [128, 512], F32) for i in range(2)]
        gact = sbe("gact", [128, 512], F32)
        xf = sbe("xfF", [128, KT, 512], F32)
        z2 = sbe("z2F", [128, KT, 512], F32)
        zq = [sbe("zqF%d" % i, [128, 512], F32) for i in range(2)]
        meanF = sbe("meanF", [128, 512], F32)
        msqF = sbe("msqF", [128, 512], F32)
        varF = sbe("varF", [128, 512], F32)
        sdF = sbe("sdF", [128, 512], F32)
        rsF = sbe("rsF", [128, 512], F32)
        tnF = [sbe("tnF%d" % i, [128, 512], F32) for i in range(2)]
        tn2F = [sbe("tn2F%d" % i, [128, 512], F32) for i in range(2)]
        ost = [sbe("ost%d" % i, [128, D], F32) for i in range(2)]
        wup_v = wupS.rearrange("k p c -> p k c")
        wd_v = wdS.rearrange("k p c -> p k c")
        P.begin()
        R_c, R_hg = Res(), Res()
        R_wd2 = [Res(), Res()]
        R_wuc = [[Res(), Res()], [Res(), Res()]]
        R_ct1, R_ct2, R_ct3 = [Res(), Res()], [Res(), Res()], [Res(), Res()]
        R_ga = Res()
        R_xf = Res()
        R_z2m = [Res() for _ in range(8)]
        R_zq = [Res(), Res()]
        R_mean, R_msq, R_var, R_sd, R_rs = Res(), Res(), Res(), Res(), Res()
        R_tn, R_tn2 = [Res(), Res()], [Res(), Res()]
        R_ost = [Res(), Res()]
        P.op("sync", lambda e: e.dma_start(out=cw[:], in_=cw_d[:, :, :]), writes=[R_c], dma="cE*")
        P.op("sync", lambda e: e.dma_start(out=cb[:], in_=cb_d[:, :]), writes=[R_c], dma="cE*")
        P.op("sync", lambda e: e.dma_start(out=l2g[:], in_=ln2g_d[:, :]), writes=[R_c], dma="cE*")
        P.op("sync", lambda e: e.dma_start(out=l2b[:], in_=ln2b_d[:, :]), writes=[R_c], dma="cE*")
        nchunk = 0
        ntile = 0

        def load_chunk(n):
            if n >= N_HALF_RUN * NCH:
                return
            c_ = n % NCH
            wb_ = n % 2
            for part in range(2):
                cc = part * DFF + c_ * 128
                P.op("gpsimd", lambda e, wb_=wb_, part=part, cc=cc: e.dma_start(
                    out=wuc[wb_][:, :, part * 128:(part + 1) * 128], in_=wup_v[:, :, cc:cc + 128]),
                    writes=[R_wuc[wb_][part]], dma="wuc%d%d" % (wb_, part))

        load_chunk(0)
        wd_loaded = [False]
        for hf in range(N_HALF_RUN):
            base = (HT + 2) * hf
            for c in range(NCH):
                wbuf = nchunk % 2
                nchunk += 1
                load_chunk(nchunk)
                if not wd_loaded[0]:
                    wd_loaded[0] = True
                    for q3 in range(2):
                        P.op("gpsimd", lambda e, q3=q3: e.dma_start(out=wd[:, q3 * 11:(q3 + 1) * 11, :], in_=wd_v[:, q3 * 11:(q3 + 1) * 11, :]),
                             writes=[R_wd2[q3]], dma="wdF*")
                for T in range(3):
                    col0 = base + 510 * T
                    ncol = min(512, base + HT + 2 - col0)
                    nout = ncol - 2
                    pb2 = ntile % 2
                    ntile += 1
                    for part in range(2):
                        bk = 2 * pb2 + part
                        for kt in range(KT):
                            P.op("tensor", lambda e, bk=bk, wbuf=wbuf, part=part, kt=kt, col0=col0, ncol=ncol: e.matmul(
                                banks[bk][:, 0:ncol], lhsT=wuc[wbuf][:, kt, part * 128:(part + 1) * 128],
                                rhs=x1b[:, kt, col0:col0 + ncol], start=(kt == 0), stop=(kt == KT - 1)),
                                reads=[R_wuc[wbuf][part], R_x1b], writes=[RB[bk]])
                    for part in range(2):
                        bk = 2 * pb2 + part
                        ci = part * NCH + c
                        P.op("scalar", lambda e, bk=bk, part=part, ci=ci, ncol=ncol, nout=nout: e.activation(
                            out=ct1[part][:, 0:nout], in_=banks[bk][:, 2:ncol], func=AF.Identity,
                            bias=cb[:, ci:ci + 1], scale=cw[:, ci, 2:3]), reads=[RB[bk], R_c], writes=[R_ct1[part]])
                        P.op("vector", lambda e, bk=bk, part=part, ci=ci, ncol=ncol, nout=nout: e.scalar_tensor_tensor(
                            out=ct2[part][:, 0:nout], in0=banks[bk][:, 1:ncol - 1], scalar=cw[:, ci, 1:2], in1=ct1[part][:, 0:nout],
                            op0=ALU.mult, op1=ALU.add), reads=[RB[bk], R_c, R_ct1[part]], writes=[R_ct2[part]])
                        P.op("vector", lambda e, bk=bk, part=part, ci=ci, ncol=ncol, nout=nout: e.scalar_tensor_tensor(
                            out=ct3[part][:, 0:nout], in0=banks[bk][:, 0:ncol - 2], scalar=cw[:, ci, 0:1], in1=ct2[part][:, 0:nout],
                            op0=ALU.mult, op1=ALU.add), reads=[RB[bk], R_c, R_ct2[part]], writes=[R_ct3[part]])
                    P.op("scalar", lambda e, nout=nout: e.activation(out=gact[:, 0:nout], in_=ct3[0][:, 0:nout], func=AF.Gelu),
                         reads=[R_ct3[0]], writes=[R_ga])
                    P.op("gpsimd", lambda e, c=c, T=T, nout=nout: e.tensor_tensor(
                        out=hgT[:, c, 510 * T:510 * T + nout], in0=gact[:, 0:nout], in1=ct3[1][:, 0:nout], op=ALU.mult),
                        reads=[R_ga, R_ct3[1]], writes=[R_hg])
            for T2 in range(2):
                tb0 = HT * hf + 512 * T2
                xcol = base + 2 + 512 * T2
                P.op("sync", lambda e, xcol=xcol: e.dma_start(out=xf[:], in_=x1_scr[:, :, xcol:xcol + 512]),
                     writes=[R_xf], dma="xfF")
                for mt in range(8):
                    b2 = mt % 2
                    for kt in range(NCH):
                        P.op("tensor", lambda e, b2=b2, kt=kt, mt=mt, T2=T2: e.matmul(
                            banks[b2][:, :], lhsT=wd[:, kt, mt * 128:(mt + 1) * 128], rhs=hgT[:, kt, 512 * T2:512 * (T2 + 1)],
                            start=(kt == 0), stop=(kt == NCH - 1)), reads=[R_wd2[kt // 11], R_hg], writes=[RB[b2]])
                    P.op("vector", lambda e, mt=mt, b2=b2: e.scalar_tensor_tensor(
                        out=z2[:, mt, :], in0=xf[:, mt, :], scalar=ALPHA, in1=banks[b2][:, :], op0=ALU.mult, op1=ALU.add),
                        reads=[R_xf, RB[b2]], writes=[R_z2m[mt]])
                    P.op("scalar", lambda e, mt=mt, b2=b2: e.activation(out=zq[b2][:], in_=z2[:, mt, :], func=AF.Square),
                         reads=[R_z2m[mt]], writes=[R_zq[b2]])
                    def stats_F(m_):
                        c2 = m_ % 2
                        P.op("tensor", lambda e, m_=m_: e.matmul(banks[2][:, :], lhsT=ones_f[:, :], rhs=z2[:, m_, :],
                                                                  start=(m_ == 0), stop=(m_ == 7)), reads=[R_z2m[m_], R_const], writes=[RB[2]])
                        P.op("tensor", lambda e, m_=m_, c2=c2: e.matmul(banks[3][:, :], lhsT=ones_f[:, :], rhs=zq[c2][:],
                                                                         start=(m_ == 0), stop=(m_ == 7)), reads=[R_zq[c2], R_const], writes=[RB[3]])
                    if mt >= 1:
                        stats_F(mt - 1)
                    if mt == 7:
                        stats_F(7)
                P.op("vector", lambda e: e.tensor_scalar(out=meanF[:], in0=banks[2][:, :], scalar1=1.0 / D, scalar2=None, op0=ALU.mult),
                     reads=[RB[2]], writes=[R_mean])
                P.op("vector", lambda e: e.tensor_tensor(out=msqF[:], in0=meanF[:], in1=meanF[:], op=ALU.mult), reads=[R_mean], writes=[R_msq])
                P.op("vector", lambda e: e.scalar_tensor_tensor(out=varF[:], in0=banks[3][:, :], scalar=1.0 / D, in1=msqF[:],
                                                               op0=ALU.mult, op1=ALU.subtract), reads=[RB[3], R_msq], writes=[R_var])
                P.op("scalar", lambda e: e.activation(out=sdF[:], in_=varF[:], func=AF.Sqrt, bias=eps_t[:, 0:1], scale=1.0),
                     reads=[R_var, R_const], writes=[R_sd])
                P.op("vector", lambda e: e.reciprocal(out=rsF[:], in_=sdF[:]), reads=[R_sd], writes=[R_rs])
                for mt in range(8):
                    b2 = mt % 2
                    P.op("vector", lambda e, mt=mt, b2=b2: e.tensor_tensor(out=tnF[b2][:], in0=z2[:, mt, :], in1=meanF[:], op=ALU.subtract),
                         reads=[R_z2m[mt], R_mean], writes=[R_tn[b2]])
                    P.op("vector", lambda e, b2=b2: e.tensor_tensor(out=tn2F[b2][:], in0=tnF[b2][:], in1=rsF[:], op=ALU.mult),
                         reads=[R_tn[b2], R_rs], writes=[R_tn2[b2]])
                    P.op("scalar", lambda e, mt=mt, b2=b2: e.activation(
                        out=z2[:, mt, :], in_=tn2F[b2][:], func=AF.Identity, bias=l2b[:, mt:mt + 1], scale=l2g[:, mt:mt + 1]),
                        reads=[R_tn2[b2], R_c], writes=[R_z2m[mt]])
                for tt in range(4):
                    o2 = tt % 2
                    for mt in range(8):
                        bk = 4 + 2 * o2 + mt // 4
                        P.op("tensor", lambda e, bk=bk, mt=mt, tt=tt: e.transpose(
                            out=banks[bk][:, (mt % 4) * 128:(mt % 4 + 1) * 128], in_=z2[:, mt, tt * 128:(tt + 1) * 128],
                            identity=ident_f[:, :]), reads=[R_z2m[mt], R_const], writes=[RB[bk]])
                    for hb2 in range(2):
                        bk = 4 + 2 * o2 + hb2
                        if hb2 == 0:
                            P.op("scalar", lambda e, bk=bk, o2=o2: e.copy(out=ost[o2][:, 0:512], in_=banks[bk][:, :]),
                                 reads=[RB[bk]], writes=[R_ost[o2]])
                        else:
                            P.op("vector", lambda e, bk=bk, o2=o2: e.tensor_copy(out=ost[o2][:, 512:1024], in_=banks[bk][:, :]),
                                 reads=[RB[bk]], writes=[R_ost[o2]])
                    r0 = tb0 + tt * 128
                    P.op("sync", lambda e, o2=o2, r0=r0: e.dma_start(out=out_d[r0:r0 + 128, :], in_=ost[o2][:]),
                         reads=[R_ost[o2]], dma="out%d" % o2)
        P.end("phaseEF")

    return nc, dbg_out, es


def make_in_maps(inp):
    x = np.asarray(inp["x"], np.float32)
    cst = _consts()
    shared = {}
    shared["w_in"] = np.ascontiguousarray(np.asarray(inp["w_in"], np.float32)[0].reshape(KT, 128, 4608))
    shared["b_gate"] = _fm(np.asarray(inp["b_gate"])[0], 16)
    shared["sgu_ln_g"] = np.ascontiguousarray(np.broadcast_to(np.asarray(inp["sgu_ln_g"], np.float32)[0][None], (128, 512)))
    shared["sgu_ln_b"] = np.ascontiguousarray(np.broadcast_to(np.asarray(inp["sgu_ln_b"], np.float32)[0][None], (128, 512)))
    ws = np.asarray(inp["w_spatial"], np.float32)[0]
    shared["wsT"] = np.ascontiguousarray(ws.transpose(2, 0, 1))
    bs = np.asarray(inp["b_spatial"], np.float32)[0]
    shared["bsT"] = np.ascontiguousarray(np.repeat(bs.reshape(4, 2, 1, 128), 64, axis=2).reshape(4, 128, 128).transpose(1, 0, 2))
    shared["w_branch_a"] = np.ascontiguousarray(np.asarray(inp["w_branch_a"], np.float32)[0].reshape(4, 128, D))
    shared["w_branch_b"] = np.ascontiguousarray(np.asarray(inp["w_branch_b"], np.float32)[0].reshape(4, 128, D))
    shared["w_out"] = np.ascontiguousarray(np.asarray(inp["w_out"], np.float32)[0].reshape(KT, 128, D))
    shared["ln1_g"] = _fm(np.asarray(inp["ln1_g"])[0], 8)
    shared["ln1_b"] = _fm(np.asarray(inp["ln1_b"])[0], 8)
    shared["w_up"] = np.ascontiguousarray(np.asarray(inp["w_up"], np.float32)[0].reshape(KT, 128, 2 * DFF))
    cw = np.asarray(inp["conv_w"], np.float32)[0]
    shared["conv_w"] = np.ascontiguousarray(cw.reshape(3, 44, 128).transpose(2, 1, 0))
    shared["conv_b"] = _fm(np.asarray(inp["conv_b"])[0], 44)
    shared["w_down"] = np.ascontiguousarray(np.asarray(inp["w_down"], np.float32)[0].reshape(NCH, 128, D))
    shared["ln2_g"] = _fm(np.asarray(inp["ln2_g"])[0], 8)
    shared["ln2_b"] = _fm(np.asarray(inp["ln2_b"])[0], 8)
    for k in ("perms", "ident", "mtri", "eind", "trilT"):
        shared[k] = cst[k]
    in_maps = []
    zero_blk = np.zeros((BLK, D), np.float32)
    for c in range(8):
        b, j = c // 4, c % 4
        perm = _perm_for(j)
        xp = np.concatenate([x[b, p * BLK:(p + 1) * BLK] if p >= 0 else zero_blk for p in perm], axis=0)
        m = dict(shared)
        m["xT_all"] = np.ascontiguousarray(xp.T).reshape(KT, 128, SLOTC)
        m["rope_all"] = _rope_tables(perm)
        m["vmask"] = _vmask(perm)
        hs = np.ones((128, 2), np.float32)
        if j == 0:
            hs[:, 0] = 0.0
        m["hscale"] = hs
        in_maps.append(m)
    return in_maps


def kernel(**inputs):
    in_maps = make_in_maps(inputs)
    nc, _, es = build("F", False)
    res = run_bass_kernel_spmd(nc, in_maps, core_ids=list(range(8)))
    out = np.zeros((2, SEQ, D), np.float32)
    H = NTOK // 2
    for c in range(8):
        b, j = c // 4, c % 4
        o = res.results[c]["out"]
        out[b, j * H:(j + 1) * H] = o[0:H]
        out[b, (7 - j) * H:(8 - j) * H] = o[H:2 * H]
    return out
```

```python
import contextlib
import numpy as np
import concourse.bass as bass
import concourse.mybir as mybir
from concourse.bass_utils import run_bass_kernel_spmd

F32 = mybir.dt.float32
BF16 = mybir.dt.bfloat16
AF = mybir.ActivationFunctionType
ALU = mybir.AluOpType
AX = mybir.AxisListType

D = 1024
KT = 8
SEQ = 8192
NBSEQ = 32
NB = 33
BLK = 256
SLOTC = NB * BLK
NOWN = 10
TOWN = NOWN * BLK
HALOS = (0, 5)
NOTH_A = 11
GS = 36
KROWS = 64 + NB
NTOK = 2048
DFF = 2816
NCH = 22
ALPHA = 2.0 ** 0.25
EPS = 1e-5
BIG = 32768.0
X1C = 2 * (NTOK // 2 + 2)
NSLOT_RUN = NB
N_OWN_RUN = NOWN
N_HEAD_RUN = 8
N_HALF_RUN = 2
B_PARTS = "uqgs"
Q_STAGE = 9
QM_TILES = None
G_STAGE = 9
B_TILES = None

ENGS = ("tensor", "scalar", "vector", "gpsimd", "sync")


class Res:
    __slots__ = ("w", "r", "name", "excl")

    def __init__(self, name="", excl=False):
        self.w = None
        self.r = {}
        self.name = name
        self.excl = excl


class Op:
    __slots__ = ("eng", "fn", "deps", "inc", "count", "dma", "sem", "phase")

    def __init__(self, eng, fn, dma):
        self.eng = eng
        self.fn = fn
        self.deps = []
        self.inc = False
        self.count = None
        self.dma = dma
        self.sem = None


class Prog:
    def __init__(self, nc, es):
        self.nc = nc
        self.es = es
        self.esem = {e: es.enter_context(nc.semaphore("s_" + e)) for e in ENGS}
        self.ecount = {e: 0 for e in ENGS}
        self.dsem = {}
        self.dcount = {}
        self.ops = None
        self.dma_ops = None
        self.nblock = 0

    def begin(self):
        self.ops = {e: [] for e in ENGS}
        self.dma_ops = []

    def _dsem(self, key):
        if key not in self.dsem:
            self.dsem[key] = self.es.enter_context(self.nc.semaphore("d_" + key))
            self.dcount[key] = 0
        return self.dsem[key]

    def op(self, eng, fn, reads=(), writes=(), dma=None):
        if dma is not None and dma.endswith("*"):
            self.nuniq = getattr(self, "nuniq", 0) + 1
            dma = dma[:-1] + "_u%d" % self.nuniq
        o = Op(eng, fn, dma)
        o.phase = self.nblock
        deps = []
        for r in reads:
            if r.w is not None:
                deps.append(r.w)
            if r.excl:
                for k, rd in r.r.items():
                    if k != eng:
                        deps.append(rd)
        for w in writes:
            if w.w is not None:
                deps.append(w.w)
            for rd in w.r.values():
                deps.append(rd)
        seen = set()
        for d in deps:
            if id(d) in seen or d is o or d.phase != self.nblock:
                continue
            seen.add(id(d))
            if d.dma is None and d.eng == "tensor" and eng == "tensor" and dma is None:
                continue
            o.deps.append(d)
            d.inc = True
        for r in reads:
            r.r[eng if dma is None else ("dma", dma)] = o
        for w in writes:
            w.w = o
            w.r = {}
        if dma is not None:
            o.sem = self._dsem(dma)
            self.dcount[dma] += 16
            o.count = self.dcount[dma]
            self.dma_ops.append(o)
        self.ops[eng].append(o)
        return o

    def end(self, name=None):
        nc = self.nc
        finals = {}
        for o in self.dma_ops:
            finals[o.dma] = (o.sem, max(o.count, finals.get(o.dma, (None, 0))[1]))
        fin = Op("sync", None, None)
        self.ops["sync"].append(fin)
        for e in ENGS:
            for o in self.ops[e]:
                if o.dma is None and o.inc:
                    self.ecount[e] += 1
                    o.count = self.ecount[e]
                    o.sem = self.esem[e]
        self.nblock += 1
        with nc.Block(name or ("blk%d" % self.nblock)) as block:
            for e in ENGS:
                ops = self.ops[e]

                def body(eh, ops=ops):
                    waited = {}
                    for o in ops:
                        if o is fin:
                            for (s, c) in finals.values():
                                eh.wait_ge(s, c)
                            continue
                        for d in o.deps:
                            k = id(d.sem)
                            if waited.get(k, 0) < d.count:
                                eh.wait_ge(d.sem, d.count)
                                waited[k] = d.count
                        inst = o.fn(eh)
                        if o.dma is not None:
                            inst.then_inc(o.sem, 16)
                        elif o.inc:
                            inst.then_inc(o.sem, 1)

                getattr(block, e)(body)
        self.ops = None
        self.dma_ops = None


def _perm_for(j):
    A = [4 * j + i for i in range(4)]
    hA = 4 * j - 1
    B = [28 - 4 * j + i for i in range(4)]
    hB = 27 - 4 * j
    own = [hA] + A + [hB] + B
    ownset = set(b for b in own if b >= 0)
    past = sorted(b for b in range(NBSEQ) if b < hB and b not in ownset)
    left = [b for b in range(NBSEQ) if b not in ownset and b not in past]
    fill = list(left)
    while len(past) + len(fill) < NB - NOWN:
        fill.append(NBSEQ - 1)
    others = (past + fill)[: NB - NOWN]
    assert len(past) <= NB - NOWN
    return own + others


def _consts():
    c = {}
    perm = np.zeros((128, 128), np.float32)
    for m in range(128):
        if (m % 64) < 32:
            perm[m + 32, m] = -1.0
        else:
            perm[m - 32, m] = 1.0
    c["perms"] = perm
    c["ident"] = np.eye(128, dtype=np.float32)
    c["ones"] = np.ones((128, 128), np.float32)
    k = np.arange(128)[:, None]
    q = np.arange(128)[None, :]
    tri = np.where(k <= q, 0.0, -BIG).astype(np.float32)
    m0 = np.concatenate([tri, np.zeros((128, 128), np.float32)], axis=1)
    m1 = np.concatenate([np.full((128, 128), -BIG, np.float32), tri], axis=1)
    c["mtri"] = np.stack([m0, m1], axis=1)
    e = np.zeros((NB, SLOTC), np.float32)
    for j in range(NB):
        e[j, j * BLK:(j + 1) * BLK] = BIG
    c["eind"] = e
    s = np.arange(128)[:, None]
    t = np.arange(128)[None, :]
    c["trilT"] = (s <= t).astype(np.float32)
    return c


def _rope_tables(perm):
    half = 32
    inv_freq = (np.float32(10000.0) ** (-np.arange(half, dtype=np.float32) / np.float32(half))).astype(np.float32)
    pos = np.concatenate([np.arange(b * BLK, (b + 1) * BLK) for b in perm]).astype(np.float32)
    ang = (pos[None, :] * inv_freq[:, None]).astype(np.float32)
    cos = np.cos(ang).astype(np.float32)
    sin = np.sin(ang).astype(np.float32)
    cos = np.tile(cos, (4, 1))
    sin = np.tile(sin, (4, 1))
    return np.stack([cos, sin], axis=1)


def _vmask(perm):
    vm = np.full((NOWN, NB), -1e30, np.float32)
    first = {}
    for s_, b in enumerate(perm):
        if b >= 0 and b not in first:
            first[b] = s_
    for i in range(NOWN):
        a = perm[i]
        for s_ in range(NB):
            b = perm[s_]
            if b >= 0 and b < a and first[b] == s_:
                vm[i, s_] = 0.0
    return np.ascontiguousarray(np.broadcast_to(vm[None], (128, NOWN, NB)))


def _fm(v, n):
    return np.ascontiguousarray(np.asarray(v, np.float32).reshape(n, 128).T)


def build(upto="F", dbg=False):
    nc = bass.Bass("TRN2", target_bir_lowering=False)
    es = contextlib.ExitStack()

    def din(name, shape, dt=F32):
        return nc.dram_tensor(name, list(shape), dt, kind="ExternalInput").ap()

    xT_all = din("xT_all", [KT, 128, SLOTC])
    rope_all = din("rope_all", [128, 2, SLOTC])
    vmask_d = din("vmask", [128, NOWN, NB])
    hscale_d = din("hscale", [128, 2])
    w_in_d = din("w_in", [KT, 128, 4608])
    bgate_d = din("b_gate", [128, 16])
    lng_d = din("sgu_ln_g", [128, 512])
    lnb_d = din("sgu_ln_b", [128, 512])
    wsT_d = din("wsT", [128, 8, 128])
    bsT_d = din("bsT", [128, 4, 128])
    wa_d = din("w_branch_a", [4, 128, D])
    wb_d = din("w_branch_b", [4, 128, D])
    wo_d = din("w_out", [KT, 128, D])
    ln1g_d = din("ln1_g", [128, 8])
    ln1b_d = din("ln1_b", [128, 8])
    wup_d = din("w_up", [KT, 128, 2 * DFF])
    cw_d = din("conv_w", [128, 44, 3])
    cb_d = din("conv_b", [128, 44])
    wd_d = din("w_down", [NCH, 128, D])
    ln2g_d = din("ln2_g", [128, 8])
    ln2b_d = din("ln2_b", [128, 8])
    perms_d = din("perms", [128, 128])
    ident_d = din("ident", [128, 128])
    mtri_d = din("mtri", [128, 2, 256])
    eind_d = din("eind", [NB, SLOTC])
    trilT_d = din("trilT", [128, 128])

    out_d = nc.dram_tensor("out", [NTOK, D], F32, kind="ExternalOutput").ap()

    kT_scr = nc.dram_tensor("kT_scr", [8, 64, SLOTC], BF16, kind="Internal").ap()
    v_scr = nc.dram_tensor("v_scr", [SLOTC, 8, 128], BF16, kind="Internal").ap()
    x1_scr = nc.dram_tensor("x1_scr", [128, KT, X1C], F32, kind="Internal").ap()

    dbg_out = {}

    def dout(name, shape, dt=F32):
        a = nc.dram_tensor(name, list(shape), dt, kind="ExternalOutput").ap()
        dbg_out[name] = a
        return a

    P = Prog(nc, es)

    def sb(name, shape, dt):
        return es.enter_context(nc.sbuf_tensor(name, list(shape), dt))

    ps_all = es.enter_context(nc.psum_tensor("ps_all", [128, 8 * 512], F32))
    banks = [ps_all[:, b * 512:(b + 1) * 512] for b in range(8)]
    RB = [Res("bank%d" % b, excl=True) for b in range(8)]

    ident_bf = sb("ident_bf", [128, 128], BF16)
    ident_f = sb("ident_f", [128, 128], F32)
    ones_f = sb("ones_f", [128, 128], F32)
    perms_bf = sb("perms_bf", [128, 128], BF16)
    kmean = sb("kmean", [128, 4, NB], F32)
    eps_t = sb("eps_t", [128, 1], F32)
    R_kmean = Res("kmean")

    with contextlib.ExitStack() as pa:
        def sba(name, shape, dt):
            return pa.enter_context(nc.sbuf_tensor(name, list(shape), dt))

        wk = sba("wk", [128, KT, 512], BF16)
        wv = sba("wv", [128, KT, 512], BF16)
        xt = [sba("xtA%d" % i, [128, KT, BLK], BF16) for i in range(2)]
        cs = [sba("csA%d" % i, [128, 2, BLK], F32) for i in range(2)]
        kb = [sba("kbA%d" % i, [128, 4, BLK], BF16) for i in range(2)]
        ta = [sba("taA%d" % i, [128, 2, BLK], F32) for i in range(2)]
        tb = [sba("tbA%d" % i, [128, 2, BLK], F32) for i in range(2)]
        kst = [sba("kstA%d" % i, [128, 4, BLK], BF16) for i in range(2)]
        vst = [sba("vstA%d" % i, [128, 2, 8, 128], BF16) for i in range(2)]
        ksum = sba("ksum", [128, 4, NB], F32)

        P.begin()
        R_const = Res("const")
        R_wk, R_wv = Res("wk"), Res("wv")
        R_xt = [Res("xt0"), Res("xt1")]
        R_cs = [Res("cs0"), Res("cs1")]
        R_kraw = [[RB[2 * a + p // 2] for p in range(4)] for a in range(2)]
        R_rot = [RB[4 + p % 2] for p in range(4)]
        R_vps = [RB[6], RB[7]]
        R_kb = [[Res() for p in range(4)] for a in range(2)]
        R_ta, R_tb = [Res(), Res()], [Res(), Res()]
        R_kst = [[Res() for p in range(4)] for a in range(2)]
        R_vst = [Res(), Res()]
        R_ksum = Res("ksum")
        R_scr = Res("scr")

        w_in_v = w_in_d.rearrange("k p c -> p k c")
        xT_v = xT_all.rearrange("k p t -> p k t")
        P.op("gpsimd", lambda e: e.dma_start(out=ident_bf[:], in_=ident_d[:, :]), writes=[R_const], dma="c0*")
        P.op("gpsimd", lambda e: e.dma_start(out=perms_bf[:], in_=perms_d[:, :]), writes=[R_const], dma="c0*")
        P.op("sync", lambda e: e.dma_start(out=ident_f[:], in_=ident_d[:, :]), writes=[R_const], dma="c1*")
        P.op("vector", lambda e: e.memset(ones_f[:], 1.0), writes=[R_const])
        P.op("vector", lambda e: e.memset(eps_t[:], EPS), writes=[R_const])
        P.op("gpsimd", lambda e: e.dma_start(out=wk[:], in_=w_in_v[:, :, 1536:2048]), writes=[R_wk], dma="wk*")
        P.op("gpsimd", lambda e: e.dma_start(out=wv[:], in_=w_in_v[:, :, 2048:2560]), writes=[R_wv], dma="wv*")
        for i in range(2):
            P.op("vector", lambda e, i=i: e.memset(vst[i][:], 1.0), writes=[R_vst[i]])
        P.op("vector", lambda e: e.memset(ksum[:], 0.0), writes=[R_ksum])

        kT_v = kT_scr.rearrange("(q hh) d t -> (hh d) q t", hh=2)
        v_v = v_scr.rearrange("(s p) h e -> p s h e", p=128)

        def load_A(s_):
            if s_ >= NSLOT_RUN:
                return
            a_ = s_ % 2
            c0_ = s_ * BLK
            P.op("gpsimd", lambda e, a_=a_, c0_=c0_: e.dma_start(out=xt[a_][:], in_=xT_v[:, :, c0_:c0_ + BLK]),
                 writes=[R_xt[a_]], dma="xt%d" % a_)
            P.op("sync", lambda e, a_=a_, c0_=c0_: e.dma_start(out=cs[a_][:], in_=rope_all[:, :, c0_:c0_ + BLK]),
                 writes=[R_cs[a_]], dma="cs%d" % a_)

        load_A(0)
        for s in range(NSLOT_RUN):
            a = s % 2
            c0 = s * BLK
            load_A(s + 1)
            for p in range(4):
                kr = banks[2 * a + p // 2][:, (p % 2) * BLK:(p % 2 + 1) * BLK]
                for kt in range(KT):
                    P.op("tensor", lambda e, kr=kr, a=a, p=p, kt=kt: e.matmul(
                        kr, lhsT=wk[:, kt, p * 128:(p + 1) * 128], rhs=xt[a][:, kt, :],
                        start=(kt == 0), stop=(kt == KT - 1)),
                        reads=[R_wk, R_xt[a]], writes=[R_kraw[a][p]])
            for sub in range(2):
                vp = banks[6 + sub]
                for kt in range(KT):
                    P.op("tensor", lambda e, vp=vp, a=a, sub=sub, kt=kt: e.matmul(
                        vp[:, :], lhsT=xt[a][:, kt, sub * 128:(sub + 1) * 128], rhs=wv[:, kt, :],
                        start=(kt == 0), stop=(kt == KT - 1)),
                        reads=[R_wv, R_xt[a]], writes=[R_vps[sub]])
            for q2 in range(2):
                kr2 = banks[2 * a + q2][:, :].rearrange("p (j n) -> p j n", j=2)
                rp2 = banks[4 + q2][:, :].rearrange("p (j n) -> p j n", j=2)
                RK = R_kraw[a][2 * q2]
                RR = RB[4 + q2]
                P.op("scalar", lambda e, kr2=kr2, a=a, q2=q2: e.copy(out=kb[a][:, 2 * q2:2 * q2 + 2, :], in_=kr2),
                     reads=[RK], writes=[R_kb[a][q2]])
                for j in range(2):
                    p = 2 * q2 + j
                    P.op("tensor", lambda e, q2=q2, j=j, a=a, p=p: e.matmul(
                        banks[4 + q2][:, j * BLK:(j + 1) * BLK], lhsT=perms_bf[:, :], rhs=kb[a][:, p, :], start=True, stop=True),
                        reads=[R_kb[a][q2], R_const], writes=[RR])
                P.op("vector", lambda e, kr2=kr2, a=a, q2=q2: e.tensor_tensor(
                    out=ta[q2][:], in0=kr2, in1=cs[a][:, 0:1, :].to_broadcast([128, 2, BLK]), op=ALU.mult),
                    reads=[RK, R_cs[a]], writes=[R_ta[q2]])
                P.op("vector", lambda e, rp2=rp2, a=a, q2=q2: e.tensor_tensor(
                    out=tb[q2][:], in0=rp2, in1=cs[a][:, 1:2, :].to_broadcast([128, 2, BLK]), op=ALU.mult),
                    reads=[RR, R_cs[a]], writes=[R_tb[q2]])
                P.op("vector", lambda e, a=a, q2=q2: e.tensor_tensor(
                    out=kst[a][:, 2 * q2:2 * q2 + 2, :], in0=ta[q2][:], in1=tb[q2][:], op=ALU.add),
                    reads=[R_ta[q2], R_tb[q2]], writes=[R_kst[a][q2]])
                P.op("vector", lambda e, a=a, q2=q2, s=s: e.tensor_reduce(
                    out=ksum[:, 2 * q2:2 * q2 + 2, s], in_=kst[a][:, 2 * q2:2 * q2 + 2, :], axis=AX.X, op=ALU.add),
                    reads=[R_kst[a][q2]], writes=[R_ksum])
            P.op("sync", lambda e, a=a, c0=c0: e.dma_start(out=kT_v[:, :, c0:c0 + BLK], in_=kst[a][:]),
                 reads=R_kst[a][0:2], writes=[], dma="ko%d" % a)
            for sub in range(2):
                P.op("scalar", lambda e, a=a, sub=sub: e.copy(
                    out=vst[a][:, sub, :, 0:64], in_=banks[6 + sub][:, :].rearrange("p (h e) -> p h e", h=8)),
                    reads=[R_vps[sub]], writes=[R_vst[a]])
            P.op("sync", lambda e, a=a, s=s: e.dma_start(out=v_v[:, 2 * s:2 * s + 2, :, :], in_=vst[a][:]),
                 reads=[R_vst[a]], writes=[], dma="vo%d" % a)
        P.op("vector", lambda e: e.tensor_scalar(out=kmean[:], in0=ksum[:], scalar1=1.0 / BLK, scalar2=None,
                                                 op0=ALU.mult),
             reads=[R_ksum], writes=[R_kmean])
        if dbg:
            dk = dout("dbg_kmean", [128, 4, NB])
            P.op("sync", lambda e: e.dma_start(out=dk[:, :, :], in_=kmean[:]), reads=[R_kmean], dma="dbg*")
        P.end("phaseA")

    if dbg and upto == "A":
        nt = NSLOT_RUN * BLK
        dko = dout("dbg_kT", [8, 64, nt], BF16)
        dvo = dout("dbg_v", [nt, 8, 128], BF16)
        P.begin()
        P.op("sync", lambda e: e.dma_start(out=dko[:, :, :], in_=kT_scr[:, :, 0:nt]), dma="dbg*")
        P.op("sync", lambda e: e.dma_start(out=dvo[:, :, :], in_=v_scr[0:nt, :, :]), dma="dbg*")
        P.end("dump")
    if upto == "A":
        return nc, dbg_out, es

    w_in_v = w_in_d.rearrange("k p c -> p k c")
    xT_v = xT_all.rearrange("k p t -> p k t")
    x1b = sb("x1b", [128, KT, X1C], BF16)
    R_x1b = Res("x1b")

    with contextlib.ExitStack() as pbd:
        def sbp(name, shape, dt):
            return pbd.enter_context(nc.sbuf_tensor(name, list(shape), dt))
        yAT = sbp("yAT", [128, 4, TOWN], BF16)
        yBT = sbp("yBT", [128, 4, TOWN], BF16)
        R_yAT, R_yBT = Res("yAT"), Res("yBT")
        with contextlib.ExitStack() as pbc:
            QM = pbc.enter_context(nc.sbuf_tensor("QM", [KROWS, 8, TOWN], BF16))
            R_QM = Res("QM")
            with contextlib.ExitStack() as pb:
                def sbb(name, shape, dt):
                    return pb.enter_context(nc.sbuf_tensor(name, list(shape), dt))
                wu = sbb("wu", [128, KT, 512], BF16)
                wq = sbb("wq", [128, KT, 512], BF16)
                wva = sbb("wva", [128, KT, 512], BF16)
                xt = [sbb("xtB%d" % i, [128, KT, BLK], BF16) for i in range(3)]
                cs = [sbb("csB%d" % i, [128, 2, BLK], F32) for i in range(3)]
                ug = [sbb("ugB%d" % i, [128, 4, BLK], BF16) for i in range(2)]
                qb = [[sbb("qbB%d_%d" % (j, i), [128, 2, BLK], BF16) for i in range(2)] for j in range(2)]
                ta = [[sbb("taB%d_%d" % (j, i), [128, 2, BLK], F32) for i in range(2)] for j in range(2)]
                tb = [sbb("tbB%d" % i, [128, 2, BLK], F32) for i in range(2)]
                qf = [sbb("qfB%d" % i, [128, 2, BLK], F32) for i in range(2)]
                vg = [sbb("vgB%d" % i, [128, 512], F32) for i in range(2)]
                vn = [sbb("vn_%d" % i, [128, 512], BF16) for i in range(2)]
                tt4 = [sbb("tt4_%d" % i, [128, 512], F32) for i in range(2)]
                trilT = sbb("trilT_sb", [128, 128], F32)
                wsT_b = sbb("wsT_b", [128, 8, 128], BF16)
                lng = sbb("lng", [128, 512], F32)
                lnb = sbb("lnb", [128, 512], F32)
                bsT = sbb("bsT_sb", [128, 512], F32)
                vmask = sbb("vmask_sb", [128, NOWN, NB], F32)
                st6 = [sbb("st6_%d" % i, [128, 6], F32) for i in range(2)]
                mv = [sbb("mv_%d" % i, [128, 2], F32) for i in range(2)]
                sd = [sbb("sd_%d" % i, [128, 1], F32) for i in range(2)]
                rstd = [sbb("rstd_%d" % i, [128, 1], F32) for i in range(2)]
                gm = sbb("gm", [128, 16, NB], F32)
                top8 = sbb("top8", [128, 16, 8], F32)
                thr = sbb("thr", [128, 16], F32)
                gsel = sbb("gsel", [128, 16, NB], F32)
                msel = sbb("msel", [128, 16, NB], BF16)

                P.begin()
                R_w = Res("wB")
                R_c = Res("cB")
                R_xt = [Res(), Res(), Res()]
                R_cs = [Res(), Res(), Res()]
                R_ug = [Res(), Res()]
                R_qb = [[Res(), Res()], [Res(), Res()]]
                R_ta = [[Res(), Res()], [Res(), Res()]]
                R_tb, R_qf = [Res() for _ in range(4)], [Res() for _ in range(4)]
                R_vg = [Res(), Res()]
                R_vn0, R_vn1, R_vn, R_tt4 = [Res(), Res()], [Res(), Res()], [Res(), Res()], [Res(), Res()]
                R_ws = Res()
                R_st, R_mv, R_sd, R_rstd = [Res(), Res()], [Res(), Res()], [Res(), Res()], [Res(), Res()]
                R_gm, R_top, R_thr, R_msel, R_gsel = Res(), Res(), Res(), Res(), Res()

                P.op("gpsimd", lambda e: e.dma_start(out=wu[:], in_=w_in_v[:, :, 0:512]), writes=[R_w], dma="wB*")
                P.op("gpsimd", lambda e: e.dma_start(out=wq[:], in_=w_in_v[:, :, 1024:1536]), writes=[R_w], dma="wB*")
                P.op("gpsimd", lambda e: e.dma_start(out=wva[:], in_=w_in_v[:, :, 512:1024]), writes=[R_w], dma="wB*")
                P.op("gpsimd", lambda e: e.dma_start(out=wsT_b[:], in_=wsT_d[:, :, :]), writes=[R_ws], dma="cB*")
                P.op("sync", lambda e: e.dma_start(out=trilT[:], in_=trilT_d[:, :]), writes=[R_c], dma="cB*")
                P.op("sync", lambda e: e.dma_start(out=lng[:], in_=lng_d[:, :]), writes=[R_c], dma="cB*")
                P.op("sync", lambda e: e.dma_start(out=lnb[:], in_=lnb_d[:, :]), writes=[R_c], dma="cB*")
                P.op("sync", lambda e: e.dma_start(out=bsT[:], in_=bsT_d.rearrange("p a t -> p (a t)")), writes=[R_c], dma="cB*")
                P.op("sync", lambda e: e.dma_start(out=vmask[:], in_=vmask_d[:, :, :]), writes=[R_c], dma="cB*")
                if dbg:
                    P.op("gpsimd", lambda e: e.memset(yBT[:], 0.0), writes=[R_yBT])
                    P.op("gpsimd", lambda e: e.memset(yAT[:], 0.0), writes=[R_yAT])
                for g in range(8):
                    P.op("vector", lambda e, g=g: e.tensor_tensor(out=wsT_b[:, g, :], in0=wsT_b[:, g, :], in1=trilT[:], op=ALU.mult),
                         reads=[R_ws, R_c], writes=[R_ws])

                def load_tile_B(it):
                    if it >= N_OWN_RUN:
                        return
                    a_ = it % 3
                    c0_ = it * BLK
                    P.op("gpsimd", lambda e, a_=a_, c0_=c0_: e.dma_start(out=xt[a_][:], in_=xT_v[:, :, c0_:c0_ + BLK]),
                         writes=[R_xt[a_]], dma="xtB%d" % a_)
                    P.op("sync", lambda e, a_=a_, c0_=c0_: e.dma_start(out=cs[a_][:], in_=rope_all[:, :, c0_:c0_ + BLK]),
                         writes=[R_cs[a_]], dma="csB%d" % a_)

                def part1_B(it):
                    a = it % 2
                    x3 = it % 3
                    load_tile_B(it + 1)
                    for mt in range(4):
                        ur = banks[mt // 2][:, (mt % 2) * BLK:(mt % 2 + 1) * BLK]
                        for kt in range(KT):
                            P.op("tensor", lambda e, ur=ur, x3=x3, mt=mt, kt=kt: e.matmul(
                                ur, lhsT=wu[:, kt, mt * 128:(mt + 1) * 128], rhs=xt[x3][:, kt, :],
                                start=(kt == 0), stop=(kt == KT - 1)), reads=[R_w, R_xt[x3]], writes=[RB[mt // 2]])
                    for p in range(4):
                        qr = banks[2 + p // 2][:, (p % 2) * BLK:(p % 2 + 1) * BLK]
                        for kt in range(KT):
                            P.op("tensor", lambda e, qr=qr, x3=x3, p=p, kt=kt: e.matmul(
                                qr, lhsT=wq[:, kt, p * 128:(p + 1) * 128], rhs=xt[x3][:, kt, :],
                                start=(kt == 0), stop=(kt == KT - 1)), reads=[R_w, R_xt[x3]], writes=[RB[2 + p // 2]])
                    for q2 in range(2):
                        ur2 = banks[q2][:, :].rearrange("p (j n) -> p j n", j=2)
                        P.op("scalar", lambda e, ur2=ur2, a=a, q2=q2: e.activation(out=ug[a][:, 2 * q2:2 * q2 + 2, :], in_=ur2, func=AF.Gelu),
                             reads=[RB[q2]], writes=[R_ug[a]])
                    for q2 in range(2):
                        qr2 = banks[2 + q2][:, :].rearrange("p (j n) -> p j n", j=2)
                        P.op("scalar", lambda e, qr2=qr2, a=a, q2=q2: e.copy(out=qb[a][q2][:], in_=qr2), reads=[RB[2 + q2]], writes=[R_qb[a][q2]])
                    for q2 in range(2):
                        qr2 = banks[2 + q2][:, :].rearrange("p (j n) -> p j n", j=2)
                        P.op("vector", lambda e, qr2=qr2, a=a, q2=q2, x3=x3: e.tensor_tensor(
                            out=ta[a][q2][:], in0=qr2, in1=cs[x3][:, 0:1, :].to_broadcast([128, 2, BLK]), op=ALU.mult),
                            reads=[RB[2 + q2], R_cs[x3]], writes=[R_ta[a][q2]])

                def part2_B(it):
                    a = it % 2
                    x3 = it % 3
                    c0 = it * BLK
                    subs = [1] if it in HALOS else [0, 1]
                    for p in range(4):
                        rp = banks[4 + p // 2][:, (p % 2) * BLK:(p % 2 + 1) * BLK]
                        P.op("tensor", lambda e, rp=rp, p=p, a=a: e.matmul(rp, lhsT=perms_bf[:, :], rhs=qb[a][p // 2][:, p % 2, :], start=True, stop=True),
                             reads=[R_qb[a][p // 2]], writes=[RB[4 + p // 2]])
                    for sub in subs:
                        for kt in range(KT):
                            P.op("tensor", lambda e, x3=x3, sub=sub, kt=kt: e.matmul(
                                banks[6 + sub][:, :], lhsT=xt[x3][:, kt, sub * 128:(sub + 1) * 128], rhs=wva[:, kt, :],
                                start=(kt == 0), stop=(kt == KT - 1)), reads=[R_w, R_xt[x3]], writes=[RB[6 + sub]])
                    for q2 in range(2):
                        rp2 = banks[4 + q2][:, :].rearrange("p (j n) -> p j n", j=2)
                        P.op("vector", lambda e, rp2=rp2, x3=x3, q2=q2: e.tensor_tensor(
                            out=tb[q2][:], in0=rp2, in1=cs[x3][:, 1:2, :].to_broadcast([128, 2, BLK]), op=ALU.mult),
                            reads=[RB[4 + q2], R_cs[x3]], writes=[R_tb[q2]])
                        P.op("vector", lambda e, q2=q2, a=a: e.tensor_tensor(out=qf[q2][:], in0=ta[a][q2][:], in1=tb[q2][:], op=ALU.add),
                             reads=[R_ta[a][q2], R_tb[q2]], writes=[R_qf[q2]])
                        for hh in range(2):
                            h0 = 4 * q2 + hh
                            P.op("vector", lambda e, q2=q2, hh=hh, h0=h0, c0=c0, a=a: e.tensor_tensor(
                                out=QM[0:64, h0:h0 + 3:2, c0:c0 + BLK], in0=ta[a][q2][hh * 64:(hh + 1) * 64, :, :],
                                in1=tb[q2][hh * 64:(hh + 1) * 64, :, :], op=ALU.add),
                                reads=[R_ta[a][q2], R_tb[q2]], writes=[R_QM])
                    for sub in subs:
                        P.op("scalar", lambda e, sub=sub: e.activation(out=vg[sub][:], in_=banks[6 + sub][:, :], func=AF.Gelu),
                             reads=[RB[6 + sub]], writes=[R_vg[sub]])
                    for sub in subs:
                        P.op("vector", lambda e, sub=sub: e.bn_stats(out=st6[sub][:], in_=vg[sub][:]), reads=[R_vg[sub]], writes=[R_st[sub]])
                        P.op("vector", lambda e, sub=sub: e.bn_aggr(out=mv[sub][:], in_=st6[sub][:]), reads=[R_st[sub]], writes=[R_mv[sub]])
                        P.op("scalar", lambda e, sub=sub: e.activation(out=sd[sub][:], in_=mv[sub][:, 1:2], func=AF.Sqrt, bias=eps_t[:, 0:1], scale=1.0),
                             reads=[R_mv[sub], R_const], writes=[R_sd[sub]])
                        P.op("vector", lambda e, sub=sub: e.reciprocal(out=rstd[sub][:], in_=sd[sub][:]), reads=[R_sd[sub]], writes=[R_rstd[sub]])
                        P.op("vector", lambda e, sub=sub: e.tensor_scalar(
                            out=vg[sub][:], in0=vg[sub][:], scalar1=mv[sub][:, 0:1], scalar2=rstd[sub][:, 0:1],
                            op0=ALU.subtract, op1=ALU.mult), reads=[R_vg[sub], R_mv[sub], R_rstd[sub]], writes=[R_vg[sub]])
                        P.op("gpsimd", lambda e, sub=sub: e.tensor_tensor(out=vg[sub][:], in0=vg[sub][:], in1=lng[:], op=ALU.mult),
                             reads=[R_vg[sub], R_c], writes=[R_vg[sub]])
                        P.op("gpsimd", lambda e, sub=sub: e.tensor_tensor(out=vn[sub][:], in0=vg[sub][:], in1=lnb[:], op=ALU.add),
                             reads=[R_vg[sub], R_c], writes=[R_vn[sub]])
                    for p in range(4):
                        for hh in range(2):
                            for sub in subs:
                                jj = p * 2 + sub
                                P.op("tensor", lambda e, p=p, hh=hh, sub=sub, jj=jj: e.matmul(
                                    banks[7 - hh][:, jj * GS:jj * GS + NB],
                                    lhsT=qf[p // 2][hh * 64:(hh + 1) * 64, p % 2, sub * 128:(sub + 1) * 128],
                                    rhs=kmean[hh * 64:(hh + 1) * 64, p, :], start=True, stop=True),
                                    reads=[R_qf[p // 2], R_kmean], writes=[RB[7 - hh]])
                    for sub in subs:
                        for gp in range(4):
                            for hh in range(2):
                                g = 2 * gp + hh
                                P.op("tensor", lambda e, gp=gp, hh=hh, g=g, sub=sub: e.matmul(
                                    banks[4 + sub][hh * 64:(hh + 1) * 64, gp * 128:(gp + 1) * 128],
                                    lhsT=vn[sub][:, g * 64:(g + 1) * 64], rhs=wsT_b[:, g, :], start=True, stop=True),
                                    reads=[R_vn[sub], R_ws], writes=[RB[4 + sub]])
                    if it in HALOS:
                        for hh in range(2):
                            P.op("vector", lambda e, hh=hh: e.memset(gm[:, hh * 8:(hh + 1) * 8, :], 0.0), writes=[R_gm])
                    for hh in range(2):
                        if it in HALOS:
                            for p in range(4):
                                jj = p * 2 + 1
                                P.op("vector", lambda e, hh=hh, jj=jj, it=it: e.tensor_tensor(
                                    out=gm[:, hh * 8 + jj, :], in0=banks[7 - hh][:, jj * GS:jj * GS + NB],
                                    in1=vmask[:, it, :], op=ALU.add), reads=[RB[7 - hh], R_c], writes=[R_gm])
                        else:
                            P.op("vector", lambda e, hh=hh, it=it: e.tensor_tensor(
                                out=gm[:, hh * 8:(hh + 1) * 8, :],
                                in0=banks[7 - hh][:, 0:8 * GS].rearrange("p (j n) -> p j n", j=8)[:, :, 0:NB],
                                in1=vmask[:, it:it + 1, :].to_broadcast([128, 8, NB]), op=ALU.add),
                                reads=[RB[7 - hh], R_c], writes=[R_gm])
                    for j in range(16):
                        P.op("vector", lambda e, j=j: e.max(out=top8[:, j, :], in_=gm[:, j, :]), reads=[R_gm], writes=[R_top])
                    P.op("vector", lambda e: e.tensor_scalar(out=thr[:], in0=top8[:, :, 2], scalar1=-1e29, scalar2=None, op0=ALU.max),
                         reads=[R_top], writes=[R_thr])
                    P.op("vector", lambda e: e.tensor_tensor(
                        out=gsel[:], in0=gm[:], in1=thr[:, :].unsqueeze(2).to_broadcast([128, 16, NB]), op=ALU.is_ge),
                        reads=[R_gm, R_thr], writes=[R_gsel])
                    P.op("vector", lambda e: e.tensor_scalar(out=msel[:], in0=gsel[:], scalar1=-1.0, scalar2=None, op0=ALU.add),
                         reads=[R_gsel], writes=[R_msel])
                    for sub in subs:
                        P.op("vector", lambda e, sub=sub: e.tensor_tensor(out=tt4[sub][:], in0=banks[4 + sub][:, :], in1=bsT[:], op=ALU.add),
                             reads=[RB[4 + sub], R_c], writes=[R_tt4[sub]])
                        P.op("gpsimd", lambda e, a=a, sub=sub, c0=c0: e.tensor_tensor(
                            out=yAT[:, :, c0 + sub * 128:c0 + (sub + 1) * 128],
                            in0=tt4[sub][:].rearrange("p (g t) -> p g t", g=4),
                            in1=ug[a][:, :, sub * 128:(sub + 1) * 128], op=ALU.mult),
                            reads=[R_tt4[sub], R_ug[a]], writes=[R_yAT])
                    for hh in range(2):
                        tpv = banks[7 - hh][0:NB, :].bitcast(BF16)
                        for jj in range(8):
                            P.op("tensor", lambda e, hh=hh, jj=jj, tpv=tpv: e.transpose(
                                out=tpv[:, jj * 128:(jj + 1) * 128], in_=msel[:, hh * 8 + jj, :], identity=ident_bf[:, :]),
                                reads=[R_msel, R_const], writes=[RB[7 - hh]])
                        for p in range(4):
                            P.op("scalar", lambda e, hh=hh, p=p, c0=c0, tpv=tpv: e.copy(
                                out=QM[64:KROWS, 2 * p + hh, c0:c0 + BLK], in_=tpv[:, p * 256:(p + 1) * 256]),
                                reads=[RB[7 - hh]], writes=[R_QM])

                load_tile_B(0)
                part1_B(0)
                for it in range(N_OWN_RUN):
                    if it + 1 < N_OWN_RUN:
                        part1_B(it + 1)
                    part2_B(it)
                if dbg:
                    ntb = N_OWN_RUN * BLK
                    d1 = dout("dbg_QM", [KROWS, 8, ntb], BF16)
                    d2 = dout("dbg_yAT", [128, 4, ntb], BF16)
                    P.op("sync", lambda e: e.dma_start(out=d1[:, :, :], in_=QM[:, :, 0:ntb]), reads=[R_QM], dma="dbg*")
                    P.op("sync", lambda e: e.dma_start(out=d2[:, :, :], in_=yAT[:, :, 0:ntb]), reads=[R_yAT], dma="dbg*")
                P.end("phaseB")
            if upto == "B":
                return nc, dbg_out, es

            with contextlib.ExitStack() as pc:
                def sbc(name, shape, dt):
                    return pc.enter_context(nc.sbuf_tensor(name, list(shape), dt))
                KE = [sbc("KE%d" % i, [KROWS, SLOTC], BF16) for i in range(2)]
                Vb = [sbc("Vb%d" % i, [128, 2 * NB, 128], BF16) for i in range(2)]
                Pt = [sbc("Pt%d" % i, [128, 1024], BF16) for i in range(3)]
                mtri = sbc("mtri_sb", [128, 2, 256], BF16)
                rc = [sbc("rc%d" % i, [64, 2 * BLK], F32) for i in range(2)]
                P.begin()
                R_KE = [Res(), Res()]
                R_V = [[Res() for _ in range(4)] for _ in range(2)]
                R_Pt = [Res(), Res(), Res()]
                R_m = Res()
                R_rc = [Res(), Res()]
                P.op("gpsimd", lambda e: e.dma_start(out=mtri[:], in_=mtri_d[:, :, :]), writes=[R_m], dma="cC*")
                for i in range(2):
                    P.op("gpsimd", lambda e, i=i: e.dma_start(out=KE[i][64:KROWS, :], in_=eind_d[:, :]), writes=[R_KE[i]], dma="cC*")
                v_hv = v_scr.rearrange("(s p) h e -> p s h e", p=128)
                steps = []
                units = [(0, None), (1, 2), (3, 4), (5, None), (6, 7), (8, 9)]
                for h in range(N_HEAD_RUN):
                    for (t1, t2) in units:
                        if t1 >= N_OWN_RUN:
                            continue
                        oth = list(range(NOWN, min(NOWN + NOTH_A, NSLOT_RUN))) if t1 < 5 else list(range(NOWN, NSLOT_RUN))
                        ust = []
                        if t2 is None:
                            q0 = t1 * BLK + BLK - 2
                            for s_ in list(range(t1)) + oth:
                                ust.append((s_, [(False, q0, 2, 0)], 2))
                            ust.append((t1, [(True, q0, 2, 0)], 2))
                        else:
                            q0 = t1 * BLK
                            for s_ in list(range(t1)) + oth:
                                ust.append((s_, [(False, q0, 2 * BLK, 0)], 2 * BLK))
                            ust.append((t1, [(True, q0, BLK, 0), (False, q0 + BLK, BLK, BLK)], 2 * BLK))
                            ust.append((t2, [(True, q0 + BLK, BLK, BLK)], 2 * BLK))
                        for k, (s_, segs, wtot) in enumerate(ust):
                            steps.append(dict(h=h, t1=t1, t2=t2, s=s_, segs=segs, wtot=wtot,
                                              first=(k == 0), last=(k == len(ust) - 1)))
                loaded = set()
                VSPL = [0, 17, 34, 50, 2 * NB]

                def vpart(k):
                    return max(q for q in range(4) if VSPL[q] <= k)

                def load_head(h):
                    if h in loaded or h >= N_HEAD_RUN:
                        return
                    loaded.add(h)
                    hb = h % 2
                    P.op("sync", lambda e, h=h, hb=hb: e.dma_start(out=KE[hb][0:64, :], in_=kT_scr[h, :, :]),
                         writes=[R_KE[hb]], dma="ke%d" % hb)
                    for q4 in range(4):
                        k0, k1 = VSPL[q4], VSPL[q4 + 1]
                        P.op("sync", lambda e, h=h, hb=hb, k0=k0, k1=k1: e.dma_start(
                            out=Vb[hb][:, k0:k1, :], in_=v_hv[:, k0:k1, h, :]),
                            writes=[R_V[hb][q4]], dma="vb%d_%d" % (hb, q4))

                def s_region(n):
                    k3 = n % 3
                    return ps_all[:, (2 * k3) * 512:(2 * k3 + 2) * 512].rearrange("p (k n) -> p k n", k=2), [RB[2 * k3], RB[2 * k3 + 1]]

                def emit_S(n):
                    st = steps[n]
                    h, s_ = st["h"], st["s"]
                    hb = h % 2
                    k3 = n % 3
                    sreg, rbs = s_region(n)
                    for kt in range(2):
                        kc = s_ * BLK + kt * 128
                        for (own, qc, w, off) in st["segs"]:
                            ov = sreg[:, kt, off:off + w]
                            if not own:
                                P.op("tensor", lambda e, ov=ov, hb=hb, kc=kc, h=h, qc=qc, w=w: e.matmul(
                                    ov, lhsT=KE[hb][0:KROWS, kc:kc + 128], rhs=QM[0:KROWS, h, qc:qc + w], start=True, stop=True),
                                    reads=[R_KE[hb], R_QM], writes=[rbs[kt]])
                            else:
                                mo = BLK - w
                                P.op("tensor", lambda e, ov=ov, hb=hb, kc=kc, h=h, qc=qc, w=w: e.matmul(
                                    ov, lhsT=KE[hb][0:64, kc:kc + 128], rhs=QM[0:64, h, qc:qc + w], start=True, stop=False),
                                    reads=[R_KE[hb], R_QM], writes=[rbs[kt]])
                                P.op("tensor", lambda e, ov=ov, kt=kt, mo=mo: e.matmul(
                                    ov, lhsT=ident_bf[:, :], rhs=mtri[:, kt, mo:BLK], start=False, stop=True),
                                    reads=[R_m, R_const], writes=[rbs[kt]])
                    lo = min(sg_[3] for sg_ in st["segs"])
                    hi = max(sg_[3] + sg_[2] for sg_ in st["segs"])
                    ptv = Pt[k3][:].rearrange("p (k n) -> p k n", k=2)
                    P.op("scalar", lambda e, sreg=sreg, ptv=ptv, lo=lo, hi=hi: e.activation(
                        out=ptv[:, :, lo:hi], in_=sreg[:, :, lo:hi], func=AF.Exp, scale=0.125),
                        reads=rbs, writes=[R_Pt[k3]])
                    st["lo"], st["hi"] = lo, hi

                def emit_PV(n):
                    st = steps[n]
                    h, s_ = st["h"], st["s"]
                    hb = h % 2
                    k3 = n % 3
                    lo, hi = st["lo"], st["hi"]
                    ob = st["ob"]
                    ptv = Pt[k3][:].rearrange("p (k n) -> p k n", k=2)
                    for kt in range(2):
                        P.op("tensor", lambda e, ob=ob, hb=hb, s_=s_, kt=kt, ptv=ptv, lo=lo, hi=hi, st=st: e.matmul(
                            banks[6 + ob][:, lo:hi], lhsT=Vb[hb][:, 2 * s_ + kt, :], rhs=ptv[:, kt, lo:hi],
                            start=(st["first"] and kt == 0), stop=(st["last"] and kt == 1)),
                            reads=[R_V[hb][vpart(2 * s_ + kt)], R_Pt[k3]], writes=[RB[6 + ob]])
                    if st["last"]:
                        w = st["wtot"]
                        q0 = st["t1"] * BLK + (BLK - 2 if st["t2"] is None else 0)
                        P.op("vector", lambda e, ob=ob, w=w: e.reciprocal(out=rc[ob][:, 0:w], in_=banks[6 + ob][64:128, 0:w]),
                             reads=[RB[6 + ob]], writes=[R_rc[ob]])
                        P.op("vector", lambda e, ob=ob, h=h, q0=q0, w=w: e.tensor_tensor(
                            out=yBT[(h % 2) * 64:(h % 2 + 1) * 64, h // 2, q0:q0 + w],
                            in0=banks[6 + ob][0:64, 0:w], in1=rc[ob][:, 0:w], op=ALU.mult),
                            reads=[RB[6 + ob], R_rc[ob]], writes=[R_yBT])

                uc = -1
                for st in steps:
                    if st["first"]:
                        uc += 1
                    st["ob"] = uc % 2
                load_head(0)
                nsteps = len(steps)
                for n in range(nsteps + 2):
                    if n < nsteps:
                        st = steps[n]
                        if st["first"] and st["t1"] == 0:
                            load_head(st["h"])
                        if st["first"] and st["t1"] == 1:
                            load_head(st["h"] + 1)
                        emit_S(n)
                    if n >= 2:
                        emit_PV(n - 2)
                if dbg:
                    ntb = N_OWN_RUN * BLK
                    nhp = (N_HEAD_RUN + 1) // 2
                    d3 = dout("dbg_yBT", [128, nhp, ntb], BF16)
                    P.op("sync", lambda e: e.dma_start(out=d3[:, :, :], in_=yBT[:, 0:nhp, 0:ntb]), reads=[R_yBT], dma="dbg*")
                P.end("phaseC")
        if upto == "C":
            return nc, dbg_out, es

        with contextlib.ExitStack() as pd:
            def sbd(name, shape, dt):
                return pd.enter_context(nc.sbuf_tensor(name, list(shape), dt))
            wa = sbd("wa", [128, 4, D], BF16)
            wb = sbd("wb", [128, 4, D], BF16)
            wg = sbd("wg", [128, KT, 2 * D], BF16)
            wo = sbd("wo", [128, KT, D], BF16)
            bg = sbd("bg", [128, 16], F32)
            l1g = sbd("l1g", [128, 8], F32)
            l1b = sbd("l1b", [128, 8], F32)
            hsc = sbd("hsc", [128, 2], F32)
            xtb = [sbd("xtbD%d" % i, [128, KT, BLK], BF16) for i in range(2)]
            xtf2 = [sbd("xtfD%d" % i, [128, KT, BLK], F32) for i in range(2)]
            sa = [sbd("saD%d" % i, [128, BLK], F32) for i in range(2)]
            sg = [sbd("sgD%d" % i, [128, BLK], F32) for i in range(2)]
            t1 = [sbd("t1D%d" % i, [128, BLK], F32) for i in range(2)]
            t2 = [sbd("t2D%d" % i, [128, BLK], F32) for i in range(2)]
            mT = sbd("mT", [128, KT, BLK], BF16)
            zT2 = [sbd("zT%d" % i, [128, KT, BLK], F32) for i in range(2)]
            zsq = [sbd("zsqD%d" % i, [128, BLK], F32) for i in range(2)]
            mean2 = [sbd("meanD%d" % i, [128, BLK], F32) for i in range(2)]
            msq = sbd("msqD", [128, BLK], F32)
            var = sbd("varD", [128, BLK], F32)
            sdv = sbd("sdvD", [128, BLK], F32)
            rsv2 = [sbd("rsvD%d" % i, [128, BLK], F32) for i in range(2)]
            tn = [sbd("tnD%d" % i, [128, BLK], F32) for i in range(2)]
            tn2 = [sbd("tn2D%d" % i, [128, BLK], F32) for i in range(2)]
            P.begin()
            R_w, R_c = Res(), Res()
            R_wgD, R_wgD2, R_waD, R_wbD = Res(), Res(), Res(), Res()
            R_xtb = [Res(), Res()]
            R_xtf2 = [Res(), Res()]
            R_sa, R_sg, R_t1, R_t2 = [Res(), Res()], [Res(), Res()], [Res(), Res()], [Res(), Res()]
            R_mT = Res()
            R_zT2 = [[Res() for _ in range(8)] for _ in range(2)]
            R_zsq = [Res(), Res()]
            R_msq, R_var, R_sdv = Res(), Res(), Res()
            R_mean2, R_rsv2 = [Res(), Res()], [Res(), Res()]
            R_tn, R_tn2 = [Res(), Res()], [Res(), Res()]
            wa_v = wa_d.rearrange("k p c -> p k c")
            wb_v = wb_d.rearrange("k p c -> p k c")
            wo_v = wo_d.rearrange("k p c -> p k c")
            def load_weights_D():
                P.op("gpsimd", lambda e: e.dma_start(out=wg[:, :, 0:D], in_=w_in_v[:, :, 2560:3584]), writes=[R_wgD], dma="wD*")
                P.op("gpsimd", lambda e: e.dma_start(out=wg[:, :, D:2 * D], in_=w_in_v[:, :, 3584:4608]), writes=[R_wgD2], dma="wD*")
                P.op("gpsimd", lambda e: e.dma_start(out=wa[:], in_=wa_v[:, :, :]), writes=[R_waD], dma="wD*")
                P.op("gpsimd", lambda e: e.dma_start(out=wb[:], in_=wb_v[:, :, :]), writes=[R_wbD], dma="wD*")
                P.op("gpsimd", lambda e: e.dma_start(out=wo[:], in_=wo_v[:, :, :]), writes=[R_w], dma="wD*")
            P.op("sync", lambda e: e.dma_start(out=bg[:], in_=bgate_d[:, :]), writes=[R_c], dma="cD*")
            P.op("sync", lambda e: e.dma_start(out=l1g[:], in_=ln1g_d[:, :]), writes=[R_c], dma="cD*")
            P.op("sync", lambda e: e.dma_start(out=l1b[:], in_=ln1b_d[:, :]), writes=[R_c], dma="cD*")
            P.op("sync", lambda e: e.dma_start(out=hsc[:], in_=hscale_d[:, :]), writes=[R_c], dma="cD*")
            def tile_geom(it):
                xb = (it // 5) * (NTOK // 2 + 2)
                if it in HALOS:
                    return it * BLK + BLK - 2, 2, xb
                return it * BLK, BLK, xb + 2 + (it % 5 - 1) * BLK

            def load_tile_D(it):
                if it >= N_OWN_RUN:
                    return
                a_ = it % 2
                c0_, n_, _ = tile_geom(it)
                P.op("gpsimd", lambda e, a_=a_, c0_=c0_, n_=n_: e.dma_start(out=xtb[a_][:, :, 0:n_], in_=xT_v[:, :, c0_:c0_ + n_]),
                     writes=[R_xtb[a_]], dma="xtbD%d" % a_)
                P.op("sync", lambda e, a_=a_, c0_=c0_, n_=n_: e.dma_start(out=xtf2[a_][:, :, 0:n_], in_=xT_v[:, :, c0_:c0_ + n_]),
                     writes=[R_xtf2[a_]], dma="xtfD%d" % a_)

            def part1(it, prev=None):
                a = it % 2
                c0, n, xc0 = tile_geom(it)
                xtf = xtf2[a]
                R_xtf = R_xtf2[a]
                zTa, R_zTa = zT2[a], R_zT2[a]
                load_tile_D(it + 1)
                for mt in range(8):
                    b2 = mt % 2
                    ms = slice(mt * 128, (mt + 1) * 128)
                    bab = banks[b2]
                    bgg = banks[2 + b2]
                    for kt in range(KT):
                        P.op("tensor", lambda e, kt=kt, mt=mt, a=a, n=n, bgg=bgg: e.matmul(
                            bgg[:, 0:n], lhsT=wg[:, kt, mt * 128:(mt + 1) * 128], rhs=xtb[a][:, kt, 0:n],
                            start=(kt == 0), stop=(kt == KT - 1)), reads=[R_wgD, R_xtb[a]], writes=[RB[2 + b2]])
                    for kt in range(KT):
                        P.op("tensor", lambda e, kt=kt, mt=mt, a=a, n=n, bgg=bgg: e.matmul(
                            bgg[:, BLK:BLK + n], lhsT=wg[:, kt, D + mt * 128:D + (mt + 1) * 128], rhs=xtb[a][:, kt, 0:n],
                            start=(kt == 0), stop=(kt == KT - 1)), reads=[R_wgD2, R_xtb[a]], writes=[RB[2 + b2]])
                    for kt in range(4):
                        P.op("tensor", lambda e, kt=kt, ms=ms, c0=c0, n=n, bab=bab: e.matmul(
                            bab[:, 0:n], lhsT=wa[:, kt, ms], rhs=yAT[:, kt, c0:c0 + n], start=(kt == 0), stop=(kt == 3)),
                            reads=[R_waD, R_yAT], writes=[RB[b2]])
                    for kt in range(4):
                        P.op("tensor", lambda e, kt=kt, ms=ms, c0=c0, n=n, bab=bab: e.matmul(
                            bab[:, BLK:BLK + n], lhsT=wb[:, kt, ms], rhs=yBT[:, kt, c0:c0 + n], start=(kt == 0), stop=(kt == 3)),
                            reads=[R_wbD, R_yBT], writes=[RB[b2]])
                    P.op("scalar", lambda e, b2=b2, mt=mt, n=n, bgg=bgg: e.activation(
                        out=sa[b2][:, 0:n], in_=bgg[:, 0:n], func=AF.Sigmoid, bias=bg[:, mt:mt + 1], scale=1.0),
                        reads=[RB[2 + b2], R_c], writes=[R_sa[b2]])
                    P.op("scalar", lambda e, b2=b2, mt=mt, n=n, bgg=bgg: e.activation(
                        out=sg[b2][:, 0:n], in_=bgg[:, BLK:BLK + n], func=AF.Sigmoid, bias=bg[:, 8 + mt:9 + mt], scale=1.0),
                        reads=[RB[2 + b2], R_c], writes=[R_sg[b2]])
                    P.op("vector", lambda e, b2=b2, n=n, bab=bab: e.tensor_tensor(out=t1[b2][:, 0:n], in0=bab[:, 0:n], in1=sa[b2][:, 0:n], op=ALU.mult),
                         reads=[RB[b2], R_sa[b2]], writes=[R_t1[b2]])
                    P.op("vector", lambda e, b2=b2, n=n, bab=bab: e.tensor_tensor(out=t2[b2][:, 0:n], in0=bab[:, BLK:BLK + n], in1=sg[b2][:, 0:n], op=ALU.mult),
                         reads=[RB[b2], R_sg[b2]], writes=[R_t2[b2]])
                    P.op("gpsimd", lambda e, b2=b2, mt=mt, n=n: e.tensor_tensor(out=mT[:, mt, 0:n], in0=t1[b2][:, 0:n], in1=t2[b2][:, 0:n], op=ALU.add),
                         reads=[R_t1[b2], R_t2[b2]], writes=[R_mT])

                def stats_D(m_):
                    c2 = m_ % 2
                    P.op("tensor", lambda e, m_=m_, n=n: e.matmul(banks[6][:, 0:n], lhsT=ones_f[:, :], rhs=zTa[:, m_, 0:n],
                                                                   start=(m_ == 0), stop=(m_ == 7)), reads=[R_zTa[m_], R_const], writes=[RB[6]])
                    P.op("tensor", lambda e, m_=m_, c2=c2, n=n: e.matmul(banks[7][:, 0:n], lhsT=ones_f[:, :], rhs=zsq[c2][:, 0:n],
                                                                          start=(m_ == 0), stop=(m_ == 7)), reads=[R_zsq[c2], R_const], writes=[RB[7]])
                for mt in range(8):
                    b2 = mt % 2
                    for kt in range(KT):
                        P.op("tensor", lambda e, kt=kt, mt=mt, b2=b2, n=n: e.matmul(
                            banks[4 + b2][:, 0:n], lhsT=wo[:, kt, mt * 128:(mt + 1) * 128], rhs=mT[:, kt, 0:n],
                            start=(kt == 0), stop=(kt == KT - 1)), reads=[R_w, R_mT], writes=[RB[4 + b2]])
                    if mt >= 1:
                        stats_D(mt - 1)
                    P.op("vector", lambda e, mt=mt, b2=b2, n=n, xtf=xtf: e.scalar_tensor_tensor(
                        out=zTa[:, mt, 0:n], in0=xtf[:, mt, 0:n], scalar=ALPHA, in1=banks[4 + b2][:, 0:n], op0=ALU.mult, op1=ALU.add),
                        reads=[R_xtf, RB[4 + b2]], writes=[R_zTa[mt]])
                    P.op("scalar", lambda e, mt=mt, b2=b2, n=n: e.activation(out=zsq[b2][:, 0:n], in_=zTa[:, mt, 0:n], func=AF.Square),
                         reads=[R_zTa[mt]], writes=[R_zsq[b2]])
                    if prev is not None:
                        part2(prev, mts=[mt], store=(mt == 7))
                stats_D(7)
                P.op("vector", lambda e, n=n, a=a: e.tensor_scalar(out=mean2[a][:, 0:n], in0=banks[6][:, 0:n], scalar1=1.0 / D, scalar2=None, op0=ALU.mult),
                     reads=[RB[6]], writes=[R_mean2[a]])
                P.op("vector", lambda e, n=n, a=a: e.tensor_tensor(out=msq[:, 0:n], in0=mean2[a][:, 0:n], in1=mean2[a][:, 0:n], op=ALU.mult),
                     reads=[R_mean2[a]], writes=[R_msq])
                P.op("vector", lambda e, n=n: e.scalar_tensor_tensor(out=var[:, 0:n], in0=banks[7][:, 0:n], scalar=1.0 / D, in1=msq[:, 0:n],
                                                                    op0=ALU.mult, op1=ALU.subtract), reads=[RB[7], R_msq], writes=[R_var])
                P.op("scalar", lambda e, n=n: e.activation(out=sdv[:, 0:n], in_=var[:, 0:n], func=AF.Sqrt, bias=eps_t[:, 0:1], scale=1.0),
                     reads=[R_var, R_const], writes=[R_sdv])
                P.op("vector", lambda e, n=n, a=a: e.reciprocal(out=rsv2[a][:, 0:n], in_=sdv[:, 0:n]), reads=[R_sdv], writes=[R_rsv2[a]])

            def part2(it, mts=range(8), store=True):
                a = it % 2
                c0, n, xc0 = tile_geom(it)
                zTa, R_zTa = zT2[a], R_zT2[a]
                for mt in mts:
                    b2 = mt % 2
                    P.op("vector", lambda e, mt=mt, b2=b2, n=n, a=a: e.tensor_tensor(out=tn[b2][:, 0:n], in0=zTa[:, mt, 0:n], in1=mean2[a][:, 0:n], op=ALU.subtract),
                         reads=[R_zTa[mt], R_mean2[a]], writes=[R_tn[b2]])
                    P.op("vector", lambda e, b2=b2, n=n, a=a: e.tensor_tensor(out=tn2[b2][:, 0:n], in0=tn[b2][:, 0:n], in1=rsv2[a][:, 0:n], op=ALU.mult),
                         reads=[R_tn[b2], R_rsv2[a]], writes=[R_tn2[b2]])
                    P.op("scalar", lambda e, mt=mt, b2=b2, n=n: e.activation(
                        out=zTa[:, mt, 0:n], in_=tn2[b2][:, 0:n], func=AF.Identity, bias=l1b[:, mt:mt + 1], scale=l1g[:, mt:mt + 1]),
                        reads=[R_tn2[b2], R_c], writes=[R_zTa[mt]])
                    if it in HALOS:
                        P.op("gpsimd", lambda e, mt=mt, n=n, xc0=xc0, it=it: e.tensor_scalar(
                            out=x1b[:, mt, xc0:xc0 + n], in0=zTa[:, mt, 0:n], scalar1=hsc[:, it // 5:it // 5 + 1], scalar2=None, op0=ALU.mult),
                            reads=[R_zTa[mt], R_c], writes=[R_x1b])
                    else:
                        P.op("scalar", lambda e, mt=mt, b2=b2, n=n, xc0=xc0: e.activation(
                            out=x1b[:, mt, xc0:xc0 + n], in_=tn2[b2][:, 0:n], func=AF.Identity, bias=l1b[:, mt:mt + 1], scale=l1g[:, mt:mt + 1]),
                            reads=[R_tn2[b2], R_c], writes=[R_x1b])
                if store:
                    P.op("sync", lambda e, n=n, xc0=xc0: e.dma_start(out=x1_scr[:, :, xc0:xc0 + n], in_=zTa[:, :, 0:n]),
                         reads=R_zTa, dma="x1o%d" % a)

            load_tile_D(0)
            load_weights_D()
            for it in range(N_OWN_RUN):
                part1(it, prev=(it - 1 if it >= 1 else None))
            part2(N_OWN_RUN - 1)
            if dbg:
                nx = X1C
                d4 = dout("dbg_x1", [128, KT, nx], BF16)
                P.op("sync", lambda e: e.dma_start(out=d4[:, :, :], in_=x1b[:, :, 0:nx]), reads=[R_x1b], dma="dbg*")
            P.end("phaseD")
    if upto == "D":
        return nc, dbg_out, es

    with contextlib.ExitStack() as pe:
        def sbe(name, shape, dt):
            return pe.enter_context(nc.sbuf_tensor(name, list(shape), dt))
        HT = NTOK // 2
        hgT = sbe("hgT", [128, NCH, HT], BF16)
        wd = sbe("wd", [128, NCH, D], BF16)
        cw = sbe("cw", [128, 44, 3], F32)
        cb = sbe("cb", [128, 44], F32)
        l2g = sbe("l2g", [128, 8], F32)
        l2b = sbe("l2b", [128, 8], F32)
        wuc = [sbe("wuc%d" % i, [128, KT, 256], BF16) for i in range(2)]
        ct1 = [sbe("ct1_%d" % i, [128, 512], F32) for i in range(2)]
        ct2 = [sbe("ct2_%d" % i, [128, 512], F32) for i in range(2)]
        ct3 = [sbe("ct3_%d" % i, [128, 512], F32) for i in range(2)]
        gact = sbe("gact", [128, 512], F32)
        xf = sbe("xfF", [128, KT, 512], F32)
        z2 = sbe("z2F", [128, KT, 512], F32)
        zq = [sbe("zqF%d" % i, [128, 512], F32) for i in range(2)]
        meanF = sbe("meanF", [128, 512], F32)
        msqF = sbe("msqF", [128, 512], F32)
        varF = sbe("varF", [128, 512], F32)
        sdF = sbe("sdF", [128, 512], F32)
        rsF = sbe("rsF", [128, 512], F32)
        tnF = [sbe("tnF%d" % i, [128, 512], F32) for i in range(2)]
        tn2F = [sbe("tn2F%d" % i, [128, 512], F32) for i in range(2)]
        ost = [sbe("ost%d" % i, [128, D], F32) for i in range(2)]
        wup_v = wup_d.rearrange("k p c -> p k c")
        wd_v = wd_d.rearrange("k p c -> p k c")
        P.begin()
        R_c, R_hg = Res(), Res()
        R_wd2 = [Res(), Res()]
        R_wuc = [[Res(), Res()], [Res(), Res()]]
        R_ct1, R_ct2, R_ct3 = [Res(), Res()], [Res(), Res()], [Res(), Res()]
        R_ga = Res()
        R_xf = Res()
        R_z2m = [Res() for _ in range(8)]
        R_zq = [Res(), Res()]
        R_mean, R_msq, R_var, R_sd, R_rs = Res(), Res(), Res(), Res(), Res()
        R_tn, R_tn2 = [Res(), Res()], [Res(), Res()]
        R_ost = [Res(), Res()]
        P.op("sync", lambda e: e.dma_start(out=cw[:], in_=cw_d[:, :, :]), writes=[R_c], dma="cE*")
        P.op("sync", lambda e: e.dma_start(out=cb[:], in_=cb_d[:, :]), writes=[R_c], dma="cE*")
        P.op("sync", lambda e: e.dma_start(out=l2g[:], in_=ln2g_d[:, :]), writes=[R_c], dma="cE*")
        P.op("sync", lambda e: e.dma_start(out=l2b[:], in_=ln2b_d[:, :]), writes=[R_c], dma="cE*")
        nchunk = 0
        ntile = 0

        def load_chunk(n):
            if n >= N_HALF_RUN * NCH:
                return
            c_ = n % NCH
            wb_ = n % 2
            for part in range(2):
                cc = part * DFF + c_ * 128
                P.op("gpsimd", lambda e, wb_=wb_, part=part, cc=cc: e.dma_start(
                    out=wuc[wb_][:, :, part * 128:(part + 1) * 128], in_=wup_v[:, :, cc:cc + 128]),
                    writes=[R_wuc[wb_][part]], dma="wuc%d%d" % (wb_, part))

        load_chunk(0)
        wd_loaded = [False]
        for hf in range(N_HALF_RUN):
            base = (HT + 2) * hf
            for c in range(NCH):
                wbuf = nchunk % 2
                nchunk += 1
                load_chunk(nchunk)
                if not wd_loaded[0]:
                    wd_loaded[0] = True
                    for q3 in range(2):
                        P.op("gpsimd", lambda e, q3=q3: e.dma_start(out=wd[:, q3 * 11:(q3 + 1) * 11, :], in_=wd_v[:, q3 * 11:(q3 + 1) * 11, :]),
                             writes=[R_wd2[q3]], dma="wdF*")
                for T in range(3):
                    col0 = base + 510 * T
                    ncol = min(512, base + HT + 2 - col0)
                    nout = ncol - 2
                    pb2 = ntile % 2
                    ntile += 1
                    for part in range(2):
                        bk = 2 * pb2 + part
                        for kt in range(KT):
                            P.op("tensor", lambda e, bk=bk, wbuf=wbuf, part=part, kt=kt, col0=col0, ncol=ncol: e.matmul(
                                banks[bk][:, 0:ncol], lhsT=wuc[wbuf][:, kt, part * 128:(part + 1) * 128],
                                rhs=x1b[:, kt, col0:col0 + ncol], start=(kt == 0), stop=(kt == KT - 1)),
                                reads=[R_wuc[wbuf][part], R_x1b], writes=[RB[bk]])
                    for part in range(2):
                        bk = 2 * pb2 + part
                        ci = part * NCH + c
                        P.op("scalar", lambda e, bk=bk, part=part, ci=ci, ncol=ncol, nout=nout: e.activation(
                            out=ct1[part][:, 0:nout], in_=banks[bk][:, 2:ncol], func=AF.Identity,
                            bias=cb[:, ci:ci + 1], scale=cw[:, ci, 2:3]), reads=[RB[bk], R_c], writes=[R_ct1[part]])
                        P.op("vector", lambda e, bk=bk, part=part, ci=ci, ncol=ncol, nout=nout: e.scalar_tensor_tensor(
                            out=ct2[part][:, 0:nout], in0=banks[bk][:, 1:ncol - 1], scalar=cw[:, ci, 1:2], in1=ct1[part][:, 0:nout],
                            op0=ALU.mult, op1=ALU.add), reads=[RB[bk], R_c, R_ct1[part]], writes=[R_ct2[part]])
                        P.op("vector", lambda e, bk=bk, part=part, ci=ci, ncol=ncol, nout=nout: e.scalar_tensor_tensor(
                            out=ct3[part][:, 0:nout], in0=banks[bk][:, 0:ncol - 2], scalar=cw[:, ci, 0:1], in1=ct2[part][:, 0:nout],
                            op0=ALU.mult, op1=ALU.add), reads=[RB[bk], R_c, R_ct2[part]], writes=[R_ct3[part]])
                    P.op("scalar", lambda e, nout=nout: e.activation(out=gact[:, 0:nout], in_=ct3[0][:, 0:nout], func=AF.Gelu),
                         reads=[R_ct3[0]], writes=[R_ga])
                    P.op("gpsimd", lambda e, c=c, T=T, nout=nout: e.tensor_tensor(
                        out=hgT[:, c, 510 * T:510 * T + nout], in0=gact[:, 0:nout], in1=ct3[1][:, 0:nout], op=ALU.mult),
                        reads=[R_ga, R_ct3[1]], writes=[R_hg])
            for T2 in range(2):
                tb0 = HT * hf + 512 * T2
                xcol = base + 2 + 512 * T2
                P.op("sync", lambda e, xcol=xcol: e.dma_start(out=xf[:], in_=x1_scr[:, :, xcol:xcol + 512]),
                     writes=[R_xf], dma="xfF")
                for mt in range(8):
                    b2 = mt % 2
                    for kt in range(NCH):
                        P.op("tensor", lambda e, b2=b2, kt=kt, mt=mt, T2=T2: e.matmul(
                            banks[b2][:, :], lhsT=wd[:, kt, mt * 128:(mt + 1) * 128], rhs=hgT[:, kt, 512 * T2:512 * (T2 + 1)],
                            start=(kt == 0), stop=(kt == NCH - 1)), reads=[R_wd2[kt // 11], R_hg], writes=[RB[b2]])
                    P.op("vector", lambda e, mt=mt, b2=b2: e.scalar_tensor_tensor(
                        out=z2[:, mt, :], in0=xf[:, mt, :], scalar=ALPHA, in1=banks[b2][:, :], op0=ALU.mult, op1=ALU.add),
                        reads=[R_xf, RB[b2]], writes=[R_z2m[mt]])
                    P.op("scalar", lambda e, mt=mt, b2=b2: e.activation(out=zq[b2][:], in_=z2[:, mt, :], func=AF.Square),
                         reads=[R_z2m[mt]], writes=[R_zq[b2]])
                    def stats_F(m_):
                        c2 = m_ % 2
                        P.op("tensor", lambda e, m_=m_: e.matmul(banks[2][:, :], lhsT=ones_f[:, :], rhs=z2[:, m_, :],
                                                                  start=(m_ == 0), stop=(m_ == 7)), reads=[R_z2m[m_], R_const], writes=[RB[2]])
                        P.op("tensor", lambda e, m_=m_, c2=c2: e.matmul(banks[3][:, :], lhsT=ones_f[:, :], rhs=zq[c2][:],
                                                                         start=(m_ == 0), stop=(m_ == 7)), reads=[R_zq[c2], R_const], writes=[RB[3]])
                    if mt >= 1:
                        stats_F(mt - 1)
                    if mt == 7:
                        stats_F(7)
                P.op("vector", lambda e: e.tensor_scalar(out=meanF[:], in0=banks[2][:, :], scalar1=1.0 / D, scalar2=None, op0=ALU.mult),
                     reads=[RB[2]], writes=[R_mean])
                P.op("vector", lambda e: e.tensor_tensor(out=msqF[:], in0=meanF[:], in1=meanF[:], op=ALU.mult), reads=[R_mean], writes=[R_msq])
                P.op("vector", lambda e: e.scalar_tensor_tensor(out=varF[:], in0=banks[3][:, :], scalar=1.0 / D, in1=msqF[:],
                                                               op0=ALU.mult, op1=ALU.subtract), reads=[RB[3], R_msq], writes=[R_var])
                P.op("scalar", lambda e: e.activation(out=sdF[:], in_=varF[:], func=AF.Sqrt, bias=eps_t[:, 0:1], scale=1.0),
                     reads=[R_var, R_const], writes=[R_sd])
                P.op("vector", lambda e: e.reciprocal(out=rsF[:], in_=sdF[:]), reads=[R_sd], writes=[R_rs])
                for mt in range(8):
                    b2 = mt % 2
                    P.op("vector", lambda e, mt=mt, b2=b2: e.tensor_tensor(out=tnF[b2][:], in0=z2[:, mt, :], in1=meanF[:], op=ALU.subtract),
                         reads=[R_z2m[mt], R_mean], writes=[R_tn[b2]])
                    P.op("vector", lambda e, b2=b2: e.tensor_tensor(out=tn2F[b2][:], in0=tnF[b2][:], in1=rsF[:], op=ALU.mult),
                         reads=[R_tn[b2], R_rs], writes=[R_tn2[b2]])
                    P.op("scalar", lambda e, mt=mt, b2=b2: e.activation(
                        out=z2[:, mt, :], in_=tn2F[b2][:], func=AF.Identity, bias=l2b[:, mt:mt + 1], scale=l2g[:, mt:mt + 1]),
                        reads=[R_tn2[b2], R_c], writes=[R_z2m[mt]])
                for tt in range(4):
                    o2 = tt % 2
                    for mt in range(8):
                        bk = 4 + 2 * o2 + mt // 4
                        P.op("tensor", lambda e, bk=bk, mt=mt, tt=tt: e.transpose(
                            out=banks[bk][:, (mt % 4) * 128:(mt % 4 + 1) * 128], in_=z2[:, mt, tt * 128:(tt + 1) * 128],
                            identity=ident_f[:, :]), reads=[R_z2m[mt], R_const], writes=[RB[bk]])
                    for hb2 in range(2):
                        bk = 4 + 2 * o2 + hb2
                        if hb2 == 0:
                            P.op("scalar", lambda e, bk=bk, o2=o2: e.copy(out=ost[o2][:, 0:512], in_=banks[bk][:, :]),
                                 reads=[RB[bk]], writes=[R_ost[o2]])
                        else:
                            P.op("vector", lambda e, bk=bk, o2=o2: e.tensor_copy(out=ost[o2][:, 512:1024], in_=banks[bk][:, :]),
                                 reads=[RB[bk]], writes=[R_ost[o2]])
                    r0 = tb0 + tt * 128
                    P.op("sync", lambda e, o2=o2, r0=r0: e.dma_start(out=out_d[r0:r0 + 128, :], in_=ost[o2][:]),
                         reads=[R_ost[o2]], dma="out%d" % o2)
        P.end("phaseEF")

    return nc, dbg_out, es


def make_in_maps(inp):
    x = np.asarray(inp["x"], np.float32)
    cst = _consts()
    shared = {}
    shared["w_in"] = np.ascontiguousarray(np.asarray(inp["w_in"], np.float32)[0].reshape(KT, 128, 4608))
    shared["b_gate"] = _fm(np.asarray(inp["b_gate"])[0], 16)
    shared["sgu_ln_g"] = np.ascontiguousarray(np.broadcast_to(np.asarray(inp["sgu_ln_g"], np.float32)[0][None], (128, 512)))
    shared["sgu_ln_b"] = np.ascontiguousarray(np.broadcast_to(np.asarray(inp["sgu_ln_b"], np.float32)[0][None], (128, 512)))
    ws = np.asarray(inp["w_spatial"], np.float32)[0]
    shared["wsT"] = np.ascontiguousarray(ws.transpose(2, 0, 1))
    bs = np.asarray(inp["b_spatial"], np.float32)[0]
    shared["bsT"] = np.ascontiguousarray(np.repeat(bs.reshape(4, 2, 1, 128), 64, axis=2).reshape(4, 128, 128).transpose(1, 0, 2))
    shared["w_branch_a"] = np.ascontiguousarray(np.asarray(inp["w_branch_a"], np.float32)[0].reshape(4, 128, D))
    shared["w_branch_b"] = np.ascontiguousarray(np.asarray(inp["w_branch_b"], np.float32)[0].reshape(4, 128, D))
    shared["w_out"] = np.ascontiguousarray(np.asarray(inp["w_out"], np.float32)[0].reshape(KT, 128, D))
    shared["ln1_g"] = _fm(np.asarray(inp["ln1_g"])[0], 8)
    shared["ln1_b"] = _fm(np.asarray(inp["ln1_b"])[0], 8)
    shared["w_up"] = np.ascontiguousarray(np.asarray(inp["w_up"], np.float32)[0].reshape(KT, 128, 2 * DFF))
    cw = np.asarray(inp["conv_w"], np.float32)[0]
    shared["conv_w"] = np.ascontiguousarray(cw.reshape(3, 44, 128).transpose(2, 1, 0))
    shared["conv_b"] = _fm(np.asarray(inp["conv_b"])[0], 44)
    shared["w_down"] = np.ascontiguousarray(np.asarray(inp["w_down"], np.float32)[0].reshape(NCH, 128, D))
    shared["ln2_g"] = _fm(np.asarray(inp["ln2_g"])[0], 8)
    shared["ln2_b"] = _fm(np.asarray(inp["ln2_b"])[0], 8)
    for k in ("perms", "ident", "mtri", "eind", "trilT"):
        shared[k] = cst[k]
    in_maps = []
    zero_blk = np.zeros((BLK, D), np.float32)
    for c in range(8):
        b, j = c // 4, c % 4
        perm = _perm_for(j)
        xp = np.concatenate([x[b, p * BLK:(p + 1) * BLK] if p >= 0 else zero_blk for p in perm], axis=0)
        m = dict(shared)
        m["xT_all"] = np.ascontiguousarray(xp.T).reshape(KT, 128, SLOTC)
        m["rope_all"] = _rope_tables(perm)
        m["vmask"] = _vmask(perm)
        hs = np.ones((128, 2), np.float32)
        if j == 0:
            hs[:, 0] = 0.0
        m["hscale"] = hs
        in_maps.append(m)
    return in_maps


def kernel(**inputs):
    in_maps = make_in_maps(inputs)
    nc, _, es = build("F", False)
    res = run_bass_kernel_spmd(nc, in_maps, core_ids=list(range(8)))
    out = np.zeros((2, SEQ, D), np.float32)
    H = NTOK // 2
    for c in range(8):
        b, j = c // 4, c % 4
        o = res.results[c]["out"]
        out[b, j * H:(j + 1) * H] = o[0:H]
        out[b, (7 - j) * H:(8 - j) * H] = o[H:2 * H]
    return out
```
